# Optimizing a Trainium2 kernel written in Bass

```python
import math
import jax
import jax.numpy as jnp
from jax import lax
import numpy as np

D_MODEL = 1024
BATCH = 8
SEQ = 2048
DEPTH = 4

GRID_W = 64
CTX_LEN = 256
N_EVEN = (DEPTH + 1) // 2
N_ODD = DEPTH // 2

HGRN_HEADS = 4
HGRN_HEAD_DIM = 128
HGRN_WIDTH = HGRN_HEADS * HGRN_HEAD_DIM

MLA_HEADS = 8
MLA_Q_RANK = 384
MLA_KV_RANK = 256
MLA_NOPE = 64
MLA_ROPE = 32
MLA_V = 64

RET_HEADS = 4
RET_QK = 64
RET_V = 128

DIFF_HEADS = 4
DIFF_QK = 64
DIFF_V = 128

FFN_HIDDEN = -(-8 * D_MODEL // (3 * 256)) * 256

EVEN_SIZES = (HGRN_WIDTH,) * 5 + (MLA_Q_RANK, MLA_KV_RANK, MLA_ROPE)
ODD_SIZES = (RET_HEADS * RET_QK, RET_HEADS * RET_QK, RET_HEADS * RET_V, RET_HEADS * RET_V,
             DIFF_HEADS * 2 * DIFF_QK, DIFF_HEADS * 2 * DIFF_QK, DIFF_HEADS * DIFF_V)
EVEN_IN = sum(EVEN_SIZES)
ODD_IN = sum(ODD_SIZES)
MIX_WIDTH = HGRN_WIDTH + MLA_HEADS * MLA_V

CHUNK = 64
Q_BLOCK = 128
ROPE_BASE = 10000.0
EPS = 1e-6

kernel_name = 'hybrid_prefix_diffusion_trunk'


def rms_norm(x, g=None):
    xf = x.astype(jnp.float32)
    y = xf * lax.rsqrt(jnp.mean(xf * xf, axis=-1, keepdims=True) + EPS)
    if g is not None:
        y = y * g.astype(jnp.float32)
    return y.astype(x.dtype)


def split_cols(a, sizes):
    idx = [int(i) for i in np.cumsum(sizes)[:-1]]
    return jnp.split(a, idx, axis=-1)


def rope_1d(x, pos):
    half = x.shape[-1] // 2
    inv = ROPE_BASE ** (-jnp.arange(half, dtype=jnp.float32) / half)
    ang = pos.astype(jnp.float32)[:, None] * inv[None, :]
    shape = (1, x.shape[1]) + (1,) * (x.ndim - 3) + (half,)
    cos = jnp.cos(ang).reshape(shape).astype(x.dtype)
    sin = jnp.sin(ang).reshape(shape).astype(x.dtype)
    x1, x2 = x[..., :half], x[..., half:]
    return jnp.concatenate([x1 * cos - x2 * sin, x1 * sin + x2 * cos], axis=-1)


def rope_2d(x):
    n = x.shape[1]
    rows = n // GRID_W
    row = jnp.repeat(jnp.arange(rows), GRID_W)
    col = jnp.arange(rows * GRID_W) % GRID_W
    half = x.shape[-1] // 2
    return jnp.concatenate([rope_1d(x[..., :half], row), rope_1d(x[..., half:], col)], axis=-1)


def attend(q, k, v, scale):
    s = jnp.einsum('bqhd,bkhd->bhqk', q, k).astype(jnp.float32) * scale
    p = jax.nn.softmax(s, axis=-1).astype(v.dtype)
    return jnp.einsum('bhqk,bkhv->bqhv', p, v)


def diff_attend(q, k, v, scale, lam):
    s = jnp.einsum('bqhmd,bkhmd->bhmqk', q, k).astype(jnp.float32) * scale
    p = jax.nn.softmax(s, axis=-1)
    a = (p[:, :, 0] - lam * p[:, :, 1]).astype(v.dtype)
    return jnp.einsum('bhqk,bkhv->bqhv', a, v)


def sweep_query_blocks(fn, q):
    b, n = q.shape[:2]
    nb = n // Q_BLOCK
    qb = jnp.moveaxis(q.reshape((b, nb, Q_BLOCK) + q.shape[2:]), 1, 0)
    out = lax.map(fn, qb)
    return jnp.moveaxis(out, 0, 1).reshape((b, n) + out.shape[3:])


def gla_scan(q, k, v, log_f, s0):
    b, n, h, _ = q.shape
    dv = v.shape[-1]
    nc = n // CHUNK

    def chunks(a):
        return a.astype(jnp.float32).reshape(b, nc, CHUNK, h, a.shape[-1]).transpose(1, 0, 3, 2, 4)

    causal = jnp.tril(jnp.ones((CHUNK, CHUNK), dtype=bool))[:, :, None]

    def step(S, inp):
        qc, kc, vc, gc = inp
        bcum = jnp.cumsum(gc, axis=2)
        diff = bcum[:, :, :, None, :] - bcum[:, :, None, :, :]
        dec = jnp.where(causal, jnp.exp(jnp.where(causal, diff, 0.0)), 0.0)
        att = jnp.einsum('bhtd,bhsd,bhtsd->bhts', qc, kc, dec)
        o = (jnp.einsum('bhts,bhsv->bhtv', att, vc)
             + jnp.einsum('bhtd,bhdv->bhtv', qc * jnp.exp(bcum), S))
        b_last = bcum[:, :, -1:, :]
        S = (jnp.exp(b_last[:, :, 0, :, None]) * S
             + jnp.einsum('bhsd,bhsv->bhdv', kc * jnp.exp(b_last - bcum), vc))
        return S, o

    S, o = lax.scan(step, s0, (chunks(q), chunks(k), chunks(v), chunks(log_f)))
    o = o.transpose(1, 0, 3, 2, 4).reshape(b, n, h, dv)
    return o.astype(v.dtype), S


def retention_scan(q, k, v, log_gamma, s0):
    b, n, h, _ = q.shape
    dv = v.shape[-1]
    nc = n // CHUNK
    lg = log_gamma.astype(jnp.float32)
    t = jnp.arange(CHUNK, dtype=jnp.float32)
    rel = t[:, None] - t[None, :]
    dmat = jnp.where(rel >= 0, jnp.exp(jnp.maximum(rel, 0.0)[None] * lg[:, None, None]), 0.0)
    q_dec = jnp.exp((t + 1.0)[None, :] * lg[:, None])
    k_dec = jnp.exp((CHUNK - 1.0 - t)[None, :] * lg[:, None])
    c_dec = jnp.exp(CHUNK * lg)

    def chunks(a):
        return a.astype(jnp.float32).reshape(b, nc, CHUNK, h, a.shape[-1]).transpose(1, 0, 3, 2, 4)

    def step(S, inp):
        qc, kc, vc = inp
        att = jnp.einsum('bhtd,bhsd->bhts', qc, kc) * dmat
        o = (jnp.einsum('bhts,bhsv->bhtv', att, vc)
             + jnp.einsum('bhtd,bhdv->bhtv', qc * q_dec[:, :, None], S))
        S = c_dec[:, None, None] * S + jnp.einsum('bhsd,bhsv->bhdv', kc * k_dec[:, :, None], vc)
        return S, o

    S, o = lax.scan(step, s0, (chunks(q), chunks(k), chunks(v)))
    o = o.transpose(1, 0, 3, 2, 4).reshape(b, n, h, dv)
    return o.astype(v.dtype), S


def run_prefix_scan(scan_fn, ctx_seqs, lat_seqs, extra, s0, reverse):
    flip = (lambda a: jnp.flip(a, axis=1)) if reverse else (lambda a: a)
    o_ctx, s_ctx = scan_fn(*[flip(a) for a in ctx_seqs], *extra, s0)
    o_lat, _ = scan_fn(*[flip(a) for a in lat_seqs], *extra, s_ctx)
    return flip(o_ctx), flip(o_lat)


def hgrn2_group(z_lat, z_ctx, lb, norm_g, need_ctx):
    scale = HGRN_HEAD_DIM ** -0.5

    def heads(a):
        return a.reshape(a.shape[:2] + (HGRN_HEADS, HGRN_HEAD_DIM))

    def seqs(z, d):
        q, i, f_fwd, f_bwd, _ = z
        zf = (f_fwd, f_bwd)[d].astype(jnp.float32)
        lb_d = lb[d]
        k = (1.0 - lb_d) * jax.nn.sigmoid(-zf)
        log_f = jax.nn.log_sigmoid(zf) + jnp.log1p(lb_d * jnp.exp(-zf))
        return (heads(q) * scale, heads(k), heads(i), heads(log_f))

    b = z_lat[0].shape[0]
    s0 = jnp.zeros((b, HGRN_HEADS, HGRN_HEAD_DIM, HGRN_HEAD_DIM), jnp.float32)
    fwd = run_prefix_scan(gla_scan, seqs(z_ctx, 0), seqs(z_lat, 0), (), s0, False)
    bwd = run_prefix_scan(gla_scan, seqs(z_ctx, 1), seqs(z_lat, 1), (), s0, True)

    def readout(o, g):
        return (rms_norm(o, norm_g) * jax.nn.silu(heads(g))).reshape(g.shape)

    out_lat = readout(fwd[1] + bwd[1], z_lat[4])
    out_ctx = readout(fwd[0] + bwd[0], z_ctx[4]) if need_ctx else None
    return out_lat, out_ctx


def mla_group(p_lat, p_ctx, q_norm_g, w_uq, kv_norm_g, w_ukv, need_ctx):
    scale = (MLA_NOPE + MLA_ROPE) ** -0.5

    def qkv(p, rotate):
        c_q, c_kv, k_rope = p
        b, n = c_q.shape[:2]
        q = (rms_norm(c_q, q_norm_g) @ w_uq).reshape(b, n, MLA_HEADS, MLA_NOPE + MLA_ROPE)
        kv = (rms_norm(c_kv, kv_norm_g) @ w_ukv).reshape(b, n, MLA_HEADS, MLA_NOPE + MLA_V)
        q_nope, q_rope = q[..., :MLA_NOPE], q[..., MLA_NOPE:]
        k_nope, v = kv[..., :MLA_NOPE], kv[..., MLA_NOPE:]
        k_rope = k_rope[:, :, None, :]
        if rotate:
            q_rope, k_rope = rope_2d(q_rope), rope_2d(k_rope)
        q = jnp.concatenate([q_nope, q_rope], axis=-1)
        k = jnp.concatenate([k_nope, jnp.broadcast_to(k_rope, (b, n, MLA_HEADS, MLA_ROPE))], axis=-1)
        return q, k, v

    q_l, k_l, v_l = qkv(p_lat, True)
    q_c, k_c, v_c = qkv(p_ctx, False)
    k_all = jnp.concatenate([k_c, k_l], axis=1)
    v_all = jnp.concatenate([v_c, v_l], axis=1)
    o_lat = sweep_query_blocks(lambda qb: attend(qb, k_all, v_all, scale), q_l)
    out_lat = o_lat.reshape(o_lat.shape[:2] + (MLA_HEADS * MLA_V,))
    out_ctx = None
    if need_ctx:
        o_ctx = attend(q_c, k_c, v_c, scale)
        out_ctx = o_ctx.reshape(o_ctx.shape[:2] + (MLA_HEADS * MLA_V,))
    return out_lat, out_ctx


def retention_group(p_lat, p_ctx, decay_logits, need_ctx):
    log_gamma = jax.nn.log_sigmoid(decay_logits.astype(jnp.float32))

    def qkv(p, rotate):
        q, k, v, _ = p
        b, n = q.shape[:2]
        q = q.reshape(b, n, RET_HEADS, RET_QK)
        k = k.reshape(b, n, RET_HEADS, RET_QK) * (RET_QK ** -0.5)
        v = v.reshape(b, n, RET_HEADS, RET_V)
        if rotate:
            pos = jnp.arange(n)
            q, k = rope_1d(q, pos), rope_1d(k, pos)
        return q, k, v

    lat = qkv(p_lat, True)
    ctx = qkv(p_ctx, False)
    b = p_lat[0].shape[0]
    s0 = jnp.zeros((b, RET_HEADS, RET_QK, RET_V), jnp.float32)
    fwd = run_prefix_scan(retention_scan, ctx, lat, (log_gamma[0],), s0, False)
    bwd = run_prefix_scan(retention_scan, ctx, lat, (log_gamma[1],), s0, True)

    def readout(o, g):
        gh = g.reshape(g.shape[:2] + (RET_HEADS, RET_V))
        return (rms_norm(o) * jax.nn.silu(gh)).reshape(g.shape)

    out_lat = readout(fwd[1] + bwd[1], p_lat[3])
    out_ctx = readout(fwd[0] + bwd[0], p_ctx[3]) if need_ctx else None
    return out_lat, out_ctx


def diff_group(p_lat, p_ctx, lam_params, subln_g, layer_idx, need_ctx):
    scale = DIFF_QK ** -0.5
    lambda_init = 0.8 - 0.6 * math.exp(-0.3 * layer_idx)
    lp = lam_params.astype(jnp.float32)
    lam = jnp.exp(jnp.sum(lp[0] * lp[1])) - jnp.exp(jnp.sum(lp[2] * lp[3])) + lambda_init

    def qkv(p, rotate):
        q, k, v = p
        b, n = q.shape[:2]
        q = q.reshape(b, n, DIFF_HEADS, 2, DIFF_QK)
        k = k.reshape(b, n, DIFF_HEADS, 2, DIFF_QK)
        v = v.reshape(b, n, DIFF_HEADS, DIFF_V)
        if rotate:
            q, k = rope_2d(q), rope_2d(k)
        return q, k, v

    q_l, k_l, v_l = qkv(p_lat, True)
    q_c, k_c, v_c = qkv(p_ctx, False)
    k_all = jnp.concatenate([k_c, k_l], axis=1)
    v_all = jnp.concatenate([v_c, v_l], axis=1)

    def readout(o):
        return (rms_norm(o, subln_g) * (1.0 - lambda_init)).reshape(o.shape[:2] + (DIFF_HEADS * DIFF_V,))

    out_lat = readout(sweep_query_blocks(lambda qb: diff_attend(qb, k_all, v_all, scale, lam), q_l))
    out_ctx = readout(diff_attend(q_c, k_c, v_c, scale, lam)) if need_ctx else None
    return out_lat, out_ctx


def even_mixer(h_lat, h_ctx, w_in, w_out, lb, hgrn_norm_g, q_norm_g, w_uq, kv_norm_g, w_ukv, need_ctx):
    p_lat = split_cols(h_lat @ w_in, EVEN_SIZES)
    p_ctx = split_cols(h_ctx @ w_in, EVEN_SIZES)
    a_lat, a_ctx = hgrn2_group(p_lat[:5], p_ctx[:5], lb, hgrn_norm_g, need_ctx)
    b_lat, b_ctx = mla_group(p_lat[5:], p_ctx[5:], q_norm_g, w_uq, kv_norm_g, w_ukv, need_ctx)
    out_lat = jnp.concatenate([a_lat, b_lat], axis=-1) @ w_out
    out_ctx = (jnp.concatenate([a_ctx, b_ctx], axis=-1) @ w_out) if need_ctx else None
    return out_lat, out_ctx


def odd_mixer(h_lat, h_ctx, w_in, w_out, decay_logits, lam_params, subln_g, layer_idx, need_ctx):
    p_lat = split_cols(h_lat @ w_in, ODD_SIZES)
    p_ctx = split_cols(h_ctx @ w_in, ODD_SIZES)
    a_lat, a_ctx = retention_group(p_lat[:4], p_ctx[:4], decay_logits, need_ctx)
    b_lat, b_ctx = diff_group(p_lat[4:], p_ctx[4:], lam_params, subln_g, layer_idx, need_ctx)
    out_lat = jnp.concatenate([a_lat, b_lat], axis=-1) @ w_out
    out_ctx = (jnp.concatenate([a_ctx, b_ctx], axis=-1) @ w_out) if need_ctx else None
    return out_lat, out_ctx


def swiglu(h, w_gate, w_up, w_down):
    return (jax.nn.silu(h @ w_gate) * (h @ w_up)) @ w_down


def setup_inputs(seed: int = 0) -> dict:
    key = jax.random.key(seed)
    ks = jax.random.split(key, 24)
    f32 = jnp.float32

    def nrm(k, shape, fan_in, gain=1.0):
        return gain * (fan_in ** -0.5) * jax.random.normal(k, shape, f32)

    def gain_like(k, shape):
        return 1.0 + 0.05 * jax.random.normal(k, shape, f32)

    ret_base = jnp.log(2.0 ** (5.0 + jnp.arange(RET_HEADS, dtype=f32)) - 1.0)
    return {
        'x': jax.random.normal(ks[0], (BATCH, SEQ, D_MODEL), f32),
        'c': jax.random.normal(ks[1], (BATCH, D_MODEL), f32),
        'ctx': jax.random.normal(ks[2], (BATCH, CTX_LEN, D_MODEL), f32),
        'c_ctx': jax.random.normal(ks[3], (D_MODEL,), f32),
        'ada_w': nrm(ks[4], (DEPTH, D_MODEL, 6 * D_MODEL), D_MODEL, 0.5),
        'ada_b': 0.01 * jax.random.normal(ks[5], (DEPTH, 6 * D_MODEL), f32),
        'norm_g': gain_like(ks[6], (DEPTH, 2, D_MODEL)),
        'mix_w_out': nrm(ks[7], (DEPTH, MIX_WIDTH, D_MODEL), MIX_WIDTH),
        'ffn_w_gate': nrm(ks[8], (DEPTH, D_MODEL, FFN_HIDDEN), D_MODEL),
        'ffn_w_up': nrm(ks[9], (DEPTH, D_MODEL, FFN_HIDDEN), D_MODEL),
        'ffn_w_down': nrm(ks[10], (DEPTH, FFN_HIDDEN, D_MODEL), FFN_HIDDEN),
        'even_w_in': nrm(ks[11], (N_EVEN, D_MODEL, EVEN_IN), D_MODEL),
        'hgrn_lb_logits': 0.1 * jax.random.normal(ks[12], (N_EVEN, 2, HGRN_WIDTH), f32),
        'hgrn_norm_g': gain_like(ks[13], (N_EVEN, HGRN_HEAD_DIM)),
        'mla_q_norm_g': gain_like(ks[14], (N_EVEN, MLA_Q_RANK)),
        'mla_w_uq': nrm(ks[15], (N_EVEN, MLA_Q_RANK, MLA_HEADS * (MLA_NOPE + MLA_ROPE)), MLA_Q_RANK),
        'mla_kv_norm_g': gain_like(ks[16], (N_EVEN, MLA_KV_RANK)),
        'mla_w_ukv': nrm(ks[17], (N_EVEN, MLA_KV_RANK, MLA_HEADS * (MLA_NOPE + MLA_V)), MLA_KV_RANK),
        'odd_w_in': nrm(ks[18], (N_ODD, D_MODEL, ODD_IN), D_MODEL),
        'ret_decay_logits': ret_base[None, None, :] + 0.01 * jax.random.normal(ks[19], (N_ODD, 2, RET_HEADS), f32),
        'diff_lambda': 0.1 * jax.random.normal(ks[20], (N_ODD, 4, DIFF_QK), f32),
        'diff_subln_g': gain_like(ks[21], (N_ODD, DIFF_V)),
        'final_norm_g': gain_like(ks[22], (D_MODEL,)),
    }


def reference(x, c, ctx, c_ctx, ada_w, ada_b, norm_g, mix_w_out, ffn_w_gate, ffn_w_up, ffn_w_down,
              even_w_in, hgrn_lb_logits, hgrn_norm_g, mla_q_norm_g, mla_w_uq, mla_kv_norm_g, mla_w_ukv,
              odd_w_in, ret_decay_logits, diff_lambda, diff_subln_g, final_norm_g):
    lb_soft = jax.nn.softmax(hgrn_lb_logits.astype(jnp.float32), axis=0)
    lbs = jnp.cumsum(lb_soft, axis=0) - lb_soft[0:1]
    s_lat = jax.nn.silu(c)
    s_ctx = jax.nn.silu(c_ctx)
    for layer in range(DEPTH):
        need_ctx = layer < DEPTH - 1
        mod = (s_lat @ ada_w[layer] + ada_b[layer])[:, None, :]
        mod_c = (s_ctx @ ada_w[layer] + ada_b[layer])[None, None, :]
        sh1, sc1, g1, sh2, sc2, g2 = jnp.split(mod, 6, axis=-1)
        csh1, csc1, cg1, csh2, csc2, cg2 = jnp.split(mod_c, 6, axis=-1)
        h_lat = rms_norm(x, norm_g[layer, 0]) * (1.0 + sc1) + sh1
        h_ctx = rms_norm(ctx, norm_g[layer, 0]) * (1.0 + csc1) + csh1
        if layer % 2 == 0:
            e = layer // 2
            u_lat, u_ctx = even_mixer(h_lat, h_ctx, even_w_in[e], mix_w_out[layer], lbs[e], hgrn_norm_g[e],
                                      mla_q_norm_g[e], mla_w_uq[e], mla_kv_norm_g[e], mla_w_ukv[e], need_ctx)
        else:
            o = layer // 2
            u_lat, u_ctx = odd_mixer(h_lat, h_ctx, odd_w_in[o], mix_w_out[layer], ret_decay_logits[o],
                                     diff_lambda[o], diff_subln_g[o], layer, need_ctx)
        x = x + g1 * u_lat
        x = x + g2 * swiglu(rms_norm(x, norm_g[layer, 1]) * (1.0 + sc2) + sh2,
                            ffn_w_gate[layer], ffn_w_up[layer], ffn_w_down[layer])
        if need_ctx:
            ctx = ctx + cg1 * u_ctx
            ctx = ctx + cg2 * swiglu(rms_norm(ctx, norm_g[layer, 1]) * (1.0 + csc2) + csh2,
                                     ffn_w_gate[layer], ffn_w_up[layer], ffn_w_down[layer])
    return rms_norm(x, final_norm_g)
```

```python
import numpy as np
from contextlib import ExitStack
import concourse.bass as bass
import concourse.mybir as mybir
from concourse.bass_utils import run_bass_kernel_spmd

F32 = mybir.dt.float32
BF16 = mybir.dt.bfloat16
AF = mybir.ActivationFunctionType
ALU = mybir.AluOpType
AX = mybir.AxisListType
_ESZ = {F32: 4, BF16: 2}

PHASE = 8000
NDSEM = 24


def _is_dram(ap):
    return 'DRam' in type(ap.tensor).__name__


def _bbox(ap):
    steps = ap.ap
    es = _ESZ.get(ap.dtype, 4)
    pstep, pcnt = steps[0]
    off = ap.offset
    if pstep:
        p0 = off // pstep
        f0 = off % pstep
    else:
        p0, f0 = 0, off
    ext = 1
    for s, c in steps[1:]:
        ext += (c - 1) * abs(s)
    return (p0, p0 + pcnt, f0 * es, (f0 + ext) * es)


class Prog:
    ENGS = ('pe', 'act', 'dve', 'pool', 'sp')

    def __init__(self):
        self.nc = bass.Bass("TRN2", target_bir_lowering=False)
        self.q = {e: [] for e in self.ENGS}
        self.cnt = {e: 0 for e in self.ENGS}
        self.recs = {}
        self.waited = {e: {} for e in self.ENGS}
        self.ndma = 0
        self.dma_tokens = []
        self.out_tokens = []
        self.same_engine_sync = True

    def sb(self, name, shape, dtype=F32):
        return self.nc.alloc_sbuf_tensor(name, list(shape), dtype).ap()

    def ps(self, name, shape, dtype=F32):
        return self.nc.alloc_psum_tensor(name, list(shape), dtype).ap()

    def dram(self, name, shape, dtype=F32, kind="ExternalInput"):
        return self.nc.dram_tensor(name, list(shape), dtype, kind=kind).ap()

    def _deps(self, eng, token, reads, writes, self_sync):
        deps = {}

        def add(tok):
            k, v = tok
            if deps.get(k, 0) < v:
                deps[k] = v

        for ap in reads:
            if ap is None or _is_dram(ap):
                continue
            bb = _bbox(ap)
            lst = self.recs.setdefault(ap.tensor.name, [])
            for r in lst:
                if r[5] and r[0] < bb[1] and bb[0] < r[1] and r[2] < bb[3] and bb[2] < r[3]:
                    add(r[4])
        for ap in writes:
            if ap is None or _is_dram(ap):
                continue
            bb = _bbox(ap)
            lst = self.recs.setdefault(ap.tensor.name, [])
            for r in lst:
                if r[0] < bb[1] and bb[0] < r[1] and r[2] < bb[3] and bb[2] < r[3]:
                    add(r[4])
        for ap in reads:
            if ap is None or _is_dram(ap):
                continue
            bb = _bbox(ap)
            lst = self.recs[ap.tensor.name]
            if token[0][0] == 'E':
                lst[:] = [r for r in lst if not ((not r[5]) and r[4][0] == token[0]
                                                 and bb[0] <= r[0] and r[1] <= bb[1]
                                                 and bb[2] <= r[2] and r[3] <= bb[3])]
            lst.append((bb[0], bb[1], bb[2], bb[3], token, False))
        for ap in writes:
            if ap is None or _is_dram(ap):
                continue
            bb = _bbox(ap)
            lst = self.recs[ap.tensor.name]
            lst[:] = [r for r in lst if not (bb[0] <= r[0] and r[1] <= bb[1]
                                             and bb[2] <= r[2] and r[3] <= bb[3])]
            lst.append((bb[0], bb[1], bb[2], bb[3], token, True))
        out = []
        w = self.waited[eng]
        for k, v in deps.items():
            if k[0] == 'E' and k[1] == eng and not self_sync:
                continue
            if w.get(k, 0) >= v:
                continue
            w[k] = v
            out.append((k, v))
        return out

    def op(self, eng, fn, reads, writes, self_sync=None):
        if self_sync is None:
            self_sync = self.same_engine_sync
        idx = self.cnt[eng]
        self.cnt[eng] += 1
        token = (('E', eng, idx // PHASE), idx % PHASE + 1)
        waits = self._deps(eng, token, reads, writes, self_sync)
        self.q[eng].append((waits, fn, token, 1))
        return token

    def dma(self, out, in_, eng='sp', is_output=False, **kw):
        i = self.ndma
        self.ndma += 1
        token = (('D', i % NDSEM), 16 * (i // NDSEM + 1))
        waits = self._deps(eng, token, [in_], [out], True)
        if i >= NDSEM:
            pk, pv = self.dma_tokens[i - NDSEM]
            if self.waited[eng].get(pk, 0) < pv:
                self.waited[eng][pk] = pv
                waits.append((pk, pv))
        self.dma_tokens.append(token)
        if is_output:
            self.out_tokens.append(token)
        self.q[eng].append((waits, lambda e: e.dma_start(out=out, in_=in_, **kw), token, 16))
        return token

    def mm(self, out, lhsT, rhs, start=True, stop=True, **kw):
        return self.op('pe', lambda e: e.matmul(out, lhsT, rhs, start=start, stop=stop, **kw),
                       [lhsT, rhs] + ([] if start else [out]), [out], self_sync=False)

    def transpose(self, out, in_, ident):
        return self.op('pe', lambda e: e.transpose(out, in_, ident), [in_, ident], [out], self_sync=False)

    def act(self, out, in_, func, scale=1.0, bias=0.0, eng='act', accum_out=None):
        rd = [in_]
        if not isinstance(scale, (int, float)):
            rd.append(scale)
        if not isinstance(bias, (int, float)):
            rd.append(bias)
        kw = {}
        if accum_out is not None:
            kw['accum_out'] = accum_out
        return self.op(eng, lambda e: e.activation(out=out, in_=in_, func=func, scale=scale, bias=bias, **kw),
                       rd, [out, accum_out])

    def tt(self, out, in0, in1, op, eng='dve'):
        return self.op(eng, lambda e: e.tensor_tensor(out=out, in0=in0, in1=in1, op=op), [in0, in1], [out])

    def ts(self, out, in0, s1, op0, s2=None, op1=None, eng='dve'):
        rd = [in0]
        if not isinstance(s1, (int, float)):
            rd.append(s1)
        if s2 is not None and not isinstance(s2, (int, float)):
            rd.append(s2)
        kw = {}
        if op1 is not None:
            kw['op1'] = op1
        return self.op(eng, lambda e: e.tensor_scalar(out=out, in0=in0, scalar1=s1, scalar2=s2, op0=op0, **kw),
                       rd, [out])

    def stt(self, out, in0, scalar, in1, op0, op1, eng='dve'):
        rd = [in0, in1]
        if not isinstance(scalar, (int, float)):
            rd.append(scalar)
        return self.op(eng, lambda e: e.scalar_tensor_tensor(out=out, in0=in0, scalar=scalar, in1=in1, op0=op0, op1=op1),
                       rd, [out])

    def copy(self, out, in_, eng='dve'):
        if eng == 'act':
            return self.op(eng, lambda e: e.copy(out=out, in_=in_), [in_], [out])
        return self.op(eng, lambda e: e.tensor_copy(out=out, in_=in_), [in_], [out])

    def memset(self, out, val, eng='dve'):
        return self.op(eng, lambda e: e.memset(out, val), [], [out])

    def recip(self, out, in_, eng='dve'):
        return self.op(eng, lambda e: e.reciprocal(out=out, in_=in_), [in_], [out])

    def scan(self, out, data0, data1, initial, op0, op1):
        rd = [data0, data1]
        if not isinstance(initial, (int, float)):
            rd.append(initial)
        return self.op('dve', lambda e: e.tensor_tensor_scan(out=out, data0=data0, data1=data1, initial=initial,
                                                             op0=op0, op1=op1), rd, [out])

    def finalize(self):
        nc = self.nc
        fw = []
        for k, v in self.out_tokens:
            if self.waited['sp'].get(k, 0) < v:
                self.waited['sp'][k] = v
                fw.append((k, v))
        self.q['sp'].append((fw, None, None, 0))
        with ExitStack() as st:
            sems = {}

            def sem(k):
                if k not in sems:
                    sems[k] = st.enter_context(nc.semaphore("s_" + "_".join(str(x) for x in k)))
                return sems[k]

            for e in self.ENGS:
                for ph in range((self.cnt[e] + PHASE - 1) // PHASE):
                    sem(('E', e, ph))
            for i in range(min(self.ndma, NDSEM)):
                sem(('D', i))
            block = st.enter_context(nc.Block())

            def replay(name):
                def body(eng):
                    for waits, fn, token, inc in self.q[name]:
                        for k, v in waits:
                            eng.wait_ge(sem(k), v)
                        if fn is not None:
                            fn(eng).then_inc(sem(token[0]), inc)
                return body

            block.sync(replay('sp'))
            block.tensor(replay('pe'))
            block.scalar(replay('act'))
            block.vector(replay('dve'))
            block.gpsimd(replay('pool'))
        return nc


import math

D = 1024
T = 2304
CT = 256
NT = 18
TB = [(0, 256), (256, 512), (768, 512), (1280, 512), (1792, 512)]
FFN_H = 2816
EPS = 1e-6
ROPE_RET, ROPE_DIFF, ROPE_MLA = 0, 1, 2


class Ring:
    def __init__(self, aps):
        self.aps = aps
        self.i = 0

    def next(self):
        a = self.aps[self.i % len(self.aps)]
        self.i += 1
        return a


class Arena:
    def __init__(self, P, name, nbytes):
        self.ap = P.sb(name, [128, nbytes // 4], F32)
        self.n = nbytes // 4
        self.off = 0

    def reset(self, to=0):
        self.off = to

    def alloc(self, shape, dtype=F32):
        n = 1
        for s in shape:
            n *= s
        words = (n * (4 if dtype == F32 else 2) + 3) // 4
        words = (words + 7) // 8 * 8
        assert self.off + words <= self.n, ("arena overflow", self.off, words, self.n)
        v = self.ap[:, self.off:self.off + words]
        self.off += words
        if dtype != F32:
            v = v.bitcast(dtype)
        v = v[:, 0:n]
        if len(shape) == 2:
            v = v.rearrange("p (a b) -> p a b", b=shape[1])
        elif len(shape) == 3:
            v = v.rearrange("p (a b c) -> p a b c", b=shape[1], c=shape[2])
        return v

    def ring(self, k, shape, dtype=F32):
        return Ring([self.alloc(shape, dtype) for _ in range(k)])


def vec_layout():
    cols = {}
    n = 0

    def add(name, k):
        nonlocal n
        cols[name] = n
        n += k
    add('c', 8)
    add('cctx', 8)
    for l in range(4):
        add(('ada_b', l), 48)
    for l in range(4):
        for j in range(2):
            add(('norm_g', l, j), 8)
    add('final_g', 8)
    for e in range(2):
        for d in range(2):
            add(('lb', e, d), 4)
    for e in range(2):
        add(('hgrn_g', e), 1)
        add(('qn_g', e), 3)
        add(('kvn_g', e), 2)
    for o in range(2):
        add(('subln', o), 1)
    return cols, n


VCOL, NV = vec_layout()
CB_IDENT, CB_ONES, CB_MF, CB_MB, CB_BM, CB_PERM = 0, 128, 256, 384, 512, 516
CB_M128 = 516 + 3 * 128
CB_M64 = CB_M128 + 256
CB_BM2 = CB_M64 + 256
NCB = CB_BM2 + 2
HGRN_C = (32, 64)
RET_C = 128


def build_program(n_layers=4):
    P = Prog()
    d_xT = P.dram("xT", [D, T])
    d_vec = P.dram("vecs", [128, NV])
    d_bc = P.dram("bc", [128, 16 + 512])
    d_cb = P.dram("cb", [128, NCB])
    d_rope = P.dram("rope", [3, 2, 128, 2048])
    d_ada = P.dram("ada_w", [4, D, 6 * D])
    d_wout = P.dram("mix_w_out", [4, D, D])
    d_wg = P.dram("ffn_w_gate", [4, D, FFN_H])
    d_wu = P.dram("ffn_w_up", [4, D, FFN_H])
    d_wd = P.dram("ffn_w_down", [4, FFN_H, D])
    d_ewin = P.dram("even_w_in", [2, D, 3232])
    d_wuq = P.dram("mla_w_uq", [2, 384, 768])
    d_wukv = P.dram("mla_w_ukv", [2, 256, 1024])
    d_owin = P.dram("odd_w_in", [2, D, 3072])
    d_out = P.dram("outT", [D, 2048], kind="ExternalOutput")

    XT = P.sb("XT", [128, 8, T], F32)
    HT = P.sb("HT", [128, 8, T], BF16)
    VEC = P.sb("VEC", [128, NV], F32)
    BC = P.sb("BCs", [128, 16 + 512], F32)
    CB = P.sb("CBs", [128, NCB], BF16)
    MOD = P.sb("MOD", [128, 4, 48, 2], F32)
    AMOD = P.sb("AMOD", [128, 4, 2, 8, 2], F32)
    SM = P.sb("SM", [128, 64], F32)
    S2 = P.sb("S2", [128, 8, 2], F32)
    IDENT = CB[:, CB_IDENT:CB_IDENT + 128]
    ONES = CB[:, CB_ONES:CB_ONES + 128]
    MASKS = {32: [CB[:, CB_MF:CB_MF + 128], CB[:, CB_MB:CB_MB + 128]],
             64: [CB[:, CB_M64:CB_M64 + 128], CB[:, CB_M64 + 128:CB_M64 + 256]],
             128: [CB[:, CB_M128:CB_M128 + 128], CB[:, CB_M128 + 128:CB_M128 + 256]]}
    BMS = {32: CB[:, CB_BM:CB_BM + 4], 64: CB[:, CB_BM2:CB_BM2 + 2]}
    BM4 = CB[:, CB_BM:CB_BM + 4]
    PERM = [CB[:, CB_PERM + 128 * i:CB_PERM + 128 * (i + 1)] for i in range(3)]
    C_EPS, C_ZERO, C_ONE = 0, 1, 2
    C_LG = 4
    C_NLAM = 20
    C_GS = 22
    C_OML = 24
    C_TMP = 40
    EPSC = SM[:, C_EPS:C_EPS + 1]
    ZEROC = SM[:, C_ZERO:C_ZERO + 1]
    ONEC = SM[:, C_ONE:C_ONE + 1]

    AR = Arena(P, "ARENA", 90 * 1024)
    pbanks = [P.ps("bank%d" % i, [128, 512], F32) for i in range(8)]
    PS_T = Ring(pbanks[0:4])
    PS_A = Ring(pbanks[4:8])

    P.dma(VEC, d_vec)
    P.dma(BC, d_bc)
    P.dma(CB, d_cb, eng='pool')
    P.dma(XT, d_xT.rearrange("(c p) t -> p c t", p=128))
    P.memset(SM, 0.0)
    P.memset(SM[:, C_EPS:C_EPS + 1], EPS)
    P.memset(SM[:, C_ONE:C_ONE + 1], 1.0)

    for k_, nm_ in ((0, 'c'), (1, 'cctx')):
        src_ = VEC[:, VCOL[nm_]:VCOL[nm_] + 8]
        P.act(S2[:, :, k_], src_, AF.Exp, scale=-1.0)
        P.act(S2[:, :, k_], S2[:, :, k_], AF.Ln, bias=SM[:, C_ONE:C_ONE + 1])
        P.act(S2[:, :, k_], S2[:, :, k_], AF.Exp, scale=-1.0)
        P.tt(S2[:, :, k_], S2[:, :, k_], src_, ALU.mult)

    AR.reset()
    adaring = AR.ring(3, (8, 512), BF16)
    S2b = AR.alloc((8, 2), BF16)
    P.copy(S2b, S2)
    for l in range(n_layers):
        mps = PS_A.next()
        for g in range(12):
            wa = adaring.next()
            P.dma(wa, d_ada[l, :, g * 512:(g + 1) * 512].rearrange("(kc p) f -> p kc f", p=128), eng='pool')
            for jj in range(4):
                j = g * 4 + jj
                for kc in range(8):
                    P.mm(mps[:, 2 * j:2 * j + 2], wa[:, kc, jj * 128:(jj + 1) * 128], S2b[:, kc, :],
                         start=(kc == 0), stop=(kc == 7))
        cb0 = VCOL[('ada_b', l)]
        P.tt(MOD[:, l, :, :], mps[:, 0:96].rearrange("p (j k) -> p j k", k=2),
             VEC[:, cb0:cb0 + 48].unsqueeze(2).to_broadcast([128, 48, 2]), ALU.add)
        for which in range(2):
            g0 = VCOL[('norm_g', l, which)]
            sc = MOD[:, l, 8 + 24 * which:16 + 24 * which, :]
            P.stt(AMOD[:, l, which, :, :], sc, 1.0, VEC[:, g0:g0 + 8].unsqueeze(2).to_broadcast([128, 8, 2]),
                  ALU.add, ALU.mult)

    def SH(l, which):
        return MOD[:, l, 24 * which:24 * which + 8, :]

    def GATE(l, which):
        return MOD[:, l, 16 + 24 * which:24 + 24 * which, :]

    P.act(SM[:, C_LG:C_LG + 16], BC[:, 0:16], AF.Exp, scale=-1.0)
    P.act(SM[:, C_LG:C_LG + 16], SM[:, C_LG:C_LG + 16], AF.Ln, bias=ONEC)
    P.ts(SM[:, C_LG:C_LG + 16], SM[:, C_LG:C_LG + 16], -1.0, ALU.mult)
    for o in range(2):
        layer_idx = 2 * o + 1
        lam_init = 0.8 - 0.6 * math.exp(-0.3 * layer_idx)
        dl = BC[:, 16 + 256 * o:16 + 256 * (o + 1)]
        tmp = AR.alloc((128,), F32) if o == 0 else tmp
        P.tt(tmp[:, 0:64], dl[:, 0:64], dl[:, 64:128], ALU.mult)
        P.tt(tmp[:, 64:128], dl[:, 128:192], dl[:, 192:256], ALU.mult)
        P.op('dve', lambda e, tmp=tmp: e.tensor_reduce(out=SM[:, C_TMP:C_TMP + 2],
                                                      in_=tmp.rearrange("p (a b) -> p a b", b=64),
                                                      axis=AX.X, op=ALU.add),
             [tmp], [SM[:, C_TMP:C_TMP + 2]])
        P.act(SM[:, C_TMP:C_TMP + 2], SM[:, C_TMP:C_TMP + 2], AF.Exp)
        P.tt(SM[:, C_NLAM + o:C_NLAM + o + 1], SM[:, C_TMP + 1:C_TMP + 2], SM[:, C_TMP:C_TMP + 1], ALU.subtract)
        P.ts(SM[:, C_NLAM + o:C_NLAM + o + 1], SM[:, C_NLAM + o:C_NLAM + o + 1], -lam_init, ALU.add)
        sc0 = VCOL[('subln', o)]
        P.ts(SM[:, C_GS + o:C_GS + o + 1], VEC[:, sc0:sc0 + 1], 1.0 - lam_init, ALU.mult)
    P.memset(SM[:, C_OML:C_OML + 8], 1.0)
    for d in range(2):
        a0 = VCOL[('lb', 0, d)]
        a1 = VCOL[('lb', 1, d)]
        dst = SM[:, C_OML + 8 + 4 * d:C_OML + 12 + 4 * d]
        P.tt(dst, VEC[:, a0:a0 + 4], VEC[:, a1:a1 + 4], ALU.subtract)
        P.act(dst, dst, AF.Exp, scale=-1.0)
        P.act(dst, dst, AF.Ln, bias=SM[:, C_ONE:C_ONE + 1])
        P.act(dst, dst, AF.Exp, scale=-1.0)

    def kcol(t0):
        return 1 if t0 < CT else 0

    def norm_mod(l, which):
        AR.reset()
        sqr = AR.ring(3, (512,), BF16)
        rsr = AR.ring(2, (512,), F32)
        tmr = AR.ring(3, (512,), F32)
        for (t0, n) in TB:
            k = kcol(t0)
            R = PS_T.next()
            for c in range(8):
                sq = sqr.next()[:, :n]
                P.act(sq, XT[:, c, t0:t0 + n], AF.Square)
                P.mm(R[:, :n], ONES, sq, start=(c == 0), stop=(c == 7))
            rs = rsr.next()[:, :n]
            rstd_from_sumsq(rs, R[:, :n], D)
            for c in range(8):
                tmp = tmr.next()[:, :n]
                P.stt(tmp, XT[:, c, t0:t0 + n], AMOD[:, l, which, c, k:k + 1], rs, ALU.mult, ALU.mult)
                P.act(HT[:, c, t0:t0 + n], tmp, AF.Identity, bias=SH(l, which)[:, c, k:k + 1])

    def proj(ps, w, src, t0, n, KC=8):
        for kc in range(KC):
            P.mm(ps, w[:, kc, :], src[:, kc, t0:t0 + n], start=(kc == 0), stop=(kc == KC - 1))

    def load_w(dst, src):
        P.dma(dst, src, eng='pool')

    def outproj(l, WO, OT, t0, n):
        k = kcol(t0)
        for i in range(8):
            ps = PS_T.next()
            P.mm(ps[:, :n], WO[:, i * 128:(i + 1) * 128], OT)
            P.stt(XT[:, i, t0:t0 + n], ps[:, :n], GATE(l, 0)[:, i, k:k + 1], XT[:, i, t0:t0 + n], ALU.mult, ALU.add)

    def sigmoid_act(dst, src, sign):
        P.act(dst, src, AF.Exp, scale=-float(sign))
        P.act(dst, dst, AF.Ln, bias=ONEC[:dst.shape[0], :])
        P.act(dst, dst, AF.Exp, scale=-1.0)

    def rstd_from_sumsq(rs, R, nfeat):
        P.act(rs, R, AF.Ln, scale=1.0 / nfeat, bias=EPSC[:rs.shape[0], :])
        P.act(rs, rs, AF.Exp, scale=-0.5)

    def rope(dst, ps, np_, t0, n, variant, raw, cosr, sinr, t1r):
        if t0 < CT:
            P.copy(dst, ps[:np_, :n], eng='act')
            return
        r = raw.next()[:np_, :n]
        P.copy(r, ps[:np_, :n], eng='act')
        pp = PS_T.next()
        P.mm(pp[:np_, :n], PERM[variant][:np_, :np_], r)
        co = cosr.next()[:np_, :n]
        si = sinr.next()[:np_, :n]
        P.dma(co, d_rope[variant, 0, 0:np_, t0 - CT:t0 - CT + n])
        P.dma(si, d_rope[variant, 1, 0:np_, t0 - CT:t0 - CT + n])
        t1 = t1r.next()[:np_, :n]
        P.tt(t1, r, co, ALU.mult)
        P.tt(si, pp[:np_, :n], si, ALU.mult)
        P.tt(dst, t1, si, ALU.add)

    def gla(dk, QS, KF, GBUF, VH, d, OACC, first, C):
        mark = AR.off
        ncl = 128 // C
        NCH = T // C
        B = AR.alloc((T,), F32)[:dk]
        GE = AR.alloc((80,), F32)[:dk]
        DTOT = AR.alloc((72,), F32)[:dk]
        QT_ = AR.alloc((T,), BF16)[:dk]
        KP = AR.alloc((T,), BF16)[:dk]
        KT = AR.alloc((NT, 128), BF16)
        ebr = AR.ring(3, (512,), F32)
        TOTL = AR.alloc((72,), F32)[:dk]
        k2r = AR.ring(2, (512,), BF16)
        vbr = AR.ring(2, (ncl, 128), BF16)
        amr = AR.ring(2, (128,), BF16)
        sbr = AR.ring(2, (ncl, 128), BF16)
        Sst = [AR.alloc((128,), F32)[:dk] for _ in range(4)]
        P.scan(GBUF[:, 32:32 + T], ONEC[:dk, :].to_broadcast([dk, T]), GBUF[:, 32:32 + T], 0.0, ALU.mult, ALU.add)
        P.memset(GE[:, 0:1], 0.0)
        P.copy(GE[:, 1:NCH + 1], GBUF[:, 32:32 + T].rearrange("p (c j) -> p c j", j=C)[:, :, C - 1])
        DTOT = DTOT[:, 0:NCH]
        TOTL = TOTL[:, 0:NCH]
        P.tt(TOTL, GE[:, 1:NCH + 1], GE[:, 0:NCH], ALU.subtract)
        P.act(DTOT, TOTL, AF.Exp)
        Bv = B.rearrange("p (c j) -> p c j", j=C)
        if d == 0:
            P.tt(Bv, GBUF[:, 32:32 + T].rearrange("p (c j) -> p c j", j=C),
                 GE[:, 0:NCH].unsqueeze(2).to_broadcast([dk, NCH, C]), ALU.subtract)
        else:
            P.tt(Bv, GE[:, 1:NCH + 1].unsqueeze(2).to_broadcast([dk, NCH, C]),
                 GBUF[:, 31:31 + T].rearrange("p (c j) -> p c j", j=C), ALU.subtract)
        P.ts(B, B, -80.0, ALU.max)
        for (t0, n) in TB:
            eb = ebr.next()[:dk, :n]
            P.act(eb, B[:, t0:t0 + n], AF.Exp)
            P.tt(QT_[:, t0:t0 + n], QS[:, t0:t0 + n], eb, ALU.mult)
            enb = ebr.next()[:dk, :n]
            P.act(enb, B[:, t0:t0 + n], AF.Exp, scale=-1.0)
            P.tt(KP[:, t0:t0 + n], KF[:, t0:t0 + n], enb, ALU.mult)
            e2 = ebr.next()[:dk, :n]
            P.tt(e2.rearrange("p (c j) -> p c j", j=C),
                 TOTL[:, t0 // C:(t0 + n) // C].unsqueeze(2).to_broadcast([dk, n // C, C]),
                 B[:, t0:t0 + n].rearrange("p (c j) -> p c j", j=C), ALU.subtract)
            P.act(e2, e2, AF.Exp)
            k2 = k2r.next()[:dk, :n]
            P.tt(k2, KF[:, t0:t0 + n], e2, ALU.mult)
            for i in range(n // 128):
                tt_ = t0 // 128 + i
                pt = PS_T.next()[:, 0:64].bitcast(BF16)
                P.transpose(pt[:, :dk], k2[:, i * 128:(i + 1) * 128], IDENT[:dk, :dk])
                P.copy(KT[:, tt_, :dk], pt[:, :dk], eng='act')
        order = list(range(NT)) if d == 0 else [1, 0] + list(range(NT - 1, 1, -1))
        corder = list(range(ncl)) if d == 0 else list(range(ncl - 1, -1, -1))
        P.memset(Sst[0], 0.0)
        scur = 0

        def emit_front(tt_):
            U = PS_A.next()
            if ncl == 1:
                P.mm(U[:dk, 0:128], KT[:, tt_, :dk], VH[:, tt_, :])
            else:
                vb = vbr.next()
                P.tt(vb, VH[:, tt_, :].unsqueeze(1).to_broadcast([128, ncl, 128]),
                     BMS[C].unsqueeze(2).to_broadcast([128, ncl, 128]), ALU.mult, eng='pool')
                P.mm(U[:dk, 0:ncl * 128], KT[:, tt_, :dk], vb.rearrange("p a b -> p (a b)"))
            AT_ = PS_T.next()
            P.mm(AT_[:, 0:128], KP[:, tt_ * 128:(tt_ + 1) * 128], QT_[:, tt_ * 128:(tt_ + 1) * 128])
            return U, AT_

        nxt = emit_front(order[0])
        for oi, tt_ in enumerate(order):
            U, AT_ = nxt
            if oi + 1 < NT:
                nxt = emit_front(order[oi + 1])
            am = amr.next()
            P.tt(am, AT_[:, 0:128], MASKS[C][d], ALU.mult)
            sb16 = sbr.next()
            for cl in corder:
                c = tt_ * ncl + cl
                P.copy(sb16[:dk, cl, :], Sst[scur], eng='act')
                P.stt(Sst[(scur + 1) % 4], Sst[scur], DTOT[:, c:c + 1], U[:dk, cl * 128:(cl + 1) * 128], ALU.mult, ALU.add)
                scur = (scur + 1) % 4
            O = PS_A.next()
            P.mm(O[:, 0:128], VH[:, tt_, :], am, start=True, stop=False)
            for cl in range(ncl):
                P.mm(O[:, cl * C:(cl + 1) * C], sb16[:dk, cl, :],
                     QT_[:, tt_ * 128 + cl * C:tt_ * 128 + (cl + 1) * C], start=False, stop=(cl == ncl - 1))
            oa = OACC[:, tt_ * 128:(tt_ + 1) * 128]
            if first:
                P.copy(oa, O[:, 0:128], eng='act')
            else:
                P.tt(oa, O[:, 0:128], oa, ALU.add)
        AR.reset(mark)

    def readout_gated(l, WO, OACC, gw, gain_col, t0, n, rings):
        sqr, rsr, tmr, otr = rings
        sq = sqr.next()[:, :n]
        P.act(sq, OACC[:, t0:t0 + n], AF.Square)
        R = PS_T.next()
        P.mm(R[:, :n], ONES, sq)
        rs = rsr.next()[:, :n]
        rstd_from_sumsq(rs, R[:, :n], 128)
        pg = PS_T.next()
        proj(pg[:, :n], gw, HT, t0, n)
        sg = tmr.next()[:, :n]
        sigmoid_act(sg, pg[:, :n], 1)
        P.tt(sg, pg[:, :n], sg, ALU.mult)
        t1 = tmr.next()[:, :n]
        if gain_col is None:
            P.tt(t1, OACC[:, t0:t0 + n], rs, ALU.mult)
        else:
            P.stt(t1, OACC[:, t0:t0 + n], gain_col, rs, ALU.mult, ALU.mult)
        ot = otr.next()[:, :n]
        P.tt(ot, t1, sg, ALU.mult)
        outproj(l, WO, ot, t0, n)

    def attn_block(pairs, t0, n, scale, ptr, with_den=True):
        kts = [0, 1] if t0 < CT else list(range(NT))
        res = []
        for (Kt, Qt, Vl) in pairs:
            O = PS_A.next()
            DEN = PS_A.next() if with_den else None

            def issue_S(kt):
                S = PS_T.next()
                P.mm(S[:, :n], Kt[:, kt * 128:(kt + 1) * 128], Qt[:, t0:t0 + n])
                return S
            pend = [issue_S(kt) for kt in kts[:2]]
            for ki, kt in enumerate(kts):
                S = pend.pop(0)
                if ki + 2 < len(kts):
                    pend.append(issue_S(kts[ki + 2]))
                pt = ptr.next()[:, :n]
                P.act(pt, S[:, :n], AF.Exp, scale=scale)
                P.mm(O[:, :n], Vl(kt), pt, start=(ki == 0), stop=(ki == len(kts) - 1))
                if with_den:
                    P.mm(DEN[:, :n], ONES, pt, start=(ki == 0), stop=(ki == len(kts) - 1))
            res.append((O, DEN))
        return res

    def even_mixer(l):
        e = l // 2
        W = d_ewin[e]
        AR.reset()
        WOr = AR.ring(2, (1024,), BF16)
        hmark = AR.off
        for h in range(4):
            AR.reset(hmark)
            WH = AR.alloc((8, 5, 128), BF16)
            for g in range(5):
                load_w(WH[:, :, g, :], W[:, g * 512 + h * 128:g * 512 + (h + 1) * 128].rearrange("(kc p) c -> p kc c", p=128))
            WO = WOr.next()
            load_w(WO, d_wout[l, h * 128:(h + 1) * 128, :])
            VH = AR.alloc((NT, 128), BF16)
            QS = AR.alloc((T,), F32)
            KF = AR.alloc((T,), BF16)
            GBUF = AR.alloc((32 + T,), F32)
            OACC = AR.alloc((T,), F32)
            gmark = AR.off
            tmr = AR.ring(2, (512,), F32)
            for tt_ in range(NT):
                ps = PS_T.next()
                for kc in range(8):
                    P.mm(ps[:, 0:128], HT[:, kc, tt_ * 128:(tt_ + 1) * 128], WH[:, kc, 1, :], start=(kc == 0), stop=(kc == 7))
                P.copy(VH[:, tt_, :], ps[:, 0:128], eng='act')
            for (t0, n) in TB:
                ps = PS_T.next()
                proj(ps[:, :n], WH[:, :, 0, :], HT, t0, n)
                P.act(QS[:, t0:t0 + n], ps[:, :n], AF.Copy, scale=128 ** -0.5)
            P.memset(GBUF[:, 0:32], 0.0)
            for d in range(2):
                for (t0, n) in TB:
                    ps = PS_T.next()
                    proj(ps[:, :n], WH[:, :, 2 + d, :], HT, t0, n)
                    sk = tmr.next()[:, :n]
                    sigmoid_act(sk, ps[:, :n], -1)
                    P.ts(GBUF[:, 32 + t0:32 + t0 + n], sk, SM[:, C_OML + 8 * e + 4 * d + h:C_OML + 8 * e + 4 * d + h + 1], ALU.mult)
                    P.copy(KF[:, t0:t0 + n], GBUF[:, 32 + t0:32 + t0 + n], eng='pool')
                P.act(GBUF[:, 32:32 + T], GBUF[:, 32:32 + T], AF.Ln, scale=-1.0, bias=ONEC)
                AR.reset(gmark)
                gla(128, QS, KF, GBUF, VH, d, OACC, first=(d == 0), C=HGRN_C[e])
                AR.reset(gmark)
                tmr = AR.ring(2, (512,), F32)
            AR.reset(gmark)
            rings = (AR.ring(2, (512,), BF16), AR.ring(2, (512,), F32), AR.ring(3, (512,), F32), AR.ring(2, (512,), BF16))
            gcol = VEC[:, VCOL[('hgrn_g', e)]:VCOL[('hgrn_g', e)] + 1]
            for (t0, n) in TB:
                readout_gated(l, WO, OACC, WH[:, :, 4, :], gcol, t0, n, rings)
        AR.reset(hmark)
        WUQ = AR.alloc((3, 768), BF16)
        load_w(WUQ, d_wuq[e].rearrange("(kc p) f -> p kc f", p=128))
        WUKV = AR.alloc((2, 1024), BF16)
        load_w(WUKV, d_wukv[e].rearrange("(kc p) f -> p kc f", p=128))
        CQN = AR.alloc((3, T), BF16)
        CKV = AR.alloc((2, T), BF16)
        KRT = AR.alloc((T,), BF16)
        raw = AR.ring(2, (512,), BF16)
        cosr = AR.ring(2, (512,), F32)
        sinr = AR.ring(2, (512,), F32)
        t1r = AR.ring(2, (512,), F32)
        lmark = AR.off
        WL = AR.alloc((8, 640), BF16)
        load_w(WL, W[:, 2560:3200].rearrange("(kc p) f -> p kc f", p=128))
        WKR = AR.alloc((8, 96), BF16)
        P.memset(WKR, 0.0)
        load_w(WKR[:, :, 64:96], W[:, 3200:3232].rearrange("(kc p) f -> p kc f", p=128))
        c32 = AR.alloc((3, 512), F32)
        sqr = AR.ring(2, (512,), BF16)
        rsr = AR.ring(2, (512,), F32)
        for (t0, n) in TB:
            for (c0, nch, dst, gname) in ((0, 3, CQN, 'qn_g'), (3, 2, CKV, 'kvn_g')):
                R = PS_A.next()
                for c in range(nch):
                    ps = PS_T.next()
                    proj(ps[:, :n], WL[:, :, (c0 + c) * 128:(c0 + c + 1) * 128], HT, t0, n)
                    P.copy(c32[:, c, :n], ps[:, :n], eng='act')
                    sq = sqr.next()[:, :n]
                    P.act(sq, ps[:, :n], AF.Square)
                    P.mm(R[:, :n], ONES, sq, start=(c == 0), stop=(c == nch - 1))
                rs = rsr.next()[:, :n]
                rstd_from_sumsq(rs, R[:, :n], 128 * nch)
                g0 = VCOL[(gname, e)]
                for c in range(nch):
                    P.stt(dst[:, c, t0:t0 + n], c32[:, c, :n], VEC[:, g0 + c:g0 + c + 1], rs, ALU.mult, ALU.mult)
            ps = PS_T.next()
            proj(ps[:96, :n], WKR, HT, t0, n)
            rope(KRT[:96, t0:t0 + n], ps, 96, t0, n, ROPE_MLA, raw, cosr, sinr, t1r)
        AR.reset(lmark)
        pmark = AR.off
        sc = 96 ** -0.5
        for hp in range(4):
            AR.reset(pmark)
            WO = WOr.next()
            load_w(WO, d_wout[l, (4 + hp) * 128:(5 + hp) * 128, :])
            QTs, KTs, VAs = [], [], []
            for hh in range(2):
                h = 2 * hp + hh
                QTh = AR.alloc((T,), BF16)
                KTh = AR.alloc((T,), BF16)
                VA = AR.alloc((NT, 128), BF16)
                P.memset(VA, 1.0, eng='pool')
                for (t0, n) in TB:
                    ps = PS_T.next()
                    proj(ps[:96, :n], WUQ[:, :, 96 * h:96 * h + 96], CQN, t0, n, KC=3)
                    rope(QTh[:96, t0:t0 + n], ps, 96, t0, n, ROPE_MLA, raw, cosr, sinr, t1r)
                    ps = PS_T.next()
                    proj(ps[:64, :n], WUKV[:, :, 128 * h:128 * h + 64], CKV, t0, n, KC=2)
                    P.copy(KTh[:64, t0:t0 + n], ps[:64, :n], eng='act')
                P.copy(KTh[64:96, :], KRT[64:96, :], eng='pool')
                for tt_ in range(NT):
                    ps = PS_T.next()
                    for kc in range(2):
                        P.mm(ps[:, 0:64], CKV[:, kc, tt_ * 128:(tt_ + 1) * 128], WUKV[:, kc, 128 * h + 64:128 * h + 128],
                             start=(kc == 0), stop=(kc == 1))
                    P.copy(VA[:, tt_, 64 * hh:64 * hh + 64], ps[:, 0:64], eng='act')
                QTs.append(QTh)
                KTs.append(KTh)
                VAs.append(VA)
            ptr = AR.ring(4, (512,), BF16)
            rdr = AR.ring(1, (512,), F32)
            otr = AR.ring(2, (512,), BF16)
            for (t0, n) in TB:
                pairs = [(KTs[hh][:96, :], QTs[hh][:96, :], (lambda kt, hh=hh: VAs[hh][:, kt, :])) for hh in range(2)]
                res = attn_block(pairs, t0, n, sc, ptr, with_den=False)
                ot = otr.next()[:, :n]
                for hh in range(2):
                    O, _ = res[hh]
                    rd = rdr.next()[:, :n]
                    sl = slice(64 * hh, 64 * hh + 64)
                    so = slice(64 * (1 - hh), 64 * (1 - hh) + 64)
                    P.act(rd[sl, :], O[so, :n], AF.Ln)
                    P.act(rd[sl, :], rd[sl, :], AF.Exp, scale=-1.0)
                    P.tt(ot[sl, :], O[sl, :n], rd[sl, :], ALU.mult)
                outproj(l, WO, ot, t0, n)

    def odd_mixer(l):
        o = l // 2
        W = d_owin[o]
        AR.reset()
        WOr = AR.ring(2, (1024,), BF16)
        hmark = AR.off
        for h in range(4):
            AR.reset(hmark)
            WR = AR.alloc((8, 384), BF16)
            Wv = W.rearrange("(kc p) f -> p kc f", p=128)
            load_w(WR[:, :, 0:64], Wv[:, :, 64 * h:64 * h + 64])
            load_w(WR[:, :, 64:128], Wv[:, :, 256 + 64 * h:256 + 64 * h + 64])
            load_w(WR[:, :, 128:256], Wv[:, :, 512 + 128 * h:512 + 128 * h + 128])
            load_w(WR[:, :, 256:384], Wv[:, :, 1024 + 128 * h:1024 + 128 * h + 128])
            WO = WOr.next()
            load_w(WO, d_wout[l, h * 128:(h + 1) * 128, :])
            VH = AR.alloc((NT, 128), BF16)
            QS = AR.alloc((T,), F32)
            KF = AR.alloc((T,), BF16)
            GBUF = AR.alloc((32 + T,), F32)
            OACC = AR.alloc((T,), F32)
            gmark = AR.off
            raw = AR.ring(2, (512,), BF16)
            cosr = AR.ring(2, (512,), F32)
            sinr = AR.ring(2, (512,), F32)
            t1r = AR.ring(2, (512,), F32)
            kfr = AR.ring(2, (512,), F32)
            for tt_ in range(NT):
                ps = PS_T.next()
                for kc in range(8):
                    P.mm(ps[:, 0:128], HT[:, kc, tt_ * 128:(tt_ + 1) * 128], WR[:, kc, 128:256], start=(kc == 0), stop=(kc == 7))
                P.copy(VH[:, tt_, :], ps[:, 0:128], eng='act')
            for (t0, n) in TB:
                ps = PS_T.next()
                proj(ps[:64, :n], WR[:, :, 0:64], HT, t0, n)
                rope(QS[:64, t0:t0 + n], ps, 64, t0, n, ROPE_RET, raw, cosr, sinr, t1r)
                ps = PS_T.next()
                proj(ps[:64, :n], WR[:, :, 64:128], HT, t0, n)
                kf = kfr.next()[:64, :n]
                rope(kf, ps, 64, t0, n, ROPE_RET, raw, cosr, sinr, t1r)
                P.ts(KF[:64, t0:t0 + n], kf, 64 ** -0.5, ALU.mult)
            P.memset(GBUF[:, 0:32], 0.0)
            for d in range(2):
                col = C_LG + 8 * o + 4 * d + h
                P.act(GBUF[:64, 32:32 + T], ZEROC[:64, :].to_broadcast([64, T]), AF.Identity, scale=0.0,
                      bias=SM[:64, col:col + 1])
                AR.reset(gmark)
                gla(64, QS[:64], KF[:64], GBUF[:64], VH, d, OACC, first=(d == 0), C=RET_C)
            AR.reset(gmark)
            rings = (AR.ring(2, (512,), BF16), AR.ring(2, (512,), F32), AR.ring(3, (512,), F32), AR.ring(2, (512,), BF16))
            for (t0, n) in TB:
                readout_gated(l, WO, OACC, WR[:, :, 256:384], None, t0, n, rings)
        sc = 64 ** -0.5
        for h in range(4):
            AR.reset(hmark)
            WD_ = AR.alloc((8, 384), BF16)
            Wv = W.rearrange("(kc p) f -> p kc f", p=128)
            load_w(WD_[:, :, 0:128], Wv[:, :, 1536 + 128 * h:1536 + 128 * h + 128])
            load_w(WD_[:, :, 128:256], Wv[:, :, 2048 + 128 * h:2048 + 128 * h + 128])
            load_w(WD_[:, :, 256:384], Wv[:, :, 2560 + 128 * h:2560 + 128 * h + 128])
            WO = WOr.next()
            load_w(WO, d_wout[l, (4 + h) * 128:(5 + h) * 128, :])
            VH = AR.alloc((NT, 128), BF16)
            QD = AR.alloc((T,), BF16)
            KD = AR.alloc((T,), BF16)
            raw = AR.ring(2, (512,), BF16)
            cosr = AR.ring(2, (512,), F32)
            sinr = AR.ring(2, (512,), F32)
            t1r = AR.ring(2, (512,), F32)
            for tt_ in range(NT):
                ps = PS_T.next()
                for kc in range(8):
                    P.mm(ps[:, 0:128], HT[:, kc, tt_ * 128:(tt_ + 1) * 128], WD_[:, kc, 256:384], start=(kc == 0), stop=(kc == 7))
                P.copy(VH[:, tt_, :], ps[:, 0:128], eng='act')
            for (t0, n) in TB:
                ps = PS_T.next()
                proj(ps[:, :n], WD_[:, :, 0:128], HT, t0, n)
                rope(QD[:, t0:t0 + n], ps, 128, t0, n, ROPE_DIFF, raw, cosr, sinr, t1r)
                ps = PS_T.next()
                proj(ps[:, :n], WD_[:, :, 128:256], HT, t0, n)
                rope(KD[:, t0:t0 + n], ps, 128, t0, n, ROPE_DIFF, raw, cosr, sinr, t1r)
            ptr = AR.ring(4, (512,), BF16)
            f32r = AR.ring(4, (512,), F32)
            sqr = AR.ring(2, (512,), BF16)
            otr = AR.ring(2, (512,), BF16)
            for (t0, n) in TB:
                pairs = [(KD[64 * m:64 * m + 64, :], QD[64 * m:64 * m + 64, :], (lambda kt: VH[:, kt, :])) for m in range(2)]
                res = attn_block(pairs, t0, n, sc, ptr)
                a = []
                for m in range(2):
                    O, DEN = res[m]
                    rd = f32r.next()[:, :n]
                    P.act(rd, DEN[:, :n], AF.Ln)
                    P.act(rd, rd, AF.Exp, scale=-1.0)
                    P.tt(rd, O[:, :n], rd, ALU.mult)
                    a.append(rd)
                ov = f32r.next()[:, :n]
                P.stt(ov, a[1], SM[:, C_NLAM + o:C_NLAM + o + 1], a[0], ALU.mult, ALU.add)
                sq = sqr.next()[:, :n]
                P.act(sq, ov, AF.Square)
                R = PS_T.next()
                P.mm(R[:, :n], ONES, sq)
                rs = f32r.next()[:, :n]
                rstd_from_sumsq(rs, R[:, :n], 128)
                ot = otr.next()[:, :n]
                P.stt(ot, ov, SM[:, C_GS + o:C_GS + o + 1], rs, ALU.mult, ALU.mult)
                outproj(l, WO, ot, t0, n)

    def ffn(l):
        SBS = [TB[0:3], TB[3:5]]
        for sbk in SBS:
            AR.reset()
            base = sbk[0][0]
            ntok = sum(n for _, n in sbk)
            ACTT = AR.alloc((22, ntok), BF16)
            wgr = AR.ring(2, (8, 256), BF16)
            wur = AR.ring(2, (8, 256), BF16)
            sgr = AR.ring(3, (512,), F32)
            for jg in range(11):
                wg = wgr.next()
                wu = wur.next()
                load_w(wg, d_wg[l, :, jg * 256:(jg + 1) * 256].rearrange("(kc p) f -> p kc f", p=128))
                load_w(wu, d_wu[l, :, jg * 256:(jg + 1) * 256].rearrange("(kc p) f -> p kc f", p=128))
                for jj in range(2):
                    j = jg * 2 + jj
                    for (t0, n) in sbk:
                        pg = PS_T.next()
                        proj(pg[:, :n], wg[:, :, jj * 128:(jj + 1) * 128], HT, t0, n)
                        pu = PS_A.next()
                        proj(pu[:, :n], wu[:, :, jj * 128:(jj + 1) * 128], HT, t0, n)
                        sg = sgr.next()[:, :n]
                        P.act(sg, pg[:, :n], AF.Silu)
                        P.tt(ACTT[:, j, t0 - base:t0 - base + n], sg, pu[:, :n], ALU.mult)
            wdr = AR.ring(2, (22, 128), BF16)
            for i in range(8):
                wd = wdr.next()
                load_w(wd, d_wd[l, :, i * 128:(i + 1) * 128].rearrange("(j p) c -> p j c", p=128))
                for (t0, n) in sbk:
                    k = kcol(t0)
                    ps = PS_T.next()
                    for j in range(22):
                        P.mm(ps[:, :n], wd[:, j, :], ACTT[:, j, t0 - base:t0 - base + n], start=(j == 0), stop=(j == 21))
                    P.stt(XT[:, i, t0:t0 + n], ps[:, :n], GATE(l, 1)[:, i, k:k + 1], XT[:, i, t0:t0 + n], ALU.mult, ALU.add)

    for l in range(n_layers):
        norm_mod(l, 0)
        if l % 2 == 0:
            even_mixer(l)
        else:
            odd_mixer(l)
        norm_mod(l, 1)
        ffn(l)

    AR.reset()
    sqr = AR.ring(3, (512,), BF16)
    rsr = AR.ring(2, (512,), F32)
    outr = AR.ring(3, (512,), F32)
    fg = VCOL['final_g']
    for (t0, n) in TB[1:]:
        R = PS_T.next()
        for c in range(8):
            sq = sqr.next()[:, :n]
            P.act(sq, XT[:, c, t0:t0 + n], AF.Square)
            P.mm(R[:, :n], ONES, sq, start=(c == 0), stop=(c == 7))
        rs = rsr.next()[:, :n]
        rstd_from_sumsq(rs, R[:, :n], D)
        for c in range(8):
            ob = outr.next()[:, :n]
            P.stt(ob, XT[:, c, t0:t0 + n], VEC[:, fg + c:fg + c + 1], rs, ALU.mult, ALU.mult)
            P.dma(d_out[c * 128:(c + 1) * 128, t0 - CT:t0 - CT + n], ob, is_output=True)
    return P.finalize()


def _rope_tables():
    tab = np.zeros((3, 2, 128, 2048), np.float32)
    tab[:, 0] = 1.0
    perm = np.zeros((3, 128, 128), np.float32)
    n = np.arange(2048, dtype=np.float32)
    row = np.floor(n / 64.0)
    col = n - 64.0 * row

    def fill(v, p0, width, pos):
        half = width // 2
        inv = (10000.0 ** (-np.arange(half, dtype=np.float32) / half)).astype(np.float32)
        ang = pos[None, :].astype(np.float32) * inv[:, None]
        c, s = np.cos(ang).astype(np.float32), np.sin(ang).astype(np.float32)
        tab[v, 0, p0:p0 + half] = c
        tab[v, 0, p0 + half:p0 + width] = c
        tab[v, 1, p0:p0 + half] = -s
        tab[v, 1, p0 + half:p0 + width] = s
        for j in range(half):
            perm[v, p0 + j + half, p0 + j] = 1.0
            perm[v, p0 + j, p0 + j + half] = 1.0
    for g in range(2):
        fill(ROPE_RET, 64 * g, 64, n)
        fill(ROPE_DIFF, 64 * g, 32, row)
        fill(ROPE_DIFF, 64 * g + 32, 32, col)
    fill(ROPE_MLA, 64, 16, row)
    fill(ROPE_MLA, 80, 16, col)
    return tab, perm


def _consts():
    tab, perm = _rope_tables()
    cb = np.zeros((128, NCB), np.float32)
    cb[:, CB_IDENT:CB_IDENT + 128] = np.eye(128, dtype=np.float32)
    cb[:, CB_ONES:CB_ONES + 128] = 1.0
    s = np.arange(128)[:, None]
    t = np.arange(128)[None, :]
    same = (s // 32) == (t // 32)
    cb[:, CB_MF:CB_MF + 128] = (same & (s <= t)).astype(np.float32)
    cb[:, CB_MB:CB_MB + 128] = (same & (s >= t)).astype(np.float32)
    cb[:, CB_BM:CB_BM + 4] = ((np.arange(128)[:, None] // 32) == np.arange(4)[None, :]).astype(np.float32)
    same = (s // 64) == (t // 64)
    cb[:, CB_M64:CB_M64 + 128] = (same & (s <= t)).astype(np.float32)
    cb[:, CB_M64 + 128:CB_M64 + 256] = (same & (s >= t)).astype(np.float32)
    cb[:, CB_BM2:CB_BM2 + 2] = ((np.arange(128)[:, None] // 64) == np.arange(2)[None, :]).astype(np.float32)
    cb[:, CB_M128:CB_M128 + 128] = (s <= t).astype(np.float32)
    cb[:, CB_M128 + 128:CB_M128 + 256] = (s >= t).astype(np.float32)
    for v in range(3):
        cb[:, CB_PERM + 128 * v:CB_PERM + 128 * (v + 1)] = perm[v]
    return cb, tab


_PROG_CACHE = {}


def _make_inputs(inputs, b):
    f = lambda a: np.ascontiguousarray(np.asarray(a, dtype=np.float32))
    rows = np.zeros((NV, 128), np.float32)

    def put(name, arr):
        a = f(arr).reshape(-1, 128)
        rows[VCOL[name]:VCOL[name] + a.shape[0]] = a
    put('c', inputs['c'][b])
    put('cctx', inputs['c_ctx'])
    for l in range(4):
        put(('ada_b', l), inputs['ada_b'][l])
        for j in range(2):
            put(('norm_g', l, j), inputs['norm_g'][l, j])
    put('final_g', inputs['final_norm_g'])
    for e in range(2):
        for d in range(2):
            put(('lb', e, d), inputs['hgrn_lb_logits'][e, d])
        put(('hgrn_g', e), inputs['hgrn_norm_g'][e])
        put(('qn_g', e), inputs['mla_q_norm_g'][e])
        put(('kvn_g', e), inputs['mla_kv_norm_g'][e])
    for o in range(2):
        put(('subln', o), inputs['diff_subln_g'][o])
    vecs = np.ascontiguousarray(rows.T)
    bc = np.zeros((128, 16 + 512), np.float32)
    bc[:, 0:16] = f(inputs['ret_decay_logits']).reshape(1, 16)
    bc[:, 16:] = f(inputs['diff_lambda']).reshape(1, 512)
    xT = np.ascontiguousarray(np.concatenate([f(inputs['ctx'][b]), f(inputs['x'][b])], axis=0).T)
    return xT, vecs, bc


def kernel(n_layers=4, core_ids=None, **inputs):
    inputs = {k: np.asarray(v) for k, v in inputs.items()}
    if n_layers not in _PROG_CACHE:
        _PROG_CACHE[n_layers] = build_program(n_layers)
    nc = _PROG_CACHE[n_layers]
    cb, tab = _consts()
    f = lambda a: np.ascontiguousarray(np.asarray(a, dtype=np.float32))
    shared = {
        "cb": cb, "rope": tab,
        "ada_w": f(inputs['ada_w']), "mix_w_out": f(inputs['mix_w_out']),
        "ffn_w_gate": f(inputs['ffn_w_gate']), "ffn_w_up": f(inputs['ffn_w_up']), "ffn_w_down": f(inputs['ffn_w_down']),
        "even_w_in": f(inputs['even_w_in']), "mla_w_uq": f(inputs['mla_w_uq']), "mla_w_ukv": f(inputs['mla_w_ukv']),
        "odd_w_in": f(inputs['odd_w_in']),
    }
    cores = list(range(8)) if core_ids is None else core_ids
    in_maps = []
    for b in cores:
        xT, vecs, bc = _make_inputs(inputs, b)
        m = dict(shared)
        m.update({"xT": xT, "vecs": vecs, "bc": bc})
        in_maps.append(m)
    res = run_bass_kernel_spmd(nc, in_maps, core_ids=list(range(len(cores))))
    outs = [np.ascontiguousarray(np.asarray(r["outT"]).T) for r in res.results]
    return np.stack(outs, axis=0).astype(np.float32)
```

```python
import numpy as np
from contextlib import ExitStack
import concourse.bass as bass
import concourse.mybir as mybir
from concourse.bass_utils import run_bass_kernel_spmd

F32 = mybir.dt.float32
BF16 = mybir.dt.bfloat16
AF = mybir.ActivationFunctionType
ALU = mybir.AluOpType
AX = mybir.AxisListType
_ESZ = {F32: 4, BF16: 2}

PHASE = 8000
NSWDGE = 4
NDSEM = 24


def _is_dram(ap):
    return 'DRam' in type(ap.tensor).__name__


def _bbox(ap):
    steps = ap.ap
    es = _ESZ.get(ap.dtype, 4)
    pstep, pcnt = steps[0]
    off = ap.offset
    if pstep:
        p0 = off // pstep
        f0 = off % pstep
    else:
        p0, f0 = 0, off
    ext = 1
    for s, c in steps[1:]:
        ext += (c - 1) * abs(s)
    return (p0, p0 + pcnt, f0 * es, (f0 + ext) * es)


class Prog:
    ENGS = ('pe', 'act', 'dve', 'pool', 'sp')

    def __init__(self):
        self.nc = bass.Bass("TRN2", target_bir_lowering=False)
        self.q = {e: [] for e in self.ENGS}
        self.cnt = {e: 0 for e in self.ENGS}
        self.recs = {}
        self.waited = {e: {} for e in self.ENGS}
        self.ndma = 0
        self.dma_tokens = []
        self.out_tokens = []
        self.same_engine_sync = True

    def sb(self, name, shape, dtype=F32):
        return self.nc.alloc_sbuf_tensor(name, list(shape), dtype).ap()

    def ps(self, name, shape, dtype=F32):
        return self.nc.alloc_psum_tensor(name, list(shape), dtype).ap()

    def dram(self, name, shape, dtype=F32, kind="ExternalInput"):
        return self.nc.dram_tensor(name, list(shape), dtype, kind=kind).ap()

    def _deps(self, eng, token, reads, writes, self_sync):
        deps = {}

        def add(tok):
            k, v = tok
            if deps.get(k, 0) < v:
                deps[k] = v

        for ap in reads:
            if ap is None or _is_dram(ap):
                continue
            bb = _bbox(ap)
            lst = self.recs.setdefault(ap.tensor.name, [])
            for r in lst:
                if r[5] and r[0] < bb[1] and bb[0] < r[1] and r[2] < bb[3] and bb[2] < r[3]:
                    add(r[4])
        for ap in writes:
            if ap is None or _is_dram(ap):
                continue
            bb = _bbox(ap)
            lst = self.recs.setdefault(ap.tensor.name, [])
            for r in lst:
                if r[0] < bb[1] and bb[0] < r[1] and r[2] < bb[3] and bb[2] < r[3]:
                    add(r[4])
        for ap in reads:
            if ap is None or _is_dram(ap):
                continue
            bb = _bbox(ap)
            lst = self.recs[ap.tensor.name]
            if token[0][0] == 'E':
                lst[:] = [r for r in lst if not ((not r[5]) and r[4][0] == token[0]
                                                 and bb[0] <= r[0] and r[1] <= bb[1]
                                                 and bb[2] <= r[2] and r[3] <= bb[3])]
            lst.append((bb[0], bb[1], bb[2], bb[3], token, False))
        for ap in writes:
            if ap is None or _is_dram(ap):
                continue
            bb = _bbox(ap)
            lst = self.recs[ap.tensor.name]
            lst[:] = [r for r in lst if not (bb[0] <= r[0] and r[1] <= bb[1]
                                             and bb[2] <= r[2] and r[3] <= bb[3])]
            lst.append((bb[0], bb[1], bb[2], bb[3], token, True))
        out = []
        w = self.waited[eng]
        for k, v in deps.items():
            if k[0] == 'E' and k[1] == eng and not self_sync:
                continue
            if w.get(k, 0) >= v:
                continue
            w[k] = v
            out.append((k, v))
        return out

    def op(self, eng, fn, reads, writes, self_sync=None):
        if self_sync is None:
            self_sync = self.same_engine_sync
        idx = self.cnt[eng]
        self.cnt[eng] += 1
        token = (('E', eng, idx // PHASE), idx % PHASE + 1)
        waits = self._deps(eng, token, reads, writes, self_sync)
        self.q[eng].append((waits, fn, token, 1))
        return token

    def dma(self, out, in_, eng='sp', is_output=False, **kw):
        i = self.ndma
        self.ndma += 1
        token = (('D', i % NDSEM), 16 * (i // NDSEM + 1))
        waits = self._deps(eng, token, [in_], [out], True)
        if i >= NDSEM:
            pk, pv = self.dma_tokens[i - NDSEM]
            if self.waited[eng].get(pk, 0) < pv:
                self.waited[eng][pk] = pv
                waits.append((pk, pv))
        if eng == 'pool':
            lst = self.__dict__.setdefault('pool_dma_tokens', [])
            if len(lst) >= NSWDGE:
                pk, pv = lst[-NSWDGE]
                if self.waited[eng].get(pk, 0) < pv:
                    self.waited[eng][pk] = pv
                    waits.append((pk, pv))
            lst.append(token)
        self.dma_tokens.append(token)
        if is_output:
            self.out_tokens.append(token)
        self.q[eng].append((waits, lambda e: e.dma_start(out=out, in_=in_, **kw), token, 16))
        return token

    def mm(self, out, lhsT, rhs, start=True, stop=True, **kw):
        return self.op('pe', lambda e: e.matmul(out, lhsT, rhs, start=start, stop=stop, **kw),
                       [lhsT, rhs] + ([] if start else [out]), [out], self_sync=False)

    def transpose(self, out, in_, ident):
        return self.op('pe', lambda e: e.transpose(out, in_, ident), [in_, ident], [out], self_sync=False)

    def act(self, out, in_, func, scale=1.0, bias=0.0, eng='act', accum_out=None):
        rd = [in_]
        if not isinstance(scale, (int, float)):
            rd.append(scale)
        if not isinstance(bias, (int, float)):
            rd.append(bias)
        kw = {}
        if accum_out is not None:
            kw['accum_out'] = accum_out
        return self.op(eng, lambda e: e.activation(out=out, in_=in_, func=func, scale=scale, bias=bias, **kw),
                       rd, [out, accum_out])

    def tt(self, out, in0, in1, op, eng='dve'):
        return self.op(eng, lambda e: e.tensor_tensor(out=out, in0=in0, in1=in1, op=op), [in0, in1], [out])

    def ts(self, out, in0, s1, op0, s2=None, op1=None, eng='dve'):
        rd = [in0]
        if not isinstance(s1, (int, float)):
            rd.append(s1)
        if s2 is not None and not isinstance(s2, (int, float)):
            rd.append(s2)
        kw = {}
        if op1 is not None:
            kw['op1'] = op1
        return self.op(eng, lambda e: e.tensor_scalar(out=out, in0=in0, scalar1=s1, scalar2=s2, op0=op0, **kw),
                       rd, [out])

    def stt(self, out, in0, scalar, in1, op0, op1, eng='dve'):
        rd = [in0, in1]
        if not isinstance(scalar, (int, float)):
            rd.append(scalar)
        return self.op(eng, lambda e: e.scalar_tensor_tensor(out=out, in0=in0, scalar=scalar, in1=in1, op0=op0, op1=op1),
                       rd, [out])

    def copy(self, out, in_, eng='dve'):
        if eng == 'act':
            return self.op(eng, lambda e: e.copy(out=out, in_=in_), [in_], [out])
        return self.op(eng, lambda e: e.tensor_copy(out=out, in_=in_), [in_], [out])

    def memset(self, out, val, eng='dve'):
        return self.op(eng, lambda e: e.memset(out, val), [], [out])

    def recip(self, out, in_, eng='dve'):
        return self.op(eng, lambda e: e.reciprocal(out=out, in_=in_), [in_], [out])

    def scan(self, out, data0, data1, initial, op0, op1):
        rd = [data0, data1]
        if not isinstance(initial, (int, float)):
            rd.append(initial)
        return self.op('dve', lambda e: e.tensor_tensor_scan(out=out, data0=data0, data1=data1, initial=initial,
                                                             op0=op0, op1=op1), rd, [out])

    def finalize(self):
        nc = self.nc
        fw = []
        for k, v in self.out_tokens:
            if self.waited['sp'].get(k, 0) < v:
                self.waited['sp'][k] = v
                fw.append((k, v))
        self.q['sp'].append((fw, None, None, 0))
        with ExitStack() as st:
            sems = {}

            def sem(k):
                if k not in sems:
                    sems[k] = st.enter_context(nc.semaphore("s_" + "_".join(str(x) for x in k)))
                return sems[k]

            for e in self.ENGS:
                for ph in range((self.cnt[e] + PHASE - 1) // PHASE):
                    sem(('E', e, ph))
            for i in range(min(self.ndma, NDSEM)):
                sem(('D', i))
            block = st.enter_context(nc.Block())

            def replay(name):
                def body(eng):
                    for waits, fn, token, inc in self.q[name]:
                        for k, v in waits:
                            eng.wait_ge(sem(k), v)
                        if fn is not None:
                            fn(eng).then_inc(sem(token[0]), inc)
                return body

            block.sync(replay('sp'))
            block.tensor(replay('pe'))
            block.scalar(replay('act'))
            block.vector(replay('dve'))
            block.gpsimd(replay('pool'))
        return nc


import math

D = 1024
T = 2304
CT = 256
NT = 18
TB = [(0, 256), (256, 512), (768, 512), (1280, 512), (1792, 512)]
FFN_H = 2816
EPS = 1e-6
ROPE_RET, ROPE_DIFF, ROPE_MLA = 0, 1, 2


class Ring:
    def __init__(self, aps):
        self.aps = aps
        self.i = 0

    def next(self):
        a = self.aps[self.i % len(self.aps)]
        self.i += 1
        return a


class Arena:
    def __init__(self, P, name, nbytes):
        self.ap = P.sb(name, [128, nbytes // 4], F32)
        self.n = nbytes // 4
        self.off = 0

    def reset(self, to=0):
        self.off = to

    def alloc(self, shape, dtype=F32):
        n = 1
        for s in shape:
            n *= s
        words = (n * (4 if dtype == F32 else 2) + 3) // 4
        words = (words + 7) // 8 * 8
        assert self.off + words <= self.n, ("arena overflow", self.off, words, self.n)
        v = self.ap[:, self.off:self.off + words]
        self.off += words
        if dtype != F32:
            v = v.bitcast(dtype)
        v = v[:, 0:n]
        if len(shape) == 2:
            v = v.rearrange("p (a b) -> p a b", b=shape[1])
        elif len(shape) == 3:
            v = v.rearrange("p (a b c) -> p a b c", b=shape[1], c=shape[2])
        return v

    def ring(self, k, shape, dtype=F32):
        return Ring([self.alloc(shape, dtype) for _ in range(k)])


def vec_layout():
    cols = {}
    n = 0

    def add(name, k):
        nonlocal n
        cols[name] = n
        n += k
    add('c', 8)
    add('cctx', 8)
    for l in range(4):
        add(('ada_b', l), 48)
    for l in range(4):
        for j in range(2):
            add(('norm_g', l, j), 8)
    add('final_g', 8)
    for e in range(2):
        for d in range(2):
            add(('lb', e, d), 4)
    for e in range(2):
        add(('hgrn_g', e), 1)
        add(('qn_g', e), 3)
        add(('kvn_g', e), 2)
    for o in range(2):
        add(('subln', o), 1)
    return cols, n


VCOL, NV = vec_layout()
CB_IDENT, CB_ONES, CB_MF, CB_MB, CB_BM, CB_PERM = 0, 128, 256, 384, 512, 516
CB_M128 = 516 + 3 * 128
CB_M64 = CB_M128 + 256
CB_BM2 = CB_M64 + 256
NCB = CB_BM2 + 2
HGRN_C = (32, 64)
RET_C = 128


def build_program(n_layers=4):
    P = Prog()
    d_xT = P.dram("xT", [D, T])
    d_vec = P.dram("vecs", [128, NV])
    d_bc = P.dram("bc", [128, 16 + 512])
    d_cb = P.dram("cb", [128, NCB])
    d_rope = P.dram("rope", [3, 2, 128, 2048])
    d_ada = P.dram("ada_w", [4, D, 6 * D])
    d_wout = P.dram("mix_w_out", [4, D, D])
    d_wg = P.dram("ffn_w_gate", [4, D, FFN_H])
    d_wu = P.dram("ffn_w_up", [4, D, FFN_H])
    d_wd = P.dram("ffn_w_down", [4, FFN_H, D])
    d_ewin = P.dram("even_w_in", [2, D, 3232])
    d_wuq = P.dram("mla_w_uq", [2, 384, 768])
    d_wukv = P.dram("mla_w_ukv", [2, 256, 1024])
    d_owin = P.dram("odd_w_in", [2, D, 3072])
    d_out = P.dram("outT", [D, 2048], kind="ExternalOutput")

    XT = P.sb("XT", [128, 8, T], F32)
    HT = P.sb("HT", [128, 8, T], BF16)
    VEC = P.sb("VEC", [128, NV], F32)
    BC = P.sb("BCs", [128, 16 + 512], F32)
    CB = P.sb("CBs", [128, NCB], BF16)
    MOD = P.sb("MOD", [128, 4, 48, 2], F32)
    AMOD = P.sb("AMOD", [128, 4, 2, 8, 2], F32)
    SM = P.sb("SM", [128, 64], F32)
    S2 = P.sb("S2", [128, 8, 2], F32)
    IDENT = CB[:, CB_IDENT:CB_IDENT + 128]
    ONES = CB[:, CB_ONES:CB_ONES + 128]
    MASKS = {32: [CB[:, CB_MF:CB_MF + 128], CB[:, CB_MB:CB_MB + 128]],
             64: [CB[:, CB_M64:CB_M64 + 128], CB[:, CB_M64 + 128:CB_M64 + 256]],
             128: [CB[:, CB_M128:CB_M128 + 128], CB[:, CB_M128 + 128:CB_M128 + 256]]}
    BMS = {32: CB[:, CB_BM:CB_BM + 4], 64: CB[:, CB_BM2:CB_BM2 + 2]}
    BM4 = CB[:, CB_BM:CB_BM + 4]
    PERM = [CB[:, CB_PERM + 128 * i:CB_PERM + 128 * (i + 1)] for i in range(3)]
    C_EPS, C_ZERO, C_ONE = 0, 1, 2
    C_LG = 4
    C_NLAM = 20
    C_GS = 22
    C_OML = 24
    C_TMP = 40
    EPSC = SM[:, C_EPS:C_EPS + 1]
    ZEROC = SM[:, C_ZERO:C_ZERO + 1]
    ONEC = SM[:, C_ONE:C_ONE + 1]

    AR = Arena(P, "ARENA", 90 * 1024)
    pbanks = [P.ps("bank%d" % i, [128, 512], F32) for i in range(8)]
    PS_T = Ring(pbanks[0:4])
    PS_A = Ring(pbanks[4:8])

    P.dma(VEC, d_vec)
    P.dma(BC, d_bc)
    P.dma(CB, d_cb, eng='pool')
    P.dma(XT, d_xT.rearrange("(c p) t -> p c t", p=128))
    P.memset(SM, 0.0)
    P.memset(SM[:, C_EPS:C_EPS + 1], EPS)
    P.memset(SM[:, C_ONE:C_ONE + 1], 1.0)

    for k_, nm_ in ((0, 'c'), (1, 'cctx')):
        src_ = VEC[:, VCOL[nm_]:VCOL[nm_] + 8]
        P.act(S2[:, :, k_], src_, AF.Exp, scale=-1.0)
        P.act(S2[:, :, k_], S2[:, :, k_], AF.Ln, bias=SM[:, C_ONE:C_ONE + 1])
        P.act(S2[:, :, k_], S2[:, :, k_], AF.Exp, scale=-1.0)
        P.tt(S2[:, :, k_], S2[:, :, k_], src_, ALU.mult)

    AR.reset()
    adaring = AR.ring(3, (8, 512), BF16)
    S2b = AR.alloc((8, 2), BF16)
    P.copy(S2b, S2)
    for l in range(n_layers):
        mps = PS_A.next()
        for g in range(12):
            wa = adaring.next()
            P.dma(wa, d_ada[l, :, g * 512:(g + 1) * 512].rearrange("(kc p) f -> p kc f", p=128), eng='pool')
            for jj in range(4):
                j = g * 4 + jj
                for kc in range(8):
                    P.mm(mps[:, 2 * j:2 * j + 2], wa[:, kc, jj * 128:(jj + 1) * 128], S2b[:, kc, :],
                         start=(kc == 0), stop=(kc == 7))
        cb0 = VCOL[('ada_b', l)]
        P.tt(MOD[:, l, :, :], mps[:, 0:96].rearrange("p (j k) -> p j k", k=2),
             VEC[:, cb0:cb0 + 48].unsqueeze(2).to_broadcast([128, 48, 2]), ALU.add)
        for which in range(2):
            g0 = VCOL[('norm_g', l, which)]
            sc = MOD[:, l, 8 + 24 * which:16 + 24 * which, :]
            P.stt(AMOD[:, l, which, :, :], sc, 1.0, VEC[:, g0:g0 + 8].unsqueeze(2).to_broadcast([128, 8, 2]),
                  ALU.add, ALU.mult)

    def SH(l, which):
        return MOD[:, l, 24 * which:24 * which + 8, :]

    def GATE(l, which):
        return MOD[:, l, 16 + 24 * which:24 + 24 * which, :]

    P.act(SM[:, C_LG:C_LG + 16], BC[:, 0:16], AF.Exp, scale=-1.0)
    P.act(SM[:, C_LG:C_LG + 16], SM[:, C_LG:C_LG + 16], AF.Ln, bias=ONEC)
    P.ts(SM[:, C_LG:C_LG + 16], SM[:, C_LG:C_LG + 16], -1.0, ALU.mult)
    for o in range(2):
        layer_idx = 2 * o + 1
        lam_init = 0.8 - 0.6 * math.exp(-0.3 * layer_idx)
        dl = BC[:, 16 + 256 * o:16 + 256 * (o + 1)]
        tmp = AR.alloc((128,), F32) if o == 0 else tmp
        P.tt(tmp[:, 0:64], dl[:, 0:64], dl[:, 64:128], ALU.mult)
        P.tt(tmp[:, 64:128], dl[:, 128:192], dl[:, 192:256], ALU.mult)
        P.op('dve', lambda e, tmp=tmp: e.tensor_reduce(out=SM[:, C_TMP:C_TMP + 2],
                                                      in_=tmp.rearrange("p (a b) -> p a b", b=64),
                                                      axis=AX.X, op=ALU.add),
             [tmp], [SM[:, C_TMP:C_TMP + 2]])
        P.act(SM[:, C_TMP:C_TMP + 2], SM[:, C_TMP:C_TMP + 2], AF.Exp)
        P.tt(SM[:, C_NLAM + o:C_NLAM + o + 1], SM[:, C_TMP + 1:C_TMP + 2], SM[:, C_TMP:C_TMP + 1], ALU.subtract)
        P.ts(SM[:, C_NLAM + o:C_NLAM + o + 1], SM[:, C_NLAM + o:C_NLAM + o + 1], -lam_init, ALU.add)
        sc0 = VCOL[('subln', o)]
        P.ts(SM[:, C_GS + o:C_GS + o + 1], VEC[:, sc0:sc0 + 1], 1.0 - lam_init, ALU.mult)
    P.memset(SM[:, C_OML:C_OML + 8], 1.0)
    for d in range(2):
        a0 = VCOL[('lb', 0, d)]
        a1 = VCOL[('lb', 1, d)]
        dst = SM[:, C_OML + 8 + 4 * d:C_OML + 12 + 4 * d]
        P.tt(dst, VEC[:, a0:a0 + 4], VEC[:, a1:a1 + 4], ALU.subtract)
        P.act(dst, dst, AF.Exp, scale=-1.0)
        P.act(dst, dst, AF.Ln, bias=SM[:, C_ONE:C_ONE + 1])
        P.act(dst, dst, AF.Exp, scale=-1.0)

    def kcol(t0):
        return 1 if t0 < CT else 0

    def norm_mod(l, which):
        AR.reset()
        sqr = AR.ring(3, (512,), BF16)
        rsr = AR.ring(2, (512,), F32)
        tmr = AR.ring(3, (512,), F32)
        for (t0, n) in TB:
            k = kcol(t0)
            R = PS_T.next()
            for c in range(8):
                sq = sqr.next()[:, :n]
                P.act(sq, XT[:, c, t0:t0 + n], AF.Square)
                P.mm(R[:, :n], ONES, sq, start=(c == 0), stop=(c == 7))
            rs = rsr.next()[:, :n]
            rstd_from_sumsq(rs, R[:, :n], D)
            for c in range(8):
                tmp = tmr.next()[:, :n]
                P.stt(tmp, XT[:, c, t0:t0 + n], AMOD[:, l, which, c, k:k + 1], rs, ALU.mult, ALU.mult)
                P.act(HT[:, c, t0:t0 + n], tmp, AF.Identity, bias=SH(l, which)[:, c, k:k + 1])

    def proj(ps, w, src, t0, n, KC=8):
        for kc in range(KC):
            P.mm(ps, w[:, kc, :], src[:, kc, t0:t0 + n], start=(kc == 0), stop=(kc == KC - 1))

    def load_w(dst, src):
        P.dma(dst, src, eng='pool')

    def outproj(l, WO, OT, t0, n):
        k = kcol(t0)
        for i in range(8):
            ps = PS_T.next()
            P.mm(ps[:, :n], WO[:, i * 128:(i + 1) * 128], OT)
            P.stt(XT[:, i, t0:t0 + n], ps[:, :n], GATE(l, 0)[:, i, k:k + 1], XT[:, i, t0:t0 + n], ALU.mult, ALU.add)

    def sigmoid_act(dst, src, sign):
        P.act(dst, src, AF.Exp, scale=-float(sign))
        P.act(dst, dst, AF.Ln, bias=ONEC[:dst.shape[0], :])
        P.act(dst, dst, AF.Exp, scale=-1.0)

    def rstd_from_sumsq(rs, R, nfeat):
        P.act(rs, R, AF.Ln, scale=1.0 / nfeat, bias=EPSC[:rs.shape[0], :])
        P.act(rs, rs, AF.Exp, scale=-0.5)

    def rope(dst, ps, np_, t0, n, variant, raw, cosr, sinr, t1r):
        if t0 < CT:
            P.copy(dst, ps[:np_, :n], eng='act')
            return
        r = raw.next()[:np_, :n]
        P.copy(r, ps[:np_, :n], eng='act')
        pp = PS_T.next()
        P.mm(pp[:np_, :n], PERM[variant][:np_, :np_], r)
        co = cosr.next()[:np_, :n]
        si = sinr.next()[:np_, :n]
        P.dma(co, d_rope[variant, 0, 0:np_, t0 - CT:t0 - CT + n])
        P.dma(si, d_rope[variant, 1, 0:np_, t0 - CT:t0 - CT + n])
        t1 = t1r.next()[:np_, :n]
        P.tt(t1, r, co, ALU.mult)
        P.tt(si, pp[:np_, :n], si, ALU.mult)
        P.tt(dst, t1, si, ALU.add)

    def gla(dk, QS, KF, GBUF, VH, d, OACC, first, C):
        mark = AR.off
        ncl = 128 // C
        NCH = T // C
        B = AR.alloc((T,), F32)[:dk]
        GE = AR.alloc((80,), F32)[:dk]
        DTOT = AR.alloc((72,), F32)[:dk]
        QT_ = AR.alloc((T,), BF16)[:dk]
        KP = AR.alloc((T,), BF16)[:dk]
        KT = AR.alloc((NT, 128), BF16)
        ebr = AR.ring(3, (512,), F32)
        TOTL = AR.alloc((72,), F32)[:dk]
        k2r = AR.ring(2, (512,), BF16)
        vbr = AR.ring(2, (ncl, 128), BF16)
        amr = AR.ring(2, (128,), BF16)
        sbr = AR.ring(2, (ncl, 128), BF16)
        Sst = [AR.alloc((128,), F32)[:dk] for _ in range(4)]
        P.scan(GBUF[:, 32:32 + T], ONEC[:dk, :].to_broadcast([dk, T]), GBUF[:, 32:32 + T], 0.0, ALU.mult, ALU.add)
        P.memset(GE[:, 0:1], 0.0)
        P.copy(GE[:, 1:NCH + 1], GBUF[:, 32:32 + T].rearrange("p (c j) -> p c j", j=C)[:, :, C - 1])
        DTOT = DTOT[:, 0:NCH]
        TOTL = TOTL[:, 0:NCH]
        P.tt(TOTL, GE[:, 1:NCH + 1], GE[:, 0:NCH], ALU.subtract)
        P.act(DTOT, TOTL, AF.Exp)
        Bv = B.rearrange("p (c j) -> p c j", j=C)
        if d == 0:
            P.tt(Bv, GBUF[:, 32:32 + T].rearrange("p (c j) -> p c j", j=C),
                 GE[:, 0:NCH].unsqueeze(2).to_broadcast([dk, NCH, C]), ALU.subtract)
        else:
            P.tt(Bv, GE[:, 1:NCH + 1].unsqueeze(2).to_broadcast([dk, NCH, C]),
                 GBUF[:, 31:31 + T].rearrange("p (c j) -> p c j", j=C), ALU.subtract)
        P.ts(B, B, -80.0, ALU.max)
        for (t0, n) in TB:
            eb = ebr.next()[:dk, :n]
            P.act(eb, B[:, t0:t0 + n], AF.Exp)
            P.tt(QT_[:, t0:t0 + n], QS[:, t0:t0 + n], eb, ALU.mult)
            enb = ebr.next()[:dk, :n]
            P.act(enb, B[:, t0:t0 + n], AF.Exp, scale=-1.0)
            P.tt(KP[:, t0:t0 + n], KF[:, t0:t0 + n], enb, ALU.mult)
            e2 = ebr.next()[:dk, :n]
            P.tt(e2.rearrange("p (c j) -> p c j", j=C),
                 TOTL[:, t0 // C:(t0 + n) // C].unsqueeze(2).to_broadcast([dk, n // C, C]),
                 B[:, t0:t0 + n].rearrange("p (c j) -> p c j", j=C), ALU.subtract)
            P.act(e2, e2, AF.Exp)
            k2 = k2r.next()[:dk, :n]
            P.tt(k2, KF[:, t0:t0 + n], e2, ALU.mult)
            for i in range(n // 128):
                tt_ = t0 // 128 + i
                pt = PS_T.next()[:, 0:64].bitcast(BF16)
                P.transpose(pt[:, :dk], k2[:, i * 128:(i + 1) * 128], IDENT[:dk, :dk])
                P.copy(KT[:, tt_, :dk], pt[:, :dk], eng='act')
        order = list(range(NT)) if d == 0 else [1, 0] + list(range(NT - 1, 1, -1))
        corder = list(range(ncl)) if d == 0 else list(range(ncl - 1, -1, -1))
        P.memset(Sst[0], 0.0)
        scur = 0

        def emit_front(tt_):
            U = PS_A.next()
            if ncl == 1:
                P.mm(U[:dk, 0:128], KT[:, tt_, :dk], VH[:, tt_, :])
            else:
                vb = vbr.next()
                P.tt(vb, VH[:, tt_, :].unsqueeze(1).to_broadcast([128, ncl, 128]),
                     BMS[C].unsqueeze(2).to_broadcast([128, ncl, 128]), ALU.mult, eng='pool')
                P.mm(U[:dk, 0:ncl * 128], KT[:, tt_, :dk], vb.rearrange("p a b -> p (a b)"))
            AT_ = PS_T.next()
            P.mm(AT_[:, 0:128], KP[:, tt_ * 128:(tt_ + 1) * 128], QT_[:, tt_ * 128:(tt_ + 1) * 128])
            return U, AT_

        nxt = emit_front(order[0])
        for oi, tt_ in enumerate(order):
            U, AT_ = nxt
            if oi + 1 < NT:
                nxt = emit_front(order[oi + 1])
            am = amr.next()
            P.tt(am, AT_[:, 0:128], MASKS[C][d], ALU.mult)
            sb16 = sbr.next()
            for cl in corder:
                c = tt_ * ncl + cl
                P.copy(sb16[:dk, cl, :], Sst[scur], eng='act')
                P.stt(Sst[(scur + 1) % 4], Sst[scur], DTOT[:, c:c + 1], U[:dk, cl * 128:(cl + 1) * 128], ALU.mult, ALU.add)
                scur = (scur + 1) % 4
            O = PS_A.next()
            P.mm(O[:, 0:128], VH[:, tt_, :], am, start=True, stop=False)
            for cl in range(ncl):
                P.mm(O[:, cl * C:(cl + 1) * C], sb16[:dk, cl, :],
                     QT_[:, tt_ * 128 + cl * C:tt_ * 128 + (cl + 1) * C], start=False, stop=(cl == ncl - 1))
            oa = OACC[:, tt_ * 128:(tt_ + 1) * 128]
            if first:
                P.copy(oa, O[:, 0:128], eng='act')
            else:
                P.tt(oa, O[:, 0:128], oa, ALU.add)
        AR.reset(mark)

    def readout_gated(l, WO, OACC, gw, gain_col, t0, n, rings):
        sqr, rsr, tmr, otr = rings
        sq = sqr.next()[:, :n]
        P.act(sq, OACC[:, t0:t0 + n], AF.Square)
        R = PS_T.next()
        P.mm(R[:, :n], ONES, sq)
        rs = rsr.next()[:, :n]
        rstd_from_sumsq(rs, R[:, :n], 128)
        pg = PS_T.next()
        proj(pg[:, :n], gw, HT, t0, n)
        sg = tmr.next()[:, :n]
        sigmoid_act(sg, pg[:, :n], 1)
        P.tt(sg, pg[:, :n], sg, ALU.mult)
        t1 = tmr.next()[:, :n]
        if gain_col is None:
            P.tt(t1, OACC[:, t0:t0 + n], rs, ALU.mult)
        else:
            P.stt(t1, OACC[:, t0:t0 + n], gain_col, rs, ALU.mult, ALU.mult)
        ot = otr.next()[:, :n]
        P.tt(ot, t1, sg, ALU.mult)
        outproj(l, WO, ot, t0, n)

    def attn_block(pairs, t0, n, scale, ptr, with_den=True):
        kts = [0, 1] if t0 < CT else list(range(NT))
        res = []
        for (Kt, Qt, Vl) in pairs:
            O = PS_A.next()
            DEN = PS_A.next() if with_den else None

            def issue_S(kt):
                S = PS_T.next()
                P.mm(S[:, :n], Kt[:, kt * 128:(kt + 1) * 128], Qt[:, t0:t0 + n])
                return S
            pend = [issue_S(kt) for kt in kts[:2]]
            for ki, kt in enumerate(kts):
                S = pend.pop(0)
                if ki + 2 < len(kts):
                    pend.append(issue_S(kts[ki + 2]))
                pt = ptr.next()[:, :n]
                P.act(pt, S[:, :n], AF.Exp, scale=scale)
                P.mm(O[:, :n], Vl(kt), pt, start=(ki == 0), stop=(ki == len(kts) - 1))
                if with_den:
                    P.mm(DEN[:, :n], ONES, pt, start=(ki == 0), stop=(ki == len(kts) - 1))
            res.append((O, DEN))
        return res

    def even_mixer(l):
        e = l // 2
        W = d_ewin[e]
        AR.reset()
        WOr = AR.ring(2, (1024,), BF16)
        hmark = AR.off
        for h in range(4):
            AR.reset(hmark)
            WH = AR.alloc((8, 5, 128), BF16)
            for g in range(5):
                load_w(WH[:, :, g, :], W[:, g * 512 + h * 128:g * 512 + (h + 1) * 128].rearrange("(kc p) c -> p kc c", p=128))
            WO = WOr.next()
            load_w(WO, d_wout[l, h * 128:(h + 1) * 128, :])
            VH = AR.alloc((NT, 128), BF16)
            QS = AR.alloc((T,), F32)
            KF = AR.alloc((T,), BF16)
            GBUF = AR.alloc((32 + T,), F32)
            OACC = AR.alloc((T,), F32)
            gmark = AR.off
            tmr = AR.ring(2, (512,), F32)
            for tt_ in range(NT):
                ps = PS_T.next()
                for kc in range(8):
                    P.mm(ps[:, 0:128], HT[:, kc, tt_ * 128:(tt_ + 1) * 128], WH[:, kc, 1, :], start=(kc == 0), stop=(kc == 7))
                P.copy(VH[:, tt_, :], ps[:, 0:128], eng='act')
            for (t0, n) in TB:
                ps = PS_T.next()
                proj(ps[:, :n], WH[:, :, 0, :], HT, t0, n)
                P.act(QS[:, t0:t0 + n], ps[:, :n], AF.Copy, scale=128 ** -0.5)
            P.memset(GBUF[:, 0:32], 0.0)
            for d in range(2):
                for (t0, n) in TB:
                    ps = PS_T.next()
                    proj(ps[:, :n], WH[:, :, 2 + d, :], HT, t0, n)
                    sk = tmr.next()[:, :n]
                    sigmoid_act(sk, ps[:, :n], -1)
                    P.ts(GBUF[:, 32 + t0:32 + t0 + n], sk, SM[:, C_OML + 8 * e + 4 * d + h:C_OML + 8 * e + 4 * d + h + 1], ALU.mult)
                    P.copy(KF[:, t0:t0 + n], GBUF[:, 32 + t0:32 + t0 + n], eng='pool')
                P.act(GBUF[:, 32:32 + T], GBUF[:, 32:32 + T], AF.Ln, scale=-1.0, bias=ONEC)
                AR.reset(gmark)
                gla(128, QS, KF, GBUF, VH, d, OACC, first=(d == 0), C=HGRN_C[e])
                AR.reset(gmark)
                tmr = AR.ring(2, (512,), F32)
            AR.reset(gmark)
            rings = (AR.ring(2, (512,), BF16), AR.ring(2, (512,), F32), AR.ring(3, (512,), F32), AR.ring(2, (512,), BF16))
            gcol = VEC[:, VCOL[('hgrn_g', e)]:VCOL[('hgrn_g', e)] + 1]
            for (t0, n) in TB:
                readout_gated(l, WO, OACC, WH[:, :, 4, :], gcol, t0, n, rings)
        AR.reset(hmark)
        WUQ = AR.alloc((3, 768), BF16)
        load_w(WUQ, d_wuq[e].rearrange("(kc p) f -> p kc f", p=128))
        WUKV = AR.alloc((2, 1024), BF16)
        load_w(WUKV, d_wukv[e].rearrange("(kc p) f -> p kc f", p=128))
        CQN = AR.alloc((3, T), BF16)
        CKV = AR.alloc((2, T), BF16)
        KRT = AR.alloc((T,), BF16)
        raw = AR.ring(2, (512,), BF16)
        cosr = AR.ring(2, (512,), F32)
        sinr = AR.ring(2, (512,), F32)
        t1r = AR.ring(2, (512,), F32)
        lmark = AR.off
        WL = AR.alloc((8, 640), BF16)
        load_w(WL, W[:, 2560:3200].rearrange("(kc p) f -> p kc f", p=128))
        WKR = AR.alloc((8, 96), BF16)
        P.memset(WKR, 0.0)
        load_w(WKR[:, :, 64:96], W[:, 3200:3232].rearrange("(kc p) f -> p kc f", p=128))
        c32 = AR.alloc((3, 512), F32)
        sqr = AR.ring(2, (512,), BF16)
        rsr = AR.ring(2, (512,), F32)
        for (t0, n) in TB:
            for (c0, nch, dst, gname) in ((0, 3, CQN, 'qn_g'), (3, 2, CKV, 'kvn_g')):
                R = PS_A.next()
                for c in range(nch):
                    ps = PS_T.next()
                    proj(ps[:, :n], WL[:, :, (c0 + c) * 128:(c0 + c + 1) * 128], HT, t0, n)
                    P.copy(c32[:, c, :n], ps[:, :n], eng='act')
                    sq = sqr.next()[:, :n]
                    P.act(sq, ps[:, :n], AF.Square)
                    P.mm(R[:, :n], ONES, sq, start=(c == 0), stop=(c == nch - 1))
                rs = rsr.next()[:, :n]
                rstd_from_sumsq(rs, R[:, :n], 128 * nch)
                g0 = VCOL[(gname, e)]
                for c in range(nch):
                    P.stt(dst[:, c, t0:t0 + n], c32[:, c, :n], VEC[:, g0 + c:g0 + c + 1], rs, ALU.mult, ALU.mult)
            ps = PS_T.next()
            proj(ps[:96, :n], WKR, HT, t0, n)
            rope(KRT[:96, t0:t0 + n], ps, 96, t0, n, ROPE_MLA, raw, cosr, sinr, t1r)
        AR.reset(lmark)
        pmark = AR.off
        sc = 96 ** -0.5
        for hp in range(4):
            AR.reset(pmark)
            WO = WOr.next()
            load_w(WO, d_wout[l, (4 + hp) * 128:(5 + hp) * 128, :])
            QTs, KTs, VAs = [], [], []
            for hh in range(2):
                h = 2 * hp + hh
                QTh = AR.alloc((T,), BF16)
                KTh = AR.alloc((T,), BF16)
                VA = AR.alloc((NT, 128), BF16)
                P.memset(VA, 1.0, eng='pool')
                for (t0, n) in TB:
                    ps = PS_T.next()
                    proj(ps[:96, :n], WUQ[:, :, 96 * h:96 * h + 96], CQN, t0, n, KC=3)
                    rope(QTh[:96, t0:t0 + n], ps, 96, t0, n, ROPE_MLA, raw, cosr, sinr, t1r)
                    ps = PS_T.next()
                    proj(ps[:64, :n], WUKV[:, :, 128 * h:128 * h + 64], CKV, t0, n, KC=2)
                    P.copy(KTh[:64, t0:t0 + n], ps[:64, :n], eng='act')
                P.copy(KTh[64:96, :], KRT[64:96, :], eng='pool')
                for tt_ in range(NT):
                    ps = PS_T.next()
                    for kc in range(2):
                        P.mm(ps[:, 0:64], CKV[:, kc, tt_ * 128:(tt_ + 1) * 128], WUKV[:, kc, 128 * h + 64:128 * h + 128],
                             start=(kc == 0), stop=(kc == 1))
                    P.copy(VA[:, tt_, 64 * hh:64 * hh + 64], ps[:, 0:64], eng='act')
                QTs.append(QTh)
                KTs.append(KTh)
                VAs.append(VA)
            ptr = AR.ring(4, (512,), BF16)
            rdr = AR.ring(1, (512,), F32)
            otr = AR.ring(2, (512,), BF16)
            for (t0, n) in TB:
                pairs = [(KTs[hh][:96, :], QTs[hh][:96, :], (lambda kt, hh=hh: VAs[hh][:, kt, :])) for hh in range(2)]
                res = attn_block(pairs, t0, n, sc, ptr, with_den=False)
                ot = otr.next()[:, :n]
                for hh in range(2):
                    O, _ = res[hh]
                    rd = rdr.next()[:, :n]
                    sl = slice(64 * hh, 64 * hh + 64)
                    so = slice(64 * (1 - hh), 64 * (1 - hh) + 64)
                    P.act(rd[sl, :], O[so, :n], AF.Ln)
                    P.act(rd[sl, :], rd[sl, :], AF.Exp, scale=-1.0)
                    P.tt(ot[sl, :], O[sl, :n], rd[sl, :], ALU.mult)
                outproj(l, WO, ot, t0, n)

    def odd_mixer(l):
        o = l // 2
        W = d_owin[o]
        AR.reset()
        WOr = AR.ring(2, (1024,), BF16)
        hmark = AR.off
        for h in range(4):
            AR.reset(hmark)
            WR = AR.alloc((8, 384), BF16)
            Wv = W.rearrange("(kc p) f -> p kc f", p=128)
            load_w(WR[:, :, 0:64], Wv[:, :, 64 * h:64 * h + 64])
            load_w(WR[:, :, 64:128], Wv[:, :, 256 + 64 * h:256 + 64 * h + 64])
            load_w(WR[:, :, 128:256], Wv[:, :, 512 + 128 * h:512 + 128 * h + 128])
            load_w(WR[:, :, 256:384], Wv[:, :, 1024 + 128 * h:1024 + 128 * h + 128])
            WO = WOr.next()
            load_w(WO, d_wout[l, h * 128:(h + 1) * 128, :])
            VH = AR.alloc((NT, 128), BF16)
            QS = AR.alloc((T,), F32)
            KF = AR.alloc((T,), BF16)
            GBUF = AR.alloc((32 + T,), F32)
            OACC = AR.alloc((T,), F32)
            gmark = AR.off
            raw = AR.ring(2, (512,), BF16)
            cosr = AR.ring(2, (512,), F32)
            sinr = AR.ring(2, (512,), F32)
            t1r = AR.ring(2, (512,), F32)
            kfr = AR.ring(2, (512,), F32)
            for tt_ in range(NT):
                ps = PS_T.next()
                for kc in range(8):
                    P.mm(ps[:, 0:128], HT[:, kc, tt_ * 128:(tt_ + 1) * 128], WR[:, kc, 128:256], start=(kc == 0), stop=(kc == 7))
                P.copy(VH[:, tt_, :], ps[:, 0:128], eng='act')
            for (t0, n) in TB:
                ps = PS_T.next()
                proj(ps[:64, :n], WR[:, :, 0:64], HT, t0, n)
                rope(QS[:64, t0:t0 + n], ps, 64, t0, n, ROPE_RET, raw, cosr, sinr, t1r)
                ps = PS_T.next()
                proj(ps[:64, :n], WR[:, :, 64:128], HT, t0, n)
                kf = kfr.next()[:64, :n]
                rope(kf, ps, 64, t0, n, ROPE_RET, raw, cosr, sinr, t1r)
                P.ts(KF[:64, t0:t0 + n], kf, 64 ** -0.5, ALU.mult)
            P.memset(GBUF[:, 0:32], 0.0)
            for d in range(2):
                col = C_LG + 8 * o + 4 * d + h
                P.act(GBUF[:64, 32:32 + T], ZEROC[:64, :].to_broadcast([64, T]), AF.Identity, scale=0.0,
                      bias=SM[:64, col:col + 1])
                AR.reset(gmark)
                gla(64, QS[:64], KF[:64], GBUF[:64], VH, d, OACC, first=(d == 0), C=RET_C)
            AR.reset(gmark)
            rings = (AR.ring(2, (512,), BF16), AR.ring(2, (512,), F32), AR.ring(3, (512,), F32), AR.ring(2, (512,), BF16))
            for (t0, n) in TB:
                readout_gated(l, WO, OACC, WR[:, :, 256:384], None, t0, n, rings)
        sc = 64 ** -0.5
        for h in range(4):
            AR.reset(hmark)
            WD_ = AR.alloc((8, 384), BF16)
            Wv = W.rearrange("(kc p) f -> p kc f", p=128)
            load_w(WD_[:, :, 0:128], Wv[:, :, 1536 + 128 * h:1536 + 128 * h + 128])
            load_w(WD_[:, :, 128:256], Wv[:, :, 2048 + 128 * h:2048 + 128 * h + 128])
            load_w(WD_[:, :, 256:384], Wv[:, :, 2560 + 128 * h:2560 + 128 * h + 128])
            WO = WOr.next()
            load_w(WO, d_wout[l, (4 + h) * 128:(5 + h) * 128, :])
            VH = AR.alloc((NT, 128), BF16)
            QD = AR.alloc((T,), BF16)
            KD = AR.alloc((T,), BF16)
            raw = AR.ring(2, (512,), BF16)
            cosr = AR.ring(2, (512,), F32)
            sinr = AR.ring(2, (512,), F32)
            t1r = AR.ring(2, (512,), F32)
            for tt_ in range(NT):
                ps = PS_T.next()
                for kc in range(8):
                    P.mm(ps[:, 0:128], HT[:, kc, tt_ * 128:(tt_ + 1) * 128], WD_[:, kc, 256:384], start=(kc == 0), stop=(kc == 7))
                P.copy(VH[:, tt_, :], ps[:, 0:128], eng='act')
            for (t0, n) in TB:
                ps = PS_T.next()
                proj(ps[:, :n], WD_[:, :, 0:128], HT, t0, n)
                rope(QD[:, t0:t0 + n], ps, 128, t0, n, ROPE_DIFF, raw, cosr, sinr, t1r)
                ps = PS_T.next()
                proj(ps[:, :n], WD_[:, :, 128:256], HT, t0, n)
                rope(KD[:, t0:t0 + n], ps, 128, t0, n, ROPE_DIFF, raw, cosr, sinr, t1r)
            ptr = AR.ring(4, (512,), BF16)
            f32r = AR.ring(4, (512,), F32)
            sqr = AR.ring(2, (512,), BF16)
            otr = AR.ring(2, (512,), BF16)
            for (t0, n) in TB:
                pairs = [(KD[64 * m:64 * m + 64, :], QD[64 * m:64 * m + 64, :], (lambda kt: VH[:, kt, :])) for m in range(2)]
                res = attn_block(pairs, t0, n, sc, ptr)
                a = []
                for m in range(2):
                    O, DEN = res[m]
                    rd = f32r.next()[:, :n]
                    P.act(rd, DEN[:, :n], AF.Ln)
                    P.act(rd, rd, AF.Exp, scale=-1.0)
                    P.tt(rd, O[:, :n], rd, ALU.mult)
                    a.append(rd)
                ov = f32r.next()[:, :n]
                P.stt(ov, a[1], SM[:, C_NLAM + o:C_NLAM + o + 1], a[0], ALU.mult, ALU.add)
                sq = sqr.next()[:, :n]
                P.act(sq, ov, AF.Square)
                R = PS_T.next()
                P.mm(R[:, :n], ONES, sq)
                rs = f32r.next()[:, :n]
                rstd_from_sumsq(rs, R[:, :n], 128)
                ot = otr.next()[:, :n]
                P.stt(ot, ov, SM[:, C_GS + o:C_GS + o + 1], rs, ALU.mult, ALU.mult)
                outproj(l, WO, ot, t0, n)

    def ffn(l):
        SBS = [TB[0:3], TB[3:5]]
        for sbk in SBS:
            AR.reset()
            base = sbk[0][0]
            ntok = sum(n for _, n in sbk)
            ACTT = AR.alloc((22, ntok), BF16)
            wgr = AR.ring(2, (8, 256), BF16)
            wur = AR.ring(2, (8, 256), BF16)
            sgr = AR.ring(3, (512,), F32)
            for jg in range(11):
                wg = wgr.next()
                wu = wur.next()
                load_w(wg, d_wg[l, :, jg * 256:(jg + 1) * 256].rearrange("(kc p) f -> p kc f", p=128))
                load_w(wu, d_wu[l, :, jg * 256:(jg + 1) * 256].rearrange("(kc p) f -> p kc f", p=128))
                for jj in range(2):
                    j = jg * 2 + jj
                    for (t0, n) in sbk:
                        pg = PS_T.next()
                        proj(pg[:, :n], wg[:, :, jj * 128:(jj + 1) * 128], HT, t0, n)
                        pu = PS_A.next()
                        proj(pu[:, :n], wu[:, :, jj * 128:(jj + 1) * 128], HT, t0, n)
                        sg = sgr.next()[:, :n]
                        P.act(sg, pg[:, :n], AF.Silu)
                        P.tt(ACTT[:, j, t0 - base:t0 - base + n], sg, pu[:, :n], ALU.mult)
            wdr = AR.ring(2, (22, 128), BF16)
            for i in range(8):
                wd = wdr.next()
                load_w(wd, d_wd[l, :, i * 128:(i + 1) * 128].rearrange("(j p) c -> p j c", p=128))
                for (t0, n) in sbk:
                    k = kcol(t0)
                    ps = PS_T.next()
                    for j in range(22):
                        P.mm(ps[:, :n], wd[:, j, :], ACTT[:, j, t0 - base:t0 - base + n], start=(j == 0), stop=(j == 21))
                    P.stt(XT[:, i, t0:t0 + n], ps[:, :n], GATE(l, 1)[:, i, k:k + 1], XT[:, i, t0:t0 + n], ALU.mult, ALU.add)

    for l in range(n_layers):
        norm_mod(l, 0)
        if l % 2 == 0:
            even_mixer(l)
        else:
            odd_mixer(l)
        norm_mod(l, 1)
        ffn(l)

    AR.reset()
    sqr = AR.ring(3, (512,), BF16)
    rsr = AR.ring(2, (512,), F32)
    outr = AR.ring(3, (512,), F32)
    fg = VCOL['final_g']
    for (t0, n) in TB[1:]:
        R = PS_T.next()
        for c in range(8):
            sq = sqr.next()[:, :n]
            P.act(sq, XT[:, c, t0:t0 + n], AF.Square)
            P.mm(R[:, :n], ONES, sq, start=(c == 0), stop=(c == 7))
        rs = rsr.next()[:, :n]
        rstd_from_sumsq(rs, R[:, :n], D)
        for c in range(8):
            ob = outr.next()[:, :n]
            P.stt(ob, XT[:, c, t0:t0 + n], VEC[:, fg + c:fg + c + 1], rs, ALU.mult, ALU.mult)
            P.dma(d_out[c * 128:(c + 1) * 128, t0 - CT:t0 - CT + n], ob, is_output=True)
    return P.finalize()


def _rope_tables():
    tab = np.zeros((3, 2, 128, 2048), np.float32)
    tab[:, 0] = 1.0
    perm = np.zeros((3, 128, 128), np.float32)
    n = np.arange(2048, dtype=np.float32)
    row = np.floor(n / 64.0)
    col = n - 64.0 * row

    def fill(v, p0, width, pos):
        half = width // 2
        inv = (10000.0 ** (-np.arange(half, dtype=np.float32) / half)).astype(np.float32)
        ang = pos[None, :].astype(np.float32) * inv[:, None]
        c, s = np.cos(ang).astype(np.float32), np.sin(ang).astype(np.float32)
        tab[v, 0, p0:p0 + half] = c
        tab[v, 0, p0 + half:p0 + width] = c
        tab[v, 1, p0:p0 + half] = -s
        tab[v, 1, p0 + half:p0 + width] = s
        for j in range(half):
            perm[v, p0 + j + half, p0 + j] = 1.0
            perm[v, p0 + j, p0 + j + half] = 1.0
    for g in range(2):
        fill(ROPE_RET, 64 * g, 64, n)
        fill(ROPE_DIFF, 64 * g, 32, row)
        fill(ROPE_DIFF, 64 * g + 32, 32, col)
    fill(ROPE_MLA, 64, 16, row)
    fill(ROPE_MLA, 80, 16, col)
    return tab, perm


def _consts():
    tab, perm = _rope_tables()
    cb = np.zeros((128, NCB), np.float32)
    cb[:, CB_IDENT:CB_IDENT + 128] = np.eye(128, dtype=np.float32)
    cb[:, CB_ONES:CB_ONES + 128] = 1.0
    s = np.arange(128)[:, None]
    t = np.arange(128)[None, :]
    same = (s // 32) == (t // 32)
    cb[:, CB_MF:CB_MF + 128] = (same & (s <= t)).astype(np.float32)
    cb[:, CB_MB:CB_MB + 128] = (same & (s >= t)).astype(np.float32)
    cb[:, CB_BM:CB_BM + 4] = ((np.arange(128)[:, None] // 32) == np.arange(4)[None, :]).astype(np.float32)
    same = (s // 64) == (t // 64)
    cb[:, CB_M64:CB_M64 + 128] = (same & (s <= t)).astype(np.float32)
    cb[:, CB_M64 + 128:CB_M64 + 256] = (same & (s >= t)).astype(np.float32)
    cb[:, CB_BM2:CB_BM2 + 2] = ((np.arange(128)[:, None] // 64) == np.arange(2)[None, :]).astype(np.float32)
    cb[:, CB_M128:CB_M128 + 128] = (s <= t).astype(np.float32)
    cb[:, CB_M128 + 128:CB_M128 + 256] = (s >= t).astype(np.float32)
    for v in range(3):
        cb[:, CB_PERM + 128 * v:CB_PERM + 128 * (v + 1)] = perm[v]
    return cb, tab


_PROG_CACHE = {}


def _make_inputs(inputs, b):
    f = lambda a: np.ascontiguousarray(np.asarray(a, dtype=np.float32))
    rows = np.zeros((NV, 128), np.float32)

    def put(name, arr):
        a = f(arr).reshape(-1, 128)
        rows[VCOL[name]:VCOL[name] + a.shape[0]] = a
    put('c', inputs['c'][b])
    put('cctx', inputs['c_ctx'])
    for l in range(4):
        put(('ada_b', l), inputs['ada_b'][l])
        for j in range(2):
            put(('norm_g', l, j), inputs['norm_g'][l, j])
    put('final_g', inputs['final_norm_g'])
    for e in range(2):
        for d in range(2):
            put(('lb', e, d), inputs['hgrn_lb_logits'][e, d])
        put(('hgrn_g', e), inputs['hgrn_norm_g'][e])
        put(('qn_g', e), inputs['mla_q_norm_g'][e])
        put(('kvn_g', e), inputs['mla_kv_norm_g'][e])
    for o in range(2):
        put(('subln', o), inputs['diff_subln_g'][o])
    vecs = np.ascontiguousarray(rows.T)
    bc = np.zeros((128, 16 + 512), np.float32)
    bc[:, 0:16] = f(inputs['ret_decay_logits']).reshape(1, 16)
    bc[:, 16:] = f(inputs['diff_lambda']).reshape(1, 512)
    xT = np.ascontiguousarray(np.concatenate([f(inputs['ctx'][b]), f(inputs['x'][b])], axis=0).T)
    return xT, vecs, bc


def kernel(n_layers=4, core_ids=None, **inputs):
    inputs = {k: np.asarray(v) for k, v in inputs.items()}
    if n_layers not in _PROG_CACHE:
        _PROG_CACHE[n_layers] = build_program(n_layers)
    nc = _PROG_CACHE[n_layers]
    cb, tab = _consts()
    f = lambda a: np.ascontiguousarray(np.asarray(a, dtype=np.float32))
    shared = {
        "cb": cb, "rope": tab,
        "ada_w": f(inputs['ada_w']), "mix_w_out": f(inputs['mix_w_out']),
        "ffn_w_gate": f(inputs['ffn_w_gate']), "ffn_w_up": f(inputs['ffn_w_up']), "ffn_w_down": f(inputs['ffn_w_down']),
        "even_w_in": f(inputs['even_w_in']), "mla_w_uq": f(inputs['mla_w_uq']), "mla_w_ukv": f(inputs['mla_w_ukv']),
        "odd_w_in": f(inputs['odd_w_in']),
    }
    cores = list(range(8)) if core_ids is None else core_ids
    in_maps = []
    for b in cores:
        xT, vecs, bc = _make_inputs(inputs, b)
        m = dict(shared)
        m.update({"xT": xT, "vecs": vecs, "bc": bc})
        in_maps.append(m)
    res = run_bass_kernel_spmd(nc, in_maps, core_ids=list(range(len(cores))))
    outs = [np.ascontiguousarray(np.asarray(r["outT"]).T) for r in res.results]
    return np.stack(outs, axis=0).astype(np.float32)
```

```python
import numpy as np
from contextlib import ExitStack
import concourse.bass as bass
import concourse.mybir as mybir
from concourse.bass_utils import run_bass_kernel_spmd

F32 = mybir.dt.float32
BF16 = mybir.dt.bfloat16
AF = mybir.ActivationFunctionType
ALU = mybir.AluOpType
AX = mybir.AxisListType
_ESZ = {F32: 4, BF16: 2}

PHASE = 8000
NSWDGE = 4
NDSEM = 24


def _is_dram(ap):
    return 'DRam' in type(ap.tensor).__name__


def _bbox(ap):
    steps = ap.ap
    es = _ESZ.get(ap.dtype, 4)
    pstep, pcnt = steps[0]
    off = ap.offset
    if pstep:
        p0 = off // pstep
        f0 = off % pstep
    else:
        p0, f0 = 0, off
    ext = 1
    for s, c in steps[1:]:
        ext += (c - 1) * abs(s)
    return (p0, p0 + pcnt, f0 * es, (f0 + ext) * es)


class Prog:
    ENGS = ('pe', 'act', 'dve', 'pool', 'sp')

    def __init__(self):
        self.nc = bass.Bass("TRN2", target_bir_lowering=False)
        self.q = {e: [] for e in self.ENGS}
        self.cnt = {e: 0 for e in self.ENGS}
        self.recs = {}
        self.waited = {e: {} for e in self.ENGS}
        self.ndma = 0
        self.dma_tokens = []
        self.out_tokens = []
        self.same_engine_sync = True

    def sb(self, name, shape, dtype=F32):
        return self.nc.alloc_sbuf_tensor(name, list(shape), dtype).ap()

    def ps(self, name, shape, dtype=F32):
        return self.nc.alloc_psum_tensor(name, list(shape), dtype).ap()

    def dram(self, name, shape, dtype=F32, kind="ExternalInput"):
        return self.nc.dram_tensor(name, list(shape), dtype, kind=kind).ap()

    def _deps(self, eng, token, reads, writes, self_sync):
        deps = {}

        def add(tok):
            k, v = tok
            if deps.get(k, 0) < v:
                deps[k] = v

        for ap in reads:
            if ap is None or _is_dram(ap):
                continue
            bb = _bbox(ap)
            lst = self.recs.setdefault(ap.tensor.name, [])
            for r in lst:
                if r[5] and r[0] < bb[1] and bb[0] < r[1] and r[2] < bb[3] and bb[2] < r[3]:
                    add(r[4])
        for ap in writes:
            if ap is None or _is_dram(ap):
                continue
            bb = _bbox(ap)
            lst = self.recs.setdefault(ap.tensor.name, [])
            for r in lst:
                if r[0] < bb[1] and bb[0] < r[1] and r[2] < bb[3] and bb[2] < r[3]:
                    add(r[4])
        for ap in reads:
            if ap is None or _is_dram(ap):
                continue
            bb = _bbox(ap)
            lst = self.recs[ap.tensor.name]
            if token[0][0] == 'E':
                lst[:] = [r for r in lst if not ((not r[5]) and r[4][0] == token[0]
                                                 and bb[0] <= r[0] and r[1] <= bb[1]
                                                 and bb[2] <= r[2] and r[3] <= bb[3])]
            lst.append((bb[0], bb[1], bb[2], bb[3], token, False))
        for ap in writes:
            if ap is None or _is_dram(ap):
                continue
            bb = _bbox(ap)
            lst = self.recs[ap.tensor.name]
            lst[:] = [r for r in lst if not (bb[0] <= r[0] and r[1] <= bb[1]
                                             and bb[2] <= r[2] and r[3] <= bb[3])]
            lst.append((bb[0], bb[1], bb[2], bb[3], token, True))
        out = []
        w = self.waited[eng]
        for k, v in deps.items():
            if k[0] == 'E' and k[1] == eng and not self_sync:
                continue
            if w.get(k, 0) >= v:
                continue
            w[k] = v
            out.append((k, v))
        return out

    def op(self, eng, fn, reads, writes, self_sync=None):
        if self_sync is None:
            self_sync = self.same_engine_sync
        idx = self.cnt[eng]
        self.cnt[eng] += 1
        token = (('E', eng, idx // PHASE), idx % PHASE + 1)
        waits = self._deps(eng, token, reads, writes, self_sync)
        self.q[eng].append((waits, fn, token, 1))
        return token

    def dma(self, out, in_, eng='sp', is_output=False, **kw):
        i = self.ndma
        self.ndma += 1
        token = (('D', i % NDSEM), 16 * (i // NDSEM + 1))
        waits = self._deps(eng, token, [in_], [out], True)
        if i >= NDSEM:
            pk, pv = self.dma_tokens[i - NDSEM]
            if self.waited[eng].get(pk, 0) < pv:
                self.waited[eng][pk] = pv
                waits.append((pk, pv))
        if eng == 'pool':
            lst = self.__dict__.setdefault('pool_dma_tokens', [])
            if len(lst) >= NSWDGE:
                pk, pv = lst[-NSWDGE]
                if self.waited[eng].get(pk, 0) < pv:
                    self.waited[eng][pk] = pv
                    waits.append((pk, pv))
            lst.append(token)
        self.dma_tokens.append(token)
        if is_output:
            self.out_tokens.append(token)
        self.q[eng].append((waits, lambda e: e.dma_start(out=out, in_=in_, **kw), token, 16))
        return token

    def mm(self, out, lhsT, rhs, start=True, stop=True, **kw):
        return self.op('pe', lambda e: e.matmul(out, lhsT, rhs, start=start, stop=stop, **kw),
                       [lhsT, rhs] + ([] if start else [out]), [out], self_sync=False)

    def transpose(self, out, in_, ident):
        return self.op('pe', lambda e: e.transpose(out, in_, ident), [in_, ident], [out], self_sync=False)

    def act(self, out, in_, func, scale=1.0, bias=0.0, eng='act', accum_out=None):
        rd = [in_]
        if not isinstance(scale, (int, float)):
            rd.append(scale)
        if not isinstance(bias, (int, float)):
            rd.append(bias)
        kw = {}
        if accum_out is not None:
            kw['accum_out'] = accum_out
        return self.op(eng, lambda e: e.activation(out=out, in_=in_, func=func, scale=scale, bias=bias, **kw),
                       rd, [out, accum_out])

    def tt(self, out, in0, in1, op, eng='dve'):
        return self.op(eng, lambda e: e.tensor_tensor(out=out, in0=in0, in1=in1, op=op), [in0, in1], [out])

    def ts(self, out, in0, s1, op0, s2=None, op1=None, eng='dve'):
        rd = [in0]
        if not isinstance(s1, (int, float)):
            rd.append(s1)
        if s2 is not None and not isinstance(s2, (int, float)):
            rd.append(s2)
        kw = {}
        if op1 is not None:
            kw['op1'] = op1
        return self.op(eng, lambda e: e.tensor_scalar(out=out, in0=in0, scalar1=s1, scalar2=s2, op0=op0, **kw),
                       rd, [out])

    def stt(self, out, in0, scalar, in1, op0, op1, eng='dve'):
        rd = [in0, in1]
        if not isinstance(scalar, (int, float)):
            rd.append(scalar)
        return self.op(eng, lambda e: e.scalar_tensor_tensor(out=out, in0=in0, scalar=scalar, in1=in1, op0=op0, op1=op1),
                       rd, [out])

    def copy(self, out, in_, eng='dve'):
        if eng == 'act':
            return self.op(eng, lambda e: e.copy(out=out, in_=in_), [in_], [out])
        return self.op(eng, lambda e: e.tensor_copy(out=out, in_=in_), [in_], [out])

    def memset(self, out, val, eng='dve'):
        return self.op(eng, lambda e: e.memset(out, val), [], [out])

    def recip(self, out, in_, eng='dve'):
        return self.op(eng, lambda e: e.reciprocal(out=out, in_=in_), [in_], [out])

    def scan(self, out, data0, data1, initial, op0, op1):
        rd = [data0, data1]
        if not isinstance(initial, (int, float)):
            rd.append(initial)
        return self.op('dve', lambda e: e.tensor_tensor_scan(out=out, data0=data0, data1=data1, initial=initial,
                                                             op0=op0, op1=op1), rd, [out])

    def finalize(self):
        nc = self.nc
        fw = []
        for k, v in self.out_tokens:
            if self.waited['sp'].get(k, 0) < v:
                self.waited['sp'][k] = v
                fw.append((k, v))
        self.q['sp'].append((fw, None, None, 0))
        with ExitStack() as st:
            sems = {}

            def sem(k):
                if k not in sems:
                    sems[k] = st.enter_context(nc.semaphore("s_" + "_".join(str(x) for x in k)))
                return sems[k]

            for e in self.ENGS:
                for ph in range((self.cnt[e] + PHASE - 1) // PHASE):
                    sem(('E', e, ph))
            for i in range(min(self.ndma, NDSEM)):
                sem(('D', i))
            block = st.enter_context(nc.Block())

            def replay(name):
                def body(eng):
                    for waits, fn, token, inc in self.q[name]:
                        for k, v in waits:
                            eng.wait_ge(sem(k), v)
                        if fn is not None:
                            fn(eng).then_inc(sem(token[0]), inc)
                return body

            block.sync(replay('sp'))
            block.tensor(replay('pe'))
            block.scalar(replay('act'))
            block.vector(replay('dve'))
            block.gpsimd(replay('pool'))
        return nc


import math

D = 1024
T = 2304
CT = 256
NT = 18
TB = [(0, 256), (256, 512), (768, 512), (1280, 512), (1792, 512)]
FFN_H = 2816
EPS = 1e-6
ROPE_RET, ROPE_DIFF, ROPE_MLA = 0, 1, 2


class Ring:
    def __init__(self, aps):
        self.aps = aps
        self.i = 0

    def next(self):
        a = self.aps[self.i % len(self.aps)]
        self.i += 1
        return a


class Arena:
    def __init__(self, P, name, nbytes):
        self.ap = P.sb(name, [128, nbytes // 4], F32)
        self.n = nbytes // 4
        self.off = 0

    def reset(self, to=0):
        self.off = to

    def alloc(self, shape, dtype=F32):
        n = 1
        for s in shape:
            n *= s
        words = (n * (4 if dtype == F32 else 2) + 3) // 4
        words = (words + 7) // 8 * 8
        assert self.off + words <= self.n, ("arena overflow", self.off, words, self.n)
        v = self.ap[:, self.off:self.off + words]
        self.off += words
        if dtype != F32:
            v = v.bitcast(dtype)
        v = v[:, 0:n]
        if len(shape) == 2:
            v = v.rearrange("p (a b) -> p a b", b=shape[1])
        elif len(shape) == 3:
            v = v.rearrange("p (a b c) -> p a b c", b=shape[1], c=shape[2])
        return v

    def ring(self, k, shape, dtype=F32):
        return Ring([self.alloc(shape, dtype) for _ in range(k)])


def vec_layout():
    cols = {}
    n = 0

    def add(name, k):
        nonlocal n
        cols[name] = n
        n += k
    add('c', 8)
    add('cctx', 8)
    for l in range(4):
        add(('ada_b', l), 48)
    for l in range(4):
        for j in range(2):
            add(('norm_g', l, j), 8)
    add('final_g', 8)
    for e in range(2):
        for d in range(2):
            add(('lb', e, d), 4)
    for e in range(2):
        add(('hgrn_g', e), 1)
        add(('qn_g', e), 3)
        add(('kvn_g', e), 2)
    for o in range(2):
        add(('subln', o), 1)
    return cols, n


VCOL, NV = vec_layout()
CB_IDENT, CB_ONES, CB_MF, CB_MB, CB_BM, CB_PERM = 0, 128, 256, 384, 512, 516
CB_M128 = 516 + 3 * 128
CB_M64 = CB_M128 + 256
CB_BM2 = CB_M64 + 256
NCB = CB_BM2 + 2
HGRN_C = (32, 64)
RET_C = 128


def build_program(n_layers=4):
    P = Prog()
    d_xT = P.dram("xT", [D, T])
    d_vec = P.dram("vecs", [128, NV])
    d_bc = P.dram("bc", [128, 16 + 512])
    d_cb = P.dram("cb", [128, NCB])
    d_rope = P.dram("rope", [3, 2, 128, 2048])
    d_ada = P.dram("ada_w", [4, D, 6 * D])
    d_wout = P.dram("mix_w_out", [4, D, D])
    d_wg = P.dram("ffn_w_gate", [4, D, FFN_H])
    d_wu = P.dram("ffn_w_up", [4, D, FFN_H])
    d_wd = P.dram("ffn_w_down", [4, FFN_H, D])
    d_ewin = P.dram("even_w_in", [2, D, 3232])
    d_wuq = P.dram("mla_w_uq", [2, 384, 768])
    d_wukv = P.dram("mla_w_ukv", [2, 256, 1024])
    d_owin = P.dram("odd_w_in", [2, D, 3072])
    d_out = P.dram("outT", [D, 2048], kind="ExternalOutput")

    XT = P.sb("XT", [128, 8, T], F32)
    HT = P.sb("HT", [128, 8, T], BF16)
    VEC = P.sb("VEC", [128, NV], F32)
    BC = P.sb("BCs", [128, 16 + 512], F32)
    CB = P.sb("CBs", [128, NCB], BF16)
    MOD = P.sb("MOD", [128, 4, 48, 2], F32)
    AMOD = P.sb("AMOD", [128, 4, 2, 8, 2], F32)
    SM = P.sb("SM", [128, 64], F32)
    S2 = P.sb("S2", [128, 8, 2], F32)
    IDENT = CB[:, CB_IDENT:CB_IDENT + 128]
    ONES = CB[:, CB_ONES:CB_ONES + 128]
    MASKS = {32: [CB[:, CB_MF:CB_MF + 128], CB[:, CB_MB:CB_MB + 128]],
             64: [CB[:, CB_M64:CB_M64 + 128], CB[:, CB_M64 + 128:CB_M64 + 256]],
             128: [CB[:, CB_M128:CB_M128 + 128], CB[:, CB_M128 + 128:CB_M128 + 256]]}
    BMS = {32: CB[:, CB_BM:CB_BM + 4], 64: CB[:, CB_BM2:CB_BM2 + 2]}
    BM4 = CB[:, CB_BM:CB_BM + 4]
    PERM = [CB[:, CB_PERM + 128 * i:CB_PERM + 128 * (i + 1)] for i in range(3)]
    C_EPS, C_ZERO, C_ONE = 0, 1, 2
    C_LG = 4
    C_NLAM = 20
    C_GS = 22
    C_OML = 24
    C_TMP = 40
    EPSC = SM[:, C_EPS:C_EPS + 1]
    ZEROC = SM[:, C_ZERO:C_ZERO + 1]
    ONEC = SM[:, C_ONE:C_ONE + 1]

    AR = Arena(P, "ARENA", 90 * 1024)
    pbanks = [P.ps("bank%d" % i, [128, 512], F32) for i in range(8)]
    PS_T = Ring(pbanks[0:4])
    PS_A = Ring(pbanks[4:8])

    P.dma(VEC, d_vec)
    P.dma(BC, d_bc)
    P.dma(CB, d_cb, eng='pool')
    P.dma(XT, d_xT.rearrange("(c p) t -> p c t", p=128))
    P.memset(SM, 0.0)
    P.memset(SM[:, C_EPS:C_EPS + 1], EPS)
    P.memset(SM[:, C_ONE:C_ONE + 1], 1.0)

    for k_, nm_ in ((0, 'c'), (1, 'cctx')):
        src_ = VEC[:, VCOL[nm_]:VCOL[nm_] + 8]
        P.act(S2[:, :, k_], src_, AF.Exp, scale=-1.0)
        P.act(S2[:, :, k_], S2[:, :, k_], AF.Ln, bias=SM[:, C_ONE:C_ONE + 1])
        P.act(S2[:, :, k_], S2[:, :, k_], AF.Exp, scale=-1.0)
        P.tt(S2[:, :, k_], S2[:, :, k_], src_, ALU.mult)

    AR.reset()
    adaring = AR.ring(3, (8, 512), BF16)
    S2b = AR.alloc((8, 2), BF16)
    P.copy(S2b, S2)
    for l in range(n_layers):
        mps = PS_A.next()
        for g in range(12):
            wa = adaring.next()
            P.dma(wa, d_ada[l, :, g * 512:(g + 1) * 512].rearrange("(kc p) f -> p kc f", p=128), eng='pool')
            for jj in range(4):
                j = g * 4 + jj
                for kc in range(8):
                    P.mm(mps[:, 2 * j:2 * j + 2], wa[:, kc, jj * 128:(jj + 1) * 128], S2b[:, kc, :],
                         start=(kc == 0), stop=(kc == 7))
        cb0 = VCOL[('ada_b', l)]
        P.tt(MOD[:, l, :, :], mps[:, 0:96].rearrange("p (j k) -> p j k", k=2),
             VEC[:, cb0:cb0 + 48].unsqueeze(2).to_broadcast([128, 48, 2]), ALU.add)
        for which in range(2):
            g0 = VCOL[('norm_g', l, which)]
            sc = MOD[:, l, 8 + 24 * which:16 + 24 * which, :]
            P.stt(AMOD[:, l, which, :, :], sc, 1.0, VEC[:, g0:g0 + 8].unsqueeze(2).to_broadcast([128, 8, 2]),
                  ALU.add, ALU.mult)

    def SH(l, which):
        return MOD[:, l, 24 * which:24 * which + 8, :]

    def GATE(l, which):
        return MOD[:, l, 16 + 24 * which:24 + 24 * which, :]

    P.act(SM[:, C_LG:C_LG + 16], BC[:, 0:16], AF.Exp, scale=-1.0)
    P.act(SM[:, C_LG:C_LG + 16], SM[:, C_LG:C_LG + 16], AF.Ln, bias=ONEC)
    P.ts(SM[:, C_LG:C_LG + 16], SM[:, C_LG:C_LG + 16], -1.0, ALU.mult)
    for o in range(2):
        layer_idx = 2 * o + 1
        lam_init = 0.8 - 0.6 * math.exp(-0.3 * layer_idx)
        dl = BC[:, 16 + 256 * o:16 + 256 * (o + 1)]
        tmp = AR.alloc((128,), F32) if o == 0 else tmp
        P.tt(tmp[:, 0:64], dl[:, 0:64], dl[:, 64:128], ALU.mult)
        P.tt(tmp[:, 64:128], dl[:, 128:192], dl[:, 192:256], ALU.mult)
        P.op('dve', lambda e, tmp=tmp: e.tensor_reduce(out=SM[:, C_TMP:C_TMP + 2],
                                                      in_=tmp.rearrange("p (a b) -> p a b", b=64),
                                                      axis=AX.X, op=ALU.add),
             [tmp], [SM[:, C_TMP:C_TMP + 2]])
        P.act(SM[:, C_TMP:C_TMP + 2], SM[:, C_TMP:C_TMP + 2], AF.Exp)
        P.tt(SM[:, C_NLAM + o:C_NLAM + o + 1], SM[:, C_TMP + 1:C_TMP + 2], SM[:, C_TMP:C_TMP + 1], ALU.subtract)
        P.ts(SM[:, C_NLAM + o:C_NLAM + o + 1], SM[:, C_NLAM + o:C_NLAM + o + 1], -lam_init, ALU.add)
        sc0 = VCOL[('subln', o)]
        P.ts(SM[:, C_GS + o:C_GS + o + 1], VEC[:, sc0:sc0 + 1], 1.0 - lam_init, ALU.mult)
    P.memset(SM[:, C_OML:C_OML + 8], 1.0)
    for d in range(2):
        a0 = VCOL[('lb', 0, d)]
        a1 = VCOL[('lb', 1, d)]
        dst = SM[:, C_OML + 8 + 4 * d:C_OML + 12 + 4 * d]
        P.tt(dst, VEC[:, a0:a0 + 4], VEC[:, a1:a1 + 4], ALU.subtract)
        P.act(dst, dst, AF.Exp, scale=-1.0)
        P.act(dst, dst, AF.Ln, bias=SM[:, C_ONE:C_ONE + 1])
        P.act(dst, dst, AF.Exp, scale=-1.0)

    def kcol(t0):
        return 1 if t0 < CT else 0

    def norm_mod(l, which):
        AR.reset()
        sqr = AR.ring(3, (512,), BF16)
        rsr = AR.ring(2, (512,), F32)
        tmr = AR.ring(3, (512,), F32)
        for (t0, n) in TB:
            k = kcol(t0)
            R = PS_T.next()
            for c in range(8):
                sq = sqr.next()[:, :n]
                P.act(sq, XT[:, c, t0:t0 + n], AF.Square)
                P.mm(R[:, :n], ONES, sq, start=(c == 0), stop=(c == 7))
            rs = rsr.next()[:, :n]
            rstd_from_sumsq(rs, R[:, :n], D)
            for c in range(8):
                tmp = tmr.next()[:, :n]
                P.stt(tmp, XT[:, c, t0:t0 + n], AMOD[:, l, which, c, k:k + 1], rs, ALU.mult, ALU.mult)
                P.act(HT[:, c, t0:t0 + n], tmp, AF.Identity, bias=SH(l, which)[:, c, k:k + 1])

    def proj(ps, w, src, t0, n, KC=8):
        for kc in range(KC):
            P.mm(ps, w[:, kc, :], src[:, kc, t0:t0 + n], start=(kc == 0), stop=(kc == KC - 1))

    def load_w(dst, src):
        P.dma(dst, src, eng='pool')

    def outproj(l, WO, OT, t0, n):
        k = kcol(t0)
        for i in range(8):
            ps = PS_T.next()
            P.mm(ps[:, :n], WO[:, i * 128:(i + 1) * 128], OT)
            P.stt(XT[:, i, t0:t0 + n], ps[:, :n], GATE(l, 0)[:, i, k:k + 1], XT[:, i, t0:t0 + n], ALU.mult, ALU.add)

    def sigmoid_act(dst, src, sign):
        P.act(dst, src, AF.Exp, scale=-float(sign))
        P.act(dst, dst, AF.Ln, bias=ONEC[:dst.shape[0], :])
        P.act(dst, dst, AF.Exp, scale=-1.0)

    def rstd_from_sumsq(rs, R, nfeat):
        P.act(rs, R, AF.Ln, scale=1.0 / nfeat, bias=EPSC[:rs.shape[0], :])
        P.act(rs, rs, AF.Exp, scale=-0.5)

    def rope(dst, ps, np_, t0, n, variant, raw, cosr, sinr, t1r):
        if t0 < CT:
            P.copy(dst, ps[:np_, :n], eng='act')
            return
        r = raw.next()[:np_, :n]
        P.copy(r, ps[:np_, :n], eng='act')
        pp = PS_T.next()
        P.mm(pp[:np_, :n], PERM[variant][:np_, :np_], r)
        co = cosr.next()[:np_, :n]
        si = sinr.next()[:np_, :n]
        P.dma(co, d_rope[variant, 0, 0:np_, t0 - CT:t0 - CT + n])
        P.dma(si, d_rope[variant, 1, 0:np_, t0 - CT:t0 - CT + n])
        t1 = t1r.next()[:np_, :n]
        P.tt(t1, r, co, ALU.mult)
        P.tt(si, pp[:np_, :n], si, ALU.mult)
        P.tt(dst, t1, si, ALU.add)

    GLA_ORD = [list(range(NT)), [1, 0] + list(range(NT - 1, 1, -1))]

    def gla_alloc(dk, C):
        ncl = 128 // C
        return dict(
            QT=AR.alloc((T,), BF16)[:dk], KP=AR.alloc((T,), BF16)[:dk], KT=AR.alloc((NT, 128), BF16),
            DTOT=AR.alloc((72,), F32)[:dk], TOTL=AR.alloc((72,), F32)[:dk], GE=AR.alloc((80,), F32)[:dk],
            vbr=AR.ring(2, (ncl, 128), BF16), amr=AR.ring(2, (128,), BF16), sbr=AR.ring(2, (ncl, 128), BF16),
            S=[AR.alloc((128,), F32)[:dk] for _ in range(4)])

    def gla_prepass(dk, QS, KF, GBUF, d, st, f32r, k2r, C):
        NCH = T // C
        GE, QT_, KP, KT = st['GE'], st['QT'], st['KP'], st['KT']
        DTOT = st['DTOT'][:, 0:NCH]
        TOTL = st['TOTL'][:, 0:NCH]
        P.scan(GBUF[:, 32:32 + T], ONEC[:dk, :].to_broadcast([dk, T]), GBUF[:, 32:32 + T], 0.0, ALU.mult, ALU.add)
        P.memset(GE[:, 0:1], 0.0)
        P.copy(GE[:, 1:NCH + 1], GBUF[:, 32:32 + T].rearrange("p (c j) -> p c j", j=C)[:, :, C - 1])
        P.tt(TOTL, GE[:, 1:NCH + 1], GE[:, 0:NCH], ALU.subtract)
        P.act(DTOT, TOTL, AF.Exp)
        if d == 0:
            B = GBUF[:, 32:32 + T]
            Bv = B.rearrange("p (c j) -> p c j", j=C)
            P.tt(Bv, Bv, GE[:, 0:NCH].unsqueeze(2).to_broadcast([dk, NCH, C]), ALU.subtract)
        else:
            B = GBUF[:, 31:31 + T]
            Bv = B.rearrange("p (c j) -> p c j", j=C)
            P.tt(Bv, GE[:, 1:NCH + 1].unsqueeze(2).to_broadcast([dk, NCH, C]), Bv, ALU.subtract)
        P.ts(B, B, -80.0, ALU.max)
        for (t0, n) in TB:
            eb = f32r.next()[:dk, :n]
            P.act(eb, B[:, t0:t0 + n], AF.Exp)
            P.tt(QT_[:, t0:t0 + n], QS[:, t0:t0 + n], eb, ALU.mult)
            enb = f32r.next()[:dk, :n]
            P.act(enb, B[:, t0:t0 + n], AF.Exp, scale=-1.0)
            P.tt(KP[:, t0:t0 + n], KF[:, t0:t0 + n], enb, ALU.mult)
            e2 = f32r.next()[:dk, :n]
            P.tt(e2.rearrange("p (c j) -> p c j", j=C),
                 TOTL[:, t0 // C:(t0 + n) // C].unsqueeze(2).to_broadcast([dk, n // C, C]),
                 B[:, t0:t0 + n].rearrange("p (c j) -> p c j", j=C), ALU.subtract)
            P.act(e2, e2, AF.Exp)
            k2 = k2r.next()[:dk, :n]
            P.tt(k2, KF[:, t0:t0 + n], e2, ALU.mult)
            for i in range(n // 128):
                tt_ = t0 // 128 + i
                pt = PS_T.next()[:, 0:64].bitcast(BF16)
                P.transpose(pt[:, :dk], k2[:, i * 128:(i + 1) * 128], IDENT[:dk, :dk])
                P.copy(KT[:, tt_, :dk], pt[:, :dk], eng='act')

    def gla_tiles(dk, sts, VH, OACC, C):
        ncl = 128 // C
        scur = [0, 0]
        nxt = [None, None]
        Uring = [Ring([pbanks[4], pbanks[5]]), Ring([pbanks[6], pbanks[7]])]
        cords = [list(range(ncl)), list(range(ncl - 1, -1, -1))]

        def emit_front(d, tt_):
            st = sts[d]
            U = Uring[d].next()
            if ncl == 1:
                P.mm(U[:dk, 0:128], st['KT'][:, tt_, :dk], VH[:, tt_, :])
            else:
                vb = st['vbr'].next()
                P.tt(vb, VH[:, tt_, :].unsqueeze(1).to_broadcast([128, ncl, 128]),
                     BMS[C].unsqueeze(2).to_broadcast([128, ncl, 128]), ALU.mult, eng='pool')
                P.mm(U[:dk, 0:ncl * 128], st['KT'][:, tt_, :dk], vb.rearrange("p a b -> p (a b)"))
            AT_ = PS_T.next()
            P.mm(AT_[:, 0:128], st['KP'][:, tt_ * 128:(tt_ + 1) * 128], st['QT'][:, tt_ * 128:(tt_ + 1) * 128])
            return U, AT_

        for d in range(2):
            P.memset(sts[d]['S'][0], 0.0)
            nxt[d] = emit_front(d, GLA_ORD[d][0])
        written = set()
        for oi in range(NT):
            for d in range(2):
                st = sts[d]
                Sst = st['S']
                tt_ = GLA_ORD[d][oi]
                U, AT_ = nxt[d]
                am = st['amr'].next()
                P.tt(am, AT_[:, 0:128], MASKS[C][d], ALU.mult)
                if oi + 1 < NT:
                    nxt[d] = emit_front(d, GLA_ORD[d][oi + 1])
                sb16 = st['sbr'].next()
                for cl in cords[d]:
                    c = tt_ * ncl + cl
                    P.copy(sb16[:dk, cl, :], Sst[scur[d]], eng='act')
                    P.stt(Sst[(scur[d] + 1) % 4], Sst[scur[d]], st['DTOT'][:, c:c + 1],
                          U[:dk, cl * 128:(cl + 1) * 128], ALU.mult, ALU.add)
                    scur[d] = (scur[d] + 1) % 4
                O = PS_T.next()
                P.mm(O[:, 0:128], VH[:, tt_, :], am, start=True, stop=False)
                for cl in range(ncl):
                    P.mm(O[:, cl * C:(cl + 1) * C], sb16[:dk, cl, :],
                         st['QT'][:, tt_ * 128 + cl * C:tt_ * 128 + (cl + 1) * C], start=False, stop=(cl == ncl - 1))
                oa = OACC[:, tt_ * 128:(tt_ + 1) * 128]
                if tt_ not in written:
                    written.add(tt_)
                    P.copy(oa, O[:, 0:128], eng='act')
                else:
                    P.tt(oa, O[:, 0:128], oa, ALU.add)

    def readout_gated(l, WO, OACC, gw, gain_col, t0, n, rings):
        sqr, rsr, tmr, otr = rings
        sq = sqr.next()[:, :n]
        P.act(sq, OACC[:, t0:t0 + n], AF.Square)
        R = PS_T.next()
        P.mm(R[:, :n], ONES, sq)
        rs = rsr.next()[:, :n]
        rstd_from_sumsq(rs, R[:, :n], 128)
        pg = PS_T.next()
        proj(pg[:, :n], gw, HT, t0, n)
        sg = tmr.next()[:, :n]
        sigmoid_act(sg, pg[:, :n], 1)
        P.tt(sg, pg[:, :n], sg, ALU.mult)
        t1 = tmr.next()[:, :n]
        if gain_col is None:
            P.tt(t1, OACC[:, t0:t0 + n], rs, ALU.mult)
        else:
            P.stt(t1, OACC[:, t0:t0 + n], gain_col, rs, ALU.mult, ALU.mult)
        ot = otr.next()[:, :n]
        P.tt(ot, t1, sg, ALU.mult)
        outproj(l, WO, ot, t0, n)

    def attn_block(pairs, t0, n, scale, ptr, with_den=True):
        kts = [0, 1] if t0 < CT else list(range(NT))
        res = []
        for (Kt, Qt, Vl) in pairs:
            O = PS_A.next()
            DEN = PS_A.next() if with_den else None

            def issue_S(kt):
                S = PS_T.next()
                P.mm(S[:, :n], Kt[:, kt * 128:(kt + 1) * 128], Qt[:, t0:t0 + n])
                return S
            pend = [issue_S(kt) for kt in kts[:2]]
            for ki, kt in enumerate(kts):
                S = pend.pop(0)
                if ki + 2 < len(kts):
                    pend.append(issue_S(kts[ki + 2]))
                pt = ptr.next()[:, :n]
                P.act(pt, S[:, :n], AF.Exp, scale=scale)
                P.mm(O[:, :n], Vl(kt), pt, start=(ki == 0), stop=(ki == len(kts) - 1))
                if with_den:
                    P.mm(DEN[:, :n], ONES, pt, start=(ki == 0), stop=(ki == len(kts) - 1))
            res.append((O, DEN))
        return res

    def even_mixer(l):
        e = l // 2
        W = d_ewin[e]
        AR.reset()
        WOr = AR.ring(1, (1024,), BF16)
        hmark = AR.off
        for h in range(4):
            AR.reset(hmark)
            WH = AR.alloc((8, 5, 128), BF16)
            for g in range(5):
                load_w(WH[:, :, g, :], W[:, g * 512 + h * 128:g * 512 + (h + 1) * 128].rearrange("(kc p) c -> p kc c", p=128))
            WO = WOr.next()
            load_w(WO, d_wout[l, h * 128:(h + 1) * 128, :])
            VH = AR.alloc((NT, 128), BF16)
            Ch = HGRN_C[e]
            sts = [gla_alloc(128, Ch), gla_alloc(128, Ch)]
            tmark = AR.off
            QS = AR.alloc((T,), F32)
            KF = AR.alloc((T,), BF16)
            GBUF = AR.alloc((32 + T,), F32)
            f32r = AR.ring(3, (512,), F32)
            k2r = AR.ring(2, (512,), BF16)
            for tt_ in range(NT):
                ps = PS_T.next()
                for kc in range(8):
                    P.mm(ps[:, 0:128], HT[:, kc, tt_ * 128:(tt_ + 1) * 128], WH[:, kc, 1, :], start=(kc == 0), stop=(kc == 7))
                P.copy(VH[:, tt_, :], ps[:, 0:128], eng='act')
            for (t0, n) in TB:
                ps = PS_T.next()
                proj(ps[:, :n], WH[:, :, 0, :], HT, t0, n)
                P.act(QS[:, t0:t0 + n], ps[:, :n], AF.Copy, scale=128 ** -0.5)
            for d in range(2):
                P.memset(GBUF[:, 0:32], 0.0)
                for (t0, n) in TB:
                    ps = PS_T.next()
                    proj(ps[:, :n], WH[:, :, 2 + d, :], HT, t0, n)
                    sk = f32r.next()[:, :n]
                    sigmoid_act(sk, ps[:, :n], -1)
                    P.ts(GBUF[:, 32 + t0:32 + t0 + n], sk, SM[:, C_OML + 8 * e + 4 * d + h:C_OML + 8 * e + 4 * d + h + 1], ALU.mult)
                    P.copy(KF[:, t0:t0 + n], GBUF[:, 32 + t0:32 + t0 + n], eng='pool')
                P.act(GBUF[:, 32:32 + T], GBUF[:, 32:32 + T], AF.Ln, scale=-1.0, bias=ONEC)
                gla_prepass(128, QS, KF, GBUF, d, sts[d], f32r, k2r, Ch)
            AR.reset(tmark)
            OACC = AR.alloc((T,), F32)
            gla_tiles(128, sts, VH, OACC, Ch)
            rings = (AR.ring(2, (512,), BF16), AR.ring(2, (512,), F32), AR.ring(3, (512,), F32), AR.ring(2, (512,), BF16))
            gcol = VEC[:, VCOL[('hgrn_g', e)]:VCOL[('hgrn_g', e)] + 1]
            for (t0, n) in TB:
                readout_gated(l, WO, OACC, WH[:, :, 4, :], gcol, t0, n, rings)
        AR.reset(hmark)
        WUQ = AR.alloc((3, 768), BF16)
        load_w(WUQ, d_wuq[e].rearrange("(kc p) f -> p kc f", p=128))
        WUKV = AR.alloc((2, 1024), BF16)
        load_w(WUKV, d_wukv[e].rearrange("(kc p) f -> p kc f", p=128))
        CQN = AR.alloc((3, T), BF16)
        CKV = AR.alloc((2, T), BF16)
        KRT = AR.alloc((T,), BF16)
        raw = AR.ring(2, (512,), BF16)
        cosr = AR.ring(2, (512,), F32)
        sinr = AR.ring(2, (512,), F32)
        t1r = AR.ring(2, (512,), F32)
        lmark = AR.off
        WL = AR.alloc((8, 640), BF16)
        load_w(WL, W[:, 2560:3200].rearrange("(kc p) f -> p kc f", p=128))
        WKR = AR.alloc((8, 96), BF16)
        P.memset(WKR, 0.0)
        load_w(WKR[:, :, 64:96], W[:, 3200:3232].rearrange("(kc p) f -> p kc f", p=128))
        c32 = AR.alloc((3, 512), F32)
        sqr = AR.ring(2, (512,), BF16)
        rsr = AR.ring(2, (512,), F32)
        for (t0, n) in TB:
            for (c0, nch, dst, gname) in ((0, 3, CQN, 'qn_g'), (3, 2, CKV, 'kvn_g')):
                R = PS_A.next()
                for c in range(nch):
                    ps = PS_T.next()
                    proj(ps[:, :n], WL[:, :, (c0 + c) * 128:(c0 + c + 1) * 128], HT, t0, n)
                    P.copy(c32[:, c, :n], ps[:, :n], eng='act')
                    sq = sqr.next()[:, :n]
                    P.act(sq, ps[:, :n], AF.Square)
                    P.mm(R[:, :n], ONES, sq, start=(c == 0), stop=(c == nch - 1))
                rs = rsr.next()[:, :n]
                rstd_from_sumsq(rs, R[:, :n], 128 * nch)
                g0 = VCOL[(gname, e)]
                for c in range(nch):
                    P.stt(dst[:, c, t0:t0 + n], c32[:, c, :n], VEC[:, g0 + c:g0 + c + 1], rs, ALU.mult, ALU.mult)
            ps = PS_T.next()
            proj(ps[:96, :n], WKR, HT, t0, n)
            rope(KRT[:96, t0:t0 + n], ps, 96, t0, n, ROPE_MLA, raw, cosr, sinr, t1r)
        AR.reset(lmark)
        pmark = AR.off
        sc = 96 ** -0.5
        for hp in range(4):
            AR.reset(pmark)
            WO = WOr.next()
            load_w(WO, d_wout[l, (4 + hp) * 128:(5 + hp) * 128, :])
            QTs, KTs, VAs = [], [], []
            for hh in range(2):
                h = 2 * hp + hh
                QTh = AR.alloc((T,), BF16)
                KTh = AR.alloc((T,), BF16)
                VA = AR.alloc((NT, 128), BF16)
                P.memset(VA, 1.0, eng='pool')
                for (t0, n) in TB:
                    ps = PS_T.next()
                    proj(ps[:96, :n], WUQ[:, :, 96 * h:96 * h + 96], CQN, t0, n, KC=3)
                    rope(QTh[:96, t0:t0 + n], ps, 96, t0, n, ROPE_MLA, raw, cosr, sinr, t1r)
                    ps = PS_T.next()
                    proj(ps[:64, :n], WUKV[:, :, 128 * h:128 * h + 64], CKV, t0, n, KC=2)
                    P.copy(KTh[:64, t0:t0 + n], ps[:64, :n], eng='act')
                P.copy(KTh[64:96, :], KRT[64:96, :], eng='pool')
                for tt_ in range(NT):
                    ps = PS_T.next()
                    for kc in range(2):
                        P.mm(ps[:, 0:64], CKV[:, kc, tt_ * 128:(tt_ + 1) * 128], WUKV[:, kc, 128 * h + 64:128 * h + 128],
                             start=(kc == 0), stop=(kc == 1))
                    P.copy(VA[:, tt_, 64 * hh:64 * hh + 64], ps[:, 0:64], eng='act')
                QTs.append(QTh)
                KTs.append(KTh)
                VAs.append(VA)
            ptr = AR.ring(4, (512,), BF16)
            rdr = AR.ring(1, (512,), F32)
            otr = AR.ring(2, (512,), BF16)
            for (t0, n) in TB:
                pairs = [(KTs[hh][:96, :], QTs[hh][:96, :], (lambda kt, hh=hh: VAs[hh][:, kt, :])) for hh in range(2)]
                res = attn_block(pairs, t0, n, sc, ptr, with_den=False)
                ot = otr.next()[:, :n]
                for hh in range(2):
                    O, _ = res[hh]
                    rd = rdr.next()[:, :n]
                    sl = slice(64 * hh, 64 * hh + 64)
                    so = slice(64 * (1 - hh), 64 * (1 - hh) + 64)
                    P.act(rd[sl, :], O[so, :n], AF.Ln)
                    P.act(rd[sl, :], rd[sl, :], AF.Exp, scale=-1.0)
                    P.tt(ot[sl, :], O[sl, :n], rd[sl, :], ALU.mult)
                outproj(l, WO, ot, t0, n)

    def odd_mixer(l):
        o = l // 2
        W = d_owin[o]
        AR.reset()
        WOr = AR.ring(2, (1024,), BF16)
        hmark = AR.off
        for h in range(4):
            AR.reset(hmark)
            WR = AR.alloc((8, 384), BF16)
            Wv = W.rearrange("(kc p) f -> p kc f", p=128)
            load_w(WR[:, :, 0:64], Wv[:, :, 64 * h:64 * h + 64])
            load_w(WR[:, :, 64:128], Wv[:, :, 256 + 64 * h:256 + 64 * h + 64])
            load_w(WR[:, :, 128:256], Wv[:, :, 512 + 128 * h:512 + 128 * h + 128])
            load_w(WR[:, :, 256:384], Wv[:, :, 1024 + 128 * h:1024 + 128 * h + 128])
            WO = WOr.next()
            load_w(WO, d_wout[l, h * 128:(h + 1) * 128, :])
            VH = AR.alloc((NT, 128), BF16)
            QSf = AR.alloc((T,), F32)
            QS = QSf
            KF = AR.alloc((T,), BF16)
            GBUF = AR.alloc((32 + T,), F32)
            gmark = AR.off
            raw = AR.ring(2, (512,), BF16)
            cosr = AR.ring(2, (512,), F32)
            sinr = AR.ring(2, (512,), F32)
            t1r = AR.ring(2, (512,), F32)
            kfr = AR.ring(2, (512,), F32)
            for tt_ in range(NT):
                ps = PS_T.next()
                for kc in range(8):
                    P.mm(ps[:, 0:128], HT[:, kc, tt_ * 128:(tt_ + 1) * 128], WR[:, kc, 128:256], start=(kc == 0), stop=(kc == 7))
                P.copy(VH[:, tt_, :], ps[:, 0:128], eng='act')
            for (t0, n) in TB:
                ps = PS_T.next()
                proj(ps[:64, :n], WR[:, :, 0:64], HT, t0, n)
                rope(QS[:64, t0:t0 + n], ps, 64, t0, n, ROPE_RET, raw, cosr, sinr, t1r)
                ps = PS_T.next()
                proj(ps[:64, :n], WR[:, :, 64:128], HT, t0, n)
                kf = kfr.next()[:64, :n]
                rope(kf, ps, 64, t0, n, ROPE_RET, raw, cosr, sinr, t1r)
                P.ts(KF[:64, t0:t0 + n], kf, 64 ** -0.5, ALU.mult)
            AR.reset(gmark)
            sts = [gla_alloc(64, RET_C), gla_alloc(64, RET_C)]
            f32r = AR.ring(3, (512,), F32)
            k2r = AR.ring(2, (512,), BF16)
            for d in range(2):
                col = C_LG + 8 * o + 4 * d + h
                P.memset(GBUF[:, 0:32], 0.0)
                P.act(GBUF[:64, 32:32 + T], ZEROC[:64, :].to_broadcast([64, T]), AF.Identity, scale=0.0,
                      bias=SM[:64, col:col + 1])
                gla_prepass(64, QS[:64], KF[:64], GBUF[:64], d, sts[d], f32r, k2r, RET_C)
            OACC = QSf
            gla_tiles(64, sts, VH, OACC, RET_C)
            AR.reset(gmark)
            rings = (AR.ring(2, (512,), BF16), AR.ring(2, (512,), F32), AR.ring(3, (512,), F32), AR.ring(2, (512,), BF16))
            for (t0, n) in TB:
                readout_gated(l, WO, OACC, WR[:, :, 256:384], None, t0, n, rings)
        sc = 64 ** -0.5
        for h in range(4):
            AR.reset(hmark)
            WD_ = AR.alloc((8, 384), BF16)
            Wv = W.rearrange("(kc p) f -> p kc f", p=128)
            load_w(WD_[:, :, 0:128], Wv[:, :, 1536 + 128 * h:1536 + 128 * h + 128])
            load_w(WD_[:, :, 128:256], Wv[:, :, 2048 + 128 * h:2048 + 128 * h + 128])
            load_w(WD_[:, :, 256:384], Wv[:, :, 2560 + 128 * h:2560 + 128 * h + 128])
            WO = WOr.next()
            load_w(WO, d_wout[l, (4 + h) * 128:(5 + h) * 128, :])
            VH = AR.alloc((NT, 128), BF16)
            QD = AR.alloc((T,), BF16)
            KD = AR.alloc((T,), BF16)
            raw = AR.ring(2, (512,), BF16)
            cosr = AR.ring(2, (512,), F32)
            sinr = AR.ring(2, (512,), F32)
            t1r = AR.ring(2, (512,), F32)
            for tt_ in range(NT):
                ps = PS_T.next()
                for kc in range(8):
                    P.mm(ps[:, 0:128], HT[:, kc, tt_ * 128:(tt_ + 1) * 128], WD_[:, kc, 256:384], start=(kc == 0), stop=(kc == 7))
                P.copy(VH[:, tt_, :], ps[:, 0:128], eng='act')
            for (t0, n) in TB:
                ps = PS_T.next()
                proj(ps[:, :n], WD_[:, :, 0:128], HT, t0, n)
                rope(QD[:, t0:t0 + n], ps, 128, t0, n, ROPE_DIFF, raw, cosr, sinr, t1r)
                ps = PS_T.next()
                proj(ps[:, :n], WD_[:, :, 128:256], HT, t0, n)
                rope(KD[:, t0:t0 + n], ps, 128, t0, n, ROPE_DIFF, raw, cosr, sinr, t1r)
            ptr = AR.ring(4, (512,), BF16)
            f32r = AR.ring(4, (512,), F32)
            sqr = AR.ring(2, (512,), BF16)
            otr = AR.ring(2, (512,), BF16)
            for (t0, n) in TB:
                pairs = [(KD[64 * m:64 * m + 64, :], QD[64 * m:64 * m + 64, :], (lambda kt: VH[:, kt, :])) for m in range(2)]
                res = attn_block(pairs, t0, n, sc, ptr)
                a = []
                for m in range(2):
                    O, DEN = res[m]
                    rd = f32r.next()[:, :n]
                    P.act(rd, DEN[:, :n], AF.Ln)
                    P.act(rd, rd, AF.Exp, scale=-1.0)
                    P.tt(rd, O[:, :n], rd, ALU.mult)
                    a.append(rd)
                ov = f32r.next()[:, :n]
                P.stt(ov, a[1], SM[:, C_NLAM + o:C_NLAM + o + 1], a[0], ALU.mult, ALU.add)
                sq = sqr.next()[:, :n]
                P.act(sq, ov, AF.Square)
                R = PS_T.next()
                P.mm(R[:, :n], ONES, sq)
                rs = f32r.next()[:, :n]
                rstd_from_sumsq(rs, R[:, :n], 128)
                ot = otr.next()[:, :n]
                P.stt(ot, ov, SM[:, C_GS + o:C_GS + o + 1], rs, ALU.mult, ALU.mult)
                outproj(l, WO, ot, t0, n)

    def ffn(l):
        SBS = [TB[0:3], TB[3:5]]
        for sbk in SBS:
            AR.reset()
            base = sbk[0][0]
            ntok = sum(n for _, n in sbk)
            ACTT = AR.alloc((22, ntok), BF16)
            wgr = AR.ring(2, (8, 256), BF16)
            wur = AR.ring(2, (8, 256), BF16)
            sgr = AR.ring(3, (512,), F32)
            for jg in range(11):
                wg = wgr.next()
                wu = wur.next()
                load_w(wg, d_wg[l, :, jg * 256:(jg + 1) * 256].rearrange("(kc p) f -> p kc f", p=128))
                load_w(wu, d_wu[l, :, jg * 256:(jg + 1) * 256].rearrange("(kc p) f -> p kc f", p=128))
                for jj in range(2):
                    j = jg * 2 + jj
                    for (t0, n) in sbk:
                        pg = PS_T.next()
                        proj(pg[:, :n], wg[:, :, jj * 128:(jj + 1) * 128], HT, t0, n)
                        pu = PS_A.next()
                        proj(pu[:, :n], wu[:, :, jj * 128:(jj + 1) * 128], HT, t0, n)
                        sg = sgr.next()[:, :n]
                        P.act(sg, pg[:, :n], AF.Silu)
                        P.tt(ACTT[:, j, t0 - base:t0 - base + n], sg, pu[:, :n], ALU.mult)
            wdr = AR.ring(2, (22, 128), BF16)
            for i in range(8):
                wd = wdr.next()
                load_w(wd, d_wd[l, :, i * 128:(i + 1) * 128].rearrange("(j p) c -> p j c", p=128))
                for (t0, n) in sbk:
                    k = kcol(t0)
                    ps = PS_T.next()
                    for j in range(22):
                        P.mm(ps[:, :n], wd[:, j, :], ACTT[:, j, t0 - base:t0 - base + n], start=(j == 0), stop=(j == 21))
                    P.stt(XT[:, i, t0:t0 + n], ps[:, :n], GATE(l, 1)[:, i, k:k + 1], XT[:, i, t0:t0 + n], ALU.mult, ALU.add)

    for l in range(n_layers):
        norm_mod(l, 0)
        if l % 2 == 0:
            even_mixer(l)
        else:
            odd_mixer(l)
        norm_mod(l, 1)
        ffn(l)

    AR.reset()
    sqr = AR.ring(3, (512,), BF16)
    rsr = AR.ring(2, (512,), F32)
    outr = AR.ring(3, (512,), F32)
    fg = VCOL['final_g']
    for (t0, n) in TB[1:]:
        R = PS_T.next()
        for c in range(8):
            sq = sqr.next()[:, :n]
            P.act(sq, XT[:, c, t0:t0 + n], AF.Square)
            P.mm(R[:, :n], ONES, sq, start=(c == 0), stop=(c == 7))
        rs = rsr.next()[:, :n]
        rstd_from_sumsq(rs, R[:, :n], D)
        for c in range(8):
            ob = outr.next()[:, :n]
            P.stt(ob, XT[:, c, t0:t0 + n], VEC[:, fg + c:fg + c + 1], rs, ALU.mult, ALU.mult)
            P.dma(d_out[c * 128:(c + 1) * 128, t0 - CT:t0 - CT + n], ob, is_output=True)
    return P.finalize()


def _rope_tables():
    tab = np.zeros((3, 2, 128, 2048), np.float32)
    tab[:, 0] = 1.0
    perm = np.zeros((3, 128, 128), np.float32)
    n = np.arange(2048, dtype=np.float32)
    row = np.floor(n / 64.0)
    col = n - 64.0 * row

    def fill(v, p0, width, pos):
        half = width // 2
        inv = (10000.0 ** (-np.arange(half, dtype=np.float32) / half)).astype(np.float32)
        ang = pos[None, :].astype(np.float32) * inv[:, None]
        c, s = np.cos(ang).astype(np.float32), np.sin(ang).astype(np.float32)
        tab[v, 0, p0:p0 + half] = c
        tab[v, 0, p0 + half:p0 + width] = c
        tab[v, 1, p0:p0 + half] = -s
        tab[v, 1, p0 + half:p0 + width] = s
        for j in range(half):
            perm[v, p0 + j + half, p0 + j] = 1.0
            perm[v, p0 + j, p0 + j + half] = 1.0
    for g in range(2):
        fill(ROPE_RET, 64 * g, 64, n)
        fill(ROPE_DIFF, 64 * g, 32, row)
        fill(ROPE_DIFF, 64 * g + 32, 32, col)
    fill(ROPE_MLA, 64, 16, row)
    fill(ROPE_MLA, 80, 16, col)
    return tab, perm


def _consts():
    tab, perm = _rope_tables()
    cb = np.zeros((128, NCB), np.float32)
    cb[:, CB_IDENT:CB_IDENT + 128] = np.eye(128, dtype=np.float32)
    cb[:, CB_ONES:CB_ONES + 128] = 1.0
    s = np.arange(128)[:, None]
    t = np.arange(128)[None, :]
    same = (s // 32) == (t // 32)
    cb[:, CB_MF:CB_MF + 128] = (same & (s <= t)).astype(np.float32)
    cb[:, CB_MB:CB_MB + 128] = (same & (s >= t)).astype(np.float32)
    cb[:, CB_BM:CB_BM + 4] = ((np.arange(128)[:, None] // 32) == np.arange(4)[None, :]).astype(np.float32)
    same = (s // 64) == (t // 64)
    cb[:, CB_M64:CB_M64 + 128] = (same & (s <= t)).astype(np.float32)
    cb[:, CB_M64 + 128:CB_M64 + 256] = (same & (s >= t)).astype(np.float32)
    cb[:, CB_BM2:CB_BM2 + 2] = ((np.arange(128)[:, None] // 64) == np.arange(2)[None, :]).astype(np.float32)
    cb[:, CB_M128:CB_M128 + 128] = (s <= t).astype(np.float32)
    cb[:, CB_M128 + 128:CB_M128 + 256] = (s >= t).astype(np.float32)
    for v in range(3):
        cb[:, CB_PERM + 128 * v:CB_PERM + 128 * (v + 1)] = perm[v]
    return cb, tab


_PROG_CACHE = {}


def _make_inputs(inputs, b):
    f = lambda a: np.ascontiguousarray(np.asarray(a, dtype=np.float32))
    rows = np.zeros((NV, 128), np.float32)

    def put(name, arr):
        a = f(arr).reshape(-1, 128)
        rows[VCOL[name]:VCOL[name] + a.shape[0]] = a
    put('c', inputs['c'][b])
    put('cctx', inputs['c_ctx'])
    for l in range(4):
        put(('ada_b', l), inputs['ada_b'][l])
        for j in range(2):
            put(('norm_g', l, j), inputs['norm_g'][l, j])
    put('final_g', inputs['final_norm_g'])
    for e in range(2):
        for d in range(2):
            put(('lb', e, d), inputs['hgrn_lb_logits'][e, d])
        put(('hgrn_g', e), inputs['hgrn_norm_g'][e])
        put(('qn_g', e), inputs['mla_q_norm_g'][e])
        put(('kvn_g', e), inputs['mla_kv_norm_g'][e])
    for o in range(2):
        put(('subln', o), inputs['diff_subln_g'][o])
    vecs = np.ascontiguousarray(rows.T)
    bc = np.zeros((128, 16 + 512), np.float32)
    bc[:, 0:16] = f(inputs['ret_decay_logits']).reshape(1, 16)
    bc[:, 16:] = f(inputs['diff_lambda']).reshape(1, 512)
    xT = np.ascontiguousarray(np.concatenate([f(inputs['ctx'][b]), f(inputs['x'][b])], axis=0).T)
    return xT, vecs, bc


def kernel(n_layers=4, core_ids=None, **inputs):
    inputs = {k: np.asarray(v) for k, v in inputs.items()}
    if n_layers not in _PROG_CACHE:
        _PROG_CACHE[n_layers] = build_program(n_layers)
    nc = _PROG_CACHE[n_layers]
    cb, tab = _consts()
    f = lambda a: np.ascontiguousarray(np.asarray(a, dtype=np.float32))
    shared = {
        "cb": cb, "rope": tab,
        "ada_w": f(inputs['ada_w']), "mix_w_out": f(inputs['mix_w_out']),
        "ffn_w_gate": f(inputs['ffn_w_gate']), "ffn_w_up": f(inputs['ffn_w_up']), "ffn_w_down": f(inputs['ffn_w_down']),
        "even_w_in": f(inputs['even_w_in']), "mla_w_uq": f(inputs['mla_w_uq']), "mla_w_ukv": f(inputs['mla_w_ukv']),
        "odd_w_in": f(inputs['odd_w_in']),
    }
    cores = list(range(8)) if core_ids is None else core_ids
    in_maps = []
    for b in cores:
        xT, vecs, bc = _make_inputs(inputs, b)
        m = dict(shared)
        m.update({"xT": xT, "vecs": vecs, "bc": bc})
        in_maps.append(m)
    res = run_bass_kernel_spmd(nc, in_maps, core_ids=list(range(len(cores))))
    outs = [np.ascontiguousarray(np.asarray(r["outT"]).T) for r in res.results]
    return np.stack(outs, axis=0).astype(np.float32)
```

```python
import numpy as np
from contextlib import ExitStack
import concourse.bass as bass
import concourse.mybir as mybir
from concourse.bass_utils import run_bass_kernel_spmd

F32 = mybir.dt.float32
BF16 = mybir.dt.bfloat16
AF = mybir.ActivationFunctionType
ALU = mybir.AluOpType
AX = mybir.AxisListType
_ESZ = {F32: 4, BF16: 2}

PHASE = 8000
NSWDGE = 4
NDSEM = 24


def _is_dram(ap):
    return 'DRam' in type(ap.tensor).__name__


def _bbox(ap):
    steps = ap.ap
    es = _ESZ.get(ap.dtype, 4)
    pstep, pcnt = steps[0]
    off = ap.offset
    if pstep:
        p0 = off // pstep
        f0 = off % pstep
    else:
        p0, f0 = 0, off
    ext = 1
    for s, c in steps[1:]:
        ext += (c - 1) * abs(s)
    return (p0, p0 + pcnt, f0 * es, (f0 + ext) * es)


class Prog:
    ENGS = ('pe', 'act', 'dve', 'pool', 'sp')

    def __init__(self):
        self.nc = bass.Bass("TRN2", target_bir_lowering=False)
        self.q = {e: [] for e in self.ENGS}
        self.cnt = {e: 0 for e in self.ENGS}
        self.recs = {}
        self.waited = {e: {} for e in self.ENGS}
        self.ndma = 0
        self.dma_tokens = []
        self.out_tokens = []
        self.same_engine_sync = True

    def sb(self, name, shape, dtype=F32):
        return self.nc.alloc_sbuf_tensor(name, list(shape), dtype).ap()

    def ps(self, name, shape, dtype=F32):
        return self.nc.alloc_psum_tensor(name, list(shape), dtype).ap()

    def dram(self, name, shape, dtype=F32, kind="ExternalInput"):
        return self.nc.dram_tensor(name, list(shape), dtype, kind=kind).ap()

    def _deps(self, eng, token, reads, writes, self_sync):
        deps = {}

        def add(tok):
            k, v = tok
            if deps.get(k, 0) < v:
                deps[k] = v

        for ap in reads:
            if ap is None or _is_dram(ap):
                continue
            bb = _bbox(ap)
            lst = self.recs.setdefault(ap.tensor.name, [])
            for r in lst:
                if r[5] and r[0] < bb[1] and bb[0] < r[1] and r[2] < bb[3] and bb[2] < r[3]:
                    add(r[4])
        for ap in writes:
            if ap is None or _is_dram(ap):
                continue
            bb = _bbox(ap)
            lst = self.recs.setdefault(ap.tensor.name, [])
            for r in lst:
                if r[0] < bb[1] and bb[0] < r[1] and r[2] < bb[3] and bb[2] < r[3]:
                    add(r[4])
        for ap in reads:
            if ap is None or _is_dram(ap):
                continue
            bb = _bbox(ap)
            lst = self.recs[ap.tensor.name]
            if token[0][0] == 'E':
                lst[:] = [r for r in lst if not ((not r[5]) and r[4][0] == token[0]
                                                 and bb[0] <= r[0] and r[1] <= bb[1]
                                                 and bb[2] <= r[2] and r[3] <= bb[3])]
            lst.append((bb[0], bb[1], bb[2], bb[3], token, False))
        for ap in writes:
            if ap is None or _is_dram(ap):
                continue
            bb = _bbox(ap)
            lst = self.recs[ap.tensor.name]
            lst[:] = [r for r in lst if not (bb[0] <= r[0] and r[1] <= bb[1]
                                             and bb[2] <= r[2] and r[3] <= bb[3])]
            lst.append((bb[0], bb[1], bb[2], bb[3], token, True))
        out = []
        w = self.waited[eng]
        for k, v in deps.items():
            if k[0] == 'E' and k[1] == eng and not self_sync:
                continue
            if w.get(k, 0) >= v:
                continue
            w[k] = v
            out.append((k, v))
        return out

    def op(self, eng, fn, reads, writes, self_sync=None):
        if self_sync is None:
            self_sync = self.same_engine_sync
        idx = self.cnt[eng]
        self.cnt[eng] += 1
        token = (('E', eng, idx // PHASE), idx % PHASE + 1)
        waits = self._deps(eng, token, reads, writes, self_sync)
        self.q[eng].append((waits, fn, token, 1))
        return token

    def dma(self, out, in_, eng='sp', is_output=False, **kw):
        i = self.ndma
        self.ndma += 1
        token = (('D', i % NDSEM), 16 * (i // NDSEM + 1))
        waits = self._deps(eng, token, [in_], [out], True)
        if i >= NDSEM:
            pk, pv = self.dma_tokens[i - NDSEM]
            if self.waited[eng].get(pk, 0) < pv:
                self.waited[eng][pk] = pv
                waits.append((pk, pv))
        if eng == 'pool':
            lst = self.__dict__.setdefault('pool_dma_tokens', [])
            if len(lst) >= NSWDGE:
                pk, pv = lst[-NSWDGE]
                if self.waited[eng].get(pk, 0) < pv:
                    self.waited[eng][pk] = pv
                    waits.append((pk, pv))
            lst.append(token)
        self.dma_tokens.append(token)
        if is_output:
            self.out_tokens.append(token)
        self.q[eng].append((waits, lambda e: e.dma_start(out=out, in_=in_, **kw), token, 16))
        return token

    def mm(self, out, lhsT, rhs, start=True, stop=True, **kw):
        return self.op('pe', lambda e: e.matmul(out, lhsT, rhs, start=start, stop=stop, **kw),
                       [lhsT, rhs] + ([] if start else [out]), [out], self_sync=False)

    def transpose(self, out, in_, ident):
        return self.op('pe', lambda e: e.transpose(out, in_, ident), [in_, ident], [out], self_sync=False)

    def act(self, out, in_, func, scale=1.0, bias=0.0, eng='act', accum_out=None):
        rd = [in_]
        if not isinstance(scale, (int, float)):
            rd.append(scale)
        if not isinstance(bias, (int, float)):
            rd.append(bias)
        kw = {}
        if accum_out is not None:
            kw['accum_out'] = accum_out
        return self.op(eng, lambda e: e.activation(out=out, in_=in_, func=func, scale=scale, bias=bias, **kw),
                       rd, [out, accum_out])

    def tt(self, out, in0, in1, op, eng='dve'):
        return self.op(eng, lambda e: e.tensor_tensor(out=out, in0=in0, in1=in1, op=op), [in0, in1], [out])

    def ts(self, out, in0, s1, op0, s2=None, op1=None, eng='dve'):
        rd = [in0]
        if not isinstance(s1, (int, float)):
            rd.append(s1)
        if s2 is not None and not isinstance(s2, (int, float)):
            rd.append(s2)
        kw = {}
        if op1 is not None:
            kw['op1'] = op1
        return self.op(eng, lambda e: e.tensor_scalar(out=out, in0=in0, scalar1=s1, scalar2=s2, op0=op0, **kw),
                       rd, [out])

    def stt(self, out, in0, scalar, in1, op0, op1, eng='dve'):
        rd = [in0, in1]
        if not isinstance(scalar, (int, float)):
            rd.append(scalar)
        return self.op(eng, lambda e: e.scalar_tensor_tensor(out=out, in0=in0, scalar=scalar, in1=in1, op0=op0, op1=op1),
                       rd, [out])

    def copy(self, out, in_, eng='dve'):
        if eng == 'act':
            return self.op(eng, lambda e: e.copy(out=out, in_=in_), [in_], [out])
        return self.op(eng, lambda e: e.tensor_copy(out=out, in_=in_), [in_], [out])

    def memset(self, out, val, eng='dve'):
        return self.op(eng, lambda e: e.memset(out, val), [], [out])

    def recip(self, out, in_, eng='dve'):
        return self.op(eng, lambda e: e.reciprocal(out=out, in_=in_), [in_], [out])

    def scan(self, out, data0, data1, initial, op0, op1):
        rd = [data0, data1]
        if not isinstance(initial, (int, float)):
            rd.append(initial)
        return self.op('dve', lambda e: e.tensor_tensor_scan(out=out, data0=data0, data1=data1, initial=initial,
                                                             op0=op0, op1=op1), rd, [out])

    def finalize(self):
        nc = self.nc
        fw = []
        for k, v in self.out_tokens:
            if self.waited['sp'].get(k, 0) < v:
                self.waited['sp'][k] = v
                fw.append((k, v))
        self.q['sp'].append((fw, None, None, 0))
        with ExitStack() as st:
            sems = {}

            def sem(k):
                if k not in sems:
                    sems[k] = st.enter_context(nc.semaphore("s_" + "_".join(str(x) for x in k)))
                return sems[k]

            for e in self.ENGS:
                for ph in range((self.cnt[e] + PHASE - 1) // PHASE):
                    sem(('E', e, ph))
            for i in range(min(self.ndma, NDSEM)):
                sem(('D', i))
            block = st.enter_context(nc.Block())

            def replay(name):
                def body(eng):
                    for waits, fn, token, inc in self.q[name]:
                        for k, v in waits:
                            eng.wait_ge(sem(k), v)
                        if fn is not None:
                            fn(eng).then_inc(sem(token[0]), inc)
                return body

            block.sync(replay('sp'))
            block.tensor(replay('pe'))
            block.scalar(replay('act'))
            block.vector(replay('dve'))
            block.gpsimd(replay('pool'))
        return nc


import math

D = 1024
T = 2304
CT = 256
NT = 18
TB = [(0, 256), (256, 512), (768, 512), (1280, 512), (1792, 512)]
FFN_H = 2816
EPS = 1e-6
ROPE_RET, ROPE_DIFF, ROPE_MLA = 0, 1, 2


class Ring:
    def __init__(self, aps):
        self.aps = aps
        self.i = 0

    def next(self):
        a = self.aps[self.i % len(self.aps)]
        self.i += 1
        return a


class Arena:
    def __init__(self, P, name, nbytes):
        self.ap = P.sb(name, [128, nbytes // 4], F32)
        self.n = nbytes // 4
        self.off = 0

    def reset(self, to=0):
        self.off = to

    def alloc(self, shape, dtype=F32):
        n = 1
        for s in shape:
            n *= s
        words = (n * (4 if dtype == F32 else 2) + 3) // 4
        words = (words + 7) // 8 * 8
        assert self.off + words <= self.n, ("arena overflow", self.off, words, self.n)
        v = self.ap[:, self.off:self.off + words]
        self.off += words
        if dtype != F32:
            v = v.bitcast(dtype)
        v = v[:, 0:n]
        if len(shape) == 2:
            v = v.rearrange("p (a b) -> p a b", b=shape[1])
        elif len(shape) == 3:
            v = v.rearrange("p (a b c) -> p a b c", b=shape[1], c=shape[2])
        return v

    def ring(self, k, shape, dtype=F32):
        return Ring([self.alloc(shape, dtype) for _ in range(k)])


def vec_layout():
    cols = {}
    n = 0

    def add(name, k):
        nonlocal n
        cols[name] = n
        n += k
    add('c', 8)
    add('cctx', 8)
    for l in range(4):
        add(('ada_b', l), 48)
    for l in range(4):
        for j in range(2):
            add(('norm_g', l, j), 8)
    add('final_g', 8)
    for e in range(2):
        for d in range(2):
            add(('lb', e, d), 4)
    for e in range(2):
        add(('hgrn_g', e), 1)
        add(('qn_g', e), 3)
        add(('kvn_g', e), 2)
    for o in range(2):
        add(('subln', o), 1)
    return cols, n


VCOL, NV = vec_layout()
CB_IDENT, CB_ONES, CB_MF, CB_MB, CB_BM, CB_PERM = 0, 128, 256, 384, 512, 516
CB_M128 = 516 + 3 * 128
CB_M64 = CB_M128 + 256
CB_BM2 = CB_M64 + 256
NCB = CB_BM2 + 2
HGRN_C = (32, 64)
RET_C = 128


def build_program(n_layers=4):
    P = Prog()
    d_xT = P.dram("xT", [D, T])
    d_vec = P.dram("vecs", [128, NV])
    d_bc = P.dram("bc", [128, 16 + 512])
    d_cb = P.dram("cb", [128, NCB])
    d_rope = P.dram("rope", [3, 2, 128, 2048])
    d_ada = P.dram("ada_w", [4, D, 6 * D])
    d_wout = P.dram("mix_w_out", [4, D, D])
    d_wg = P.dram("ffn_w_gate", [4, D, FFN_H])
    d_wu = P.dram("ffn_w_up", [4, D, FFN_H])
    d_wd = P.dram("ffn_w_down", [4, FFN_H, D])
    d_ewin = P.dram("even_w_in", [2, D, 3232])
    d_wuq = P.dram("mla_w_uq", [2, 384, 768])
    d_wukv = P.dram("mla_w_ukv", [2, 256, 1024])
    d_owin = P.dram("odd_w_in", [2, D, 3072])
    d_out = P.dram("outT", [D, 2048], kind="ExternalOutput")

    XT = P.sb("XT", [128, 8, T], F32)
    HT = P.sb("HT", [128, 8, T], BF16)
    VEC = P.sb("VEC", [128, NV], F32)
    BC = P.sb("BCs", [128, 16 + 512], F32)
    CB = P.sb("CBs", [128, NCB], BF16)
    MOD = P.sb("MOD", [128, 4, 48, 2], F32)
    AMOD = P.sb("AMOD", [128, 4, 2, 8, 2], F32)
    SM = P.sb("SM", [128, 64], F32)
    S2 = P.sb("S2", [128, 8, 2], F32)
    IDENT = CB[:, CB_IDENT:CB_IDENT + 128]
    ONES = CB[:, CB_ONES:CB_ONES + 128]
    MASKS = {32: [CB[:, CB_MF:CB_MF + 128], CB[:, CB_MB:CB_MB + 128]],
             64: [CB[:, CB_M64:CB_M64 + 128], CB[:, CB_M64 + 128:CB_M64 + 256]],
             128: [CB[:, CB_M128:CB_M128 + 128], CB[:, CB_M128 + 128:CB_M128 + 256]]}
    BMS = {32: CB[:, CB_BM:CB_BM + 4], 64: CB[:, CB_BM2:CB_BM2 + 2]}
    BM4 = CB[:, CB_BM:CB_BM + 4]
    PERM = [CB[:, CB_PERM + 128 * i:CB_PERM + 128 * (i + 1)] for i in range(3)]
    C_EPS, C_ZERO, C_ONE = 0, 1, 2
    C_LG = 4
    C_NLAM = 20
    C_GS = 22
    C_OML = 24
    C_TMP = 40
    EPSC = SM[:, C_EPS:C_EPS + 1]
    ZEROC = SM[:, C_ZERO:C_ZERO + 1]
    ONEC = SM[:, C_ONE:C_ONE + 1]

    AR = Arena(P, "ARENA", 90 * 1024)
    pbanks = [P.ps("bank%d" % i, [128, 512], F32) for i in range(8)]
    PS_T = Ring(pbanks[0:4])
    PS_A = Ring(pbanks[4:8])

    P.dma(VEC, d_vec)
    P.dma(BC, d_bc)
    P.dma(CB, d_cb, eng='pool')
    P.dma(XT, d_xT.rearrange("(c p) t -> p c t", p=128))
    P.memset(SM, 0.0)
    P.memset(SM[:, C_EPS:C_EPS + 1], EPS)
    P.memset(SM[:, C_ONE:C_ONE + 1], 1.0)

    for k_, nm_ in ((0, 'c'), (1, 'cctx')):
        src_ = VEC[:, VCOL[nm_]:VCOL[nm_] + 8]
        P.act(S2[:, :, k_], src_, AF.Exp, scale=-1.0)
        P.act(S2[:, :, k_], S2[:, :, k_], AF.Ln, bias=SM[:, C_ONE:C_ONE + 1])
        P.act(S2[:, :, k_], S2[:, :, k_], AF.Exp, scale=-1.0)
        P.tt(S2[:, :, k_], S2[:, :, k_], src_, ALU.mult)

    AR.reset()
    adaring = AR.ring(3, (8, 512), BF16)
    S2b = P.sb("S2b", [128, 8, 2], BF16)
    P.copy(S2b, S2)
    for l in range(1):
        mps = PS_A.next()
        for g in range(12):
            wa = adaring.next()
            P.dma(wa, d_ada[l, :, g * 512:(g + 1) * 512].rearrange("(kc p) f -> p kc f", p=128), eng='pool')
            for jj in range(4):
                j = g * 4 + jj
                for kc in range(8):
                    P.mm(mps[:, 2 * j:2 * j + 2], wa[:, kc, jj * 128:(jj + 1) * 128], S2b[:, kc, :],
                         start=(kc == 0), stop=(kc == 7))
        cb0 = VCOL[('ada_b', l)]
        P.tt(MOD[:, l, :, :], mps[:, 0:96].rearrange("p (j k) -> p j k", k=2),
             VEC[:, cb0:cb0 + 48].unsqueeze(2).to_broadcast([128, 48, 2]), ALU.add)
        for which in range(2):
            g0 = VCOL[('norm_g', l, which)]
            sc = MOD[:, l, 8 + 24 * which:16 + 24 * which, :]
            P.stt(AMOD[:, l, which, :, :], sc, 1.0, VEC[:, g0:g0 + 8].unsqueeze(2).to_broadcast([128, 8, 2]),
                  ALU.add, ALU.mult)

    def SH(l, which):
        return MOD[:, l, 24 * which:24 * which + 8, :]

    def GATE(l, which):
        return MOD[:, l, 16 + 24 * which:24 + 24 * which, :]

    class AdaTask:
        def __init__(self, l, ring):
            self.l, self.ring, self.k, self.m = l, ring, 0, 0
            self.bufs = {}

        def _dma(self):
            k = self.k
            wa = self.ring.next()
            P.dma(wa, d_ada[self.l, :, k * 128:(k + 1) * 128].rearrange("(kc p) f -> p kc f", p=128), eng='pool')
            self.bufs[k] = wa
            self.k += 1

        def _compute(self):
            m = self.m
            wa = self.bufs.pop(m)
            ps = PS_T.next()
            for kc in range(8):
                P.mm(ps[:, 0:2], wa[:, kc, :], S2b[:, kc, :], start=(kc == 0), stop=(kc == 7))
            cb0 = VCOL[('ada_b', self.l)]
            P.tt(MOD[:, self.l, m, :], ps[:, 0:2], VEC[:, cb0 + m:cb0 + m + 1].to_broadcast([128, 2]), ALU.add)
            self.m += 1

        def step(self):
            if self.k < 48:
                self._dma()
            if self.m < self.k - 1:
                self._compute()

        def drain(self):
            while self.m < 48:
                if self.k < 48:
                    self._dma()
                self._compute()
            for which in range(2):
                g0 = VCOL[('norm_g', self.l, which)]
                sc = MOD[:, self.l, 8 + 24 * which:16 + 24 * which, :]
                P.stt(AMOD[:, self.l, which, :, :], sc, 1.0,
                      VEC[:, g0:g0 + 8].unsqueeze(2).to_broadcast([128, 8, 2]), ALU.add, ALU.mult)

    bg = {'task': None, 'n': 0}

    def bg_step():
        t = bg['task']
        if t is not None:
            bg['n'] += 1
            if bg['n'] % 8 == 0:
                t.step()

    def bg_start(l):
        if l < n_layers:
            bg['task'] = AdaTask(l, AR.ring(2, (8, 128), BF16))

    def bg_finish():
        if bg['task'] is not None:
            bg['task'].drain()
            bg['task'] = None

    P.act(SM[:, C_LG:C_LG + 16], BC[:, 0:16], AF.Exp, scale=-1.0)
    P.act(SM[:, C_LG:C_LG + 16], SM[:, C_LG:C_LG + 16], AF.Ln, bias=ONEC)
    P.ts(SM[:, C_LG:C_LG + 16], SM[:, C_LG:C_LG + 16], -1.0, ALU.mult)
    for o in range(2):
        layer_idx = 2 * o + 1
        lam_init = 0.8 - 0.6 * math.exp(-0.3 * layer_idx)
        dl = BC[:, 16 + 256 * o:16 + 256 * (o + 1)]
        tmp = AR.alloc((128,), F32) if o == 0 else tmp
        P.tt(tmp[:, 0:64], dl[:, 0:64], dl[:, 64:128], ALU.mult)
        P.tt(tmp[:, 64:128], dl[:, 128:192], dl[:, 192:256], ALU.mult)
        P.op('dve', lambda e, tmp=tmp: e.tensor_reduce(out=SM[:, C_TMP:C_TMP + 2],
                                                      in_=tmp.rearrange("p (a b) -> p a b", b=64),
                                                      axis=AX.X, op=ALU.add),
             [tmp], [SM[:, C_TMP:C_TMP + 2]])
        P.act(SM[:, C_TMP:C_TMP + 2], SM[:, C_TMP:C_TMP + 2], AF.Exp)
        P.tt(SM[:, C_NLAM + o:C_NLAM + o + 1], SM[:, C_TMP + 1:C_TMP + 2], SM[:, C_TMP:C_TMP + 1], ALU.subtract)
        P.ts(SM[:, C_NLAM + o:C_NLAM + o + 1], SM[:, C_NLAM + o:C_NLAM + o + 1], -lam_init, ALU.add)
        sc0 = VCOL[('subln', o)]
        P.ts(SM[:, C_GS + o:C_GS + o + 1], VEC[:, sc0:sc0 + 1], 1.0 - lam_init, ALU.mult)
    P.memset(SM[:, C_OML:C_OML + 8], 1.0)
    for d in range(2):
        a0 = VCOL[('lb', 0, d)]
        a1 = VCOL[('lb', 1, d)]
        dst = SM[:, C_OML + 8 + 4 * d:C_OML + 12 + 4 * d]
        P.tt(dst, VEC[:, a0:a0 + 4], VEC[:, a1:a1 + 4], ALU.subtract)
        P.act(dst, dst, AF.Exp, scale=-1.0)
        P.act(dst, dst, AF.Ln, bias=SM[:, C_ONE:C_ONE + 1])
        P.act(dst, dst, AF.Exp, scale=-1.0)

    def kcol(t0):
        return 1 if t0 < CT else 0

    def norm_mod(l, which):
        AR.reset()
        sqr = AR.ring(3, (512,), BF16)
        rsr = AR.ring(2, (512,), F32)
        tmr = AR.ring(3, (512,), F32)
        for (t0, n) in TB:
            k = kcol(t0)
            R = PS_T.next()
            for c in range(8):
                sq = sqr.next()[:, :n]
                P.act(sq, XT[:, c, t0:t0 + n], AF.Square)
                P.mm(R[:, :n], ONES, sq, start=(c == 0), stop=(c == 7))
            rs = rsr.next()[:, :n]
            rstd_from_sumsq(rs, R[:, :n], D)
            for c in range(8):
                tmp = tmr.next()[:, :n]
                P.stt(tmp, XT[:, c, t0:t0 + n], AMOD[:, l, which, c, k:k + 1], rs, ALU.mult, ALU.mult)
                P.act(HT[:, c, t0:t0 + n], tmp, AF.Identity, bias=SH(l, which)[:, c, k:k + 1])

    def proj(ps, w, src, t0, n, KC=8):
        for kc in range(KC):
            P.mm(ps, w[:, kc, :], src[:, kc, t0:t0 + n], start=(kc == 0), stop=(kc == KC - 1))

    def load_w(dst, src):
        P.dma(dst, src, eng='pool')

    def outproj(l, WO, OT, t0, n):
        k = kcol(t0)
        for i in range(8):
            ps = PS_T.next()
            P.mm(ps[:, :n], WO[:, i * 128:(i + 1) * 128], OT)
            P.stt(XT[:, i, t0:t0 + n], ps[:, :n], GATE(l, 0)[:, i, k:k + 1], XT[:, i, t0:t0 + n], ALU.mult, ALU.add)

    def sigmoid_act(dst, src, sign):
        P.act(dst, src, AF.Exp, scale=-float(sign))
        P.act(dst, dst, AF.Ln, bias=ONEC[:dst.shape[0], :])
        P.act(dst, dst, AF.Exp, scale=-1.0)

    def rstd_from_sumsq(rs, R, nfeat):
        P.act(rs, R, AF.Ln, scale=1.0 / nfeat, bias=EPSC[:rs.shape[0], :])
        P.act(rs, rs, AF.Exp, scale=-0.5)

    def rope(dst, ps, np_, t0, n, variant, raw, cosr, sinr, t1r):
        if t0 < CT:
            P.copy(dst, ps[:np_, :n], eng='act')
            return
        r = raw.next()[:np_, :n]
        P.copy(r, ps[:np_, :n], eng='act')
        pp = PS_T.next()
        P.mm(pp[:np_, :n], PERM[variant][:np_, :np_], r)
        co = cosr.next()[:np_, :n]
        si = sinr.next()[:np_, :n]
        P.dma(co, d_rope[variant, 0, 0:np_, t0 - CT:t0 - CT + n])
        P.dma(si, d_rope[variant, 1, 0:np_, t0 - CT:t0 - CT + n])
        t1 = t1r.next()[:np_, :n]
        P.tt(t1, r, co, ALU.mult)
        P.tt(si, pp[:np_, :n], si, ALU.mult)
        P.tt(dst, t1, si, ALU.add)

    def gla(dk, QS, KF, GBUF, VH, d, OACC, first, C):
        mark = AR.off
        ncl = 128 // C
        NCH = T // C
        B = AR.alloc((T,), F32)[:dk]
        GE = AR.alloc((80,), F32)[:dk]
        DTOT = AR.alloc((72,), F32)[:dk]
        QT_ = AR.alloc((T,), BF16)[:dk]
        KP = AR.alloc((T,), BF16)[:dk]
        KT = AR.alloc((NT, 128), BF16)
        ebr = AR.ring(3, (512,), F32)
        TOTL = AR.alloc((72,), F32)[:dk]
        k2r = AR.ring(2, (512,), BF16)
        vbr = AR.ring(2, (ncl, 128), BF16)
        amr = AR.ring(2, (128,), BF16)
        sbr = AR.ring(2, (ncl, 128), BF16)
        Sst = [AR.alloc((128,), F32)[:dk] for _ in range(4)]
        P.scan(GBUF[:, 32:32 + T], ONEC[:dk, :].to_broadcast([dk, T]), GBUF[:, 32:32 + T], 0.0, ALU.mult, ALU.add)
        P.memset(GE[:, 0:1], 0.0)
        P.copy(GE[:, 1:NCH + 1], GBUF[:, 32:32 + T].rearrange("p (c j) -> p c j", j=C)[:, :, C - 1])
        DTOT = DTOT[:, 0:NCH]
        TOTL = TOTL[:, 0:NCH]
        P.tt(TOTL, GE[:, 1:NCH + 1], GE[:, 0:NCH], ALU.subtract)
        P.act(DTOT, TOTL, AF.Exp)
        Bv = B.rearrange("p (c j) -> p c j", j=C)
        if d == 0:
            P.tt(Bv, GBUF[:, 32:32 + T].rearrange("p (c j) -> p c j", j=C),
                 GE[:, 0:NCH].unsqueeze(2).to_broadcast([dk, NCH, C]), ALU.subtract)
        else:
            P.tt(Bv, GE[:, 1:NCH + 1].unsqueeze(2).to_broadcast([dk, NCH, C]),
                 GBUF[:, 31:31 + T].rearrange("p (c j) -> p c j", j=C), ALU.subtract)
        P.ts(B, B, -80.0, ALU.max)
        for (t0, n) in TB:
            eb = ebr.next()[:dk, :n]
            P.act(eb, B[:, t0:t0 + n], AF.Exp)
            P.tt(QT_[:, t0:t0 + n], QS[:, t0:t0 + n], eb, ALU.mult)
            enb = ebr.next()[:dk, :n]
            P.act(enb, B[:, t0:t0 + n], AF.Exp, scale=-1.0)
            P.tt(KP[:, t0:t0 + n], KF[:, t0:t0 + n], enb, ALU.mult)
            e2 = ebr.next()[:dk, :n]
            P.tt(e2.rearrange("p (c j) -> p c j", j=C),
                 TOTL[:, t0 // C:(t0 + n) // C].unsqueeze(2).to_broadcast([dk, n // C, C]),
                 B[:, t0:t0 + n].rearrange("p (c j) -> p c j", j=C), ALU.subtract)
            P.act(e2, e2, AF.Exp)
            k2 = k2r.next()[:dk, :n]
            P.tt(k2, KF[:, t0:t0 + n], e2, ALU.mult)
            for i in range(n // 128):
                tt_ = t0 // 128 + i
                pt = PS_T.next()[:, 0:64].bitcast(BF16)
                P.transpose(pt[:, :dk], k2[:, i * 128:(i + 1) * 128], IDENT[:dk, :dk])
                P.copy(KT[:, tt_, :dk], pt[:, :dk], eng='act')
        order = list(range(NT)) if d == 0 else [1, 0] + list(range(NT - 1, 1, -1))
        corder = list(range(ncl)) if d == 0 else list(range(ncl - 1, -1, -1))
        P.memset(Sst[0], 0.0)
        scur = 0

        def emit_front(tt_):
            U = PS_A.next()
            if ncl == 1:
                P.mm(U[:dk, 0:128], KT[:, tt_, :dk], VH[:, tt_, :])
            else:
                vb = vbr.next()
                P.tt(vb, VH[:, tt_, :].unsqueeze(1).to_broadcast([128, ncl, 128]),
                     BMS[C].unsqueeze(2).to_broadcast([128, ncl, 128]), ALU.mult, eng='pool')
                P.mm(U[:dk, 0:ncl * 128], KT[:, tt_, :dk], vb.rearrange("p a b -> p (a b)"))
            AT_ = PS_T.next()
            P.mm(AT_[:, 0:128], KP[:, tt_ * 128:(tt_ + 1) * 128], QT_[:, tt_ * 128:(tt_ + 1) * 128])
            return U, AT_

        nxt = emit_front(order[0])
        for oi, tt_ in enumerate(order):
            U, AT_ = nxt
            if oi + 1 < NT:
                nxt = emit_front(order[oi + 1])
            am = amr.next()
            P.tt(am, AT_[:, 0:128], MASKS[C][d], ALU.mult)
            sb16 = sbr.next()
            for cl in corder:
                c = tt_ * ncl + cl
                P.copy(sb16[:dk, cl, :], Sst[scur], eng='act')
                P.stt(Sst[(scur + 1) % 4], Sst[scur], DTOT[:, c:c + 1], U[:dk, cl * 128:(cl + 1) * 128], ALU.mult, ALU.add)
                scur = (scur + 1) % 4
            O = PS_A.next()
            P.mm(O[:, 0:128], VH[:, tt_, :], am, start=True, stop=False)
            for cl in range(ncl):
                P.mm(O[:, cl * C:(cl + 1) * C], sb16[:dk, cl, :],
                     QT_[:, tt_ * 128 + cl * C:tt_ * 128 + (cl + 1) * C], start=False, stop=(cl == ncl - 1))
            oa = OACC[:, tt_ * 128:(tt_ + 1) * 128]
            if first:
                P.copy(oa, O[:, 0:128], eng='act')
            else:
                P.tt(oa, O[:, 0:128], oa, ALU.add)
        AR.reset(mark)

    def readout_gated(l, WO, OACC, gw, gain_col, t0, n, rings):
        sqr, rsr, tmr, otr = rings
        sq = sqr.next()[:, :n]
        P.act(sq, OACC[:, t0:t0 + n], AF.Square)
        R = PS_T.next()
        P.mm(R[:, :n], ONES, sq)
        rs = rsr.next()[:, :n]
        rstd_from_sumsq(rs, R[:, :n], 128)
        pg = PS_T.next()
        proj(pg[:, :n], gw, HT, t0, n)
        sg = tmr.next()[:, :n]
        sigmoid_act(sg, pg[:, :n], 1)
        P.tt(sg, pg[:, :n], sg, ALU.mult)
        t1 = tmr.next()[:, :n]
        if gain_col is None:
            P.tt(t1, OACC[:, t0:t0 + n], rs, ALU.mult)
        else:
            P.stt(t1, OACC[:, t0:t0 + n], gain_col, rs, ALU.mult, ALU.mult)
        ot = otr.next()[:, :n]
        P.tt(ot, t1, sg, ALU.mult)
        outproj(l, WO, ot, t0, n)

    def attn_block(pairs, t0, n, scale, ptr, with_den=True):
        kts = [0, 1] if t0 < CT else list(range(NT))
        res = []
        for (Kt, Qt, Vl) in pairs:
            O = PS_A.next()
            DEN = PS_A.next() if with_den else None

            def issue_S(kt):
                S = PS_T.next()
                P.mm(S[:, :n], Kt[:, kt * 128:(kt + 1) * 128], Qt[:, t0:t0 + n])
                return S
            pend = [issue_S(kt) for kt in kts[:2]]
            for ki, kt in enumerate(kts):
                S = pend.pop(0)
                if ki + 2 < len(kts):
                    pend.append(issue_S(kts[ki + 2]))
                pt = ptr.next()[:, :n]
                P.act(pt, S[:, :n], AF.Exp, scale=scale)
                P.mm(O[:, :n], Vl(kt), pt, start=(ki == 0), stop=(ki == len(kts) - 1))
                if with_den:
                    P.mm(DEN[:, :n], ONES, pt, start=(ki == 0), stop=(ki == len(kts) - 1))
                bg_step()
            res.append((O, DEN))
        return res

    def even_mixer(l):
        e = l // 2
        W = d_ewin[e]
        AR.reset()
        WOr = AR.ring(1, (1024,), BF16)
        hmark = AR.off
        for h in range(4):
            AR.reset(hmark)
            WH = AR.alloc((8, 5, 128), BF16)
            for g in range(5):
                load_w(WH[:, :, g, :], W[:, g * 512 + h * 128:g * 512 + (h + 1) * 128].rearrange("(kc p) c -> p kc c", p=128))
            WO = WOr.next()
            load_w(WO, d_wout[l, h * 128:(h + 1) * 128, :])
            VH = AR.alloc((NT, 128), BF16)
            QS = AR.alloc((T,), F32)
            KF = AR.alloc((T,), BF16)
            GBUF = AR.alloc((32 + T,), F32)
            OACC = AR.alloc((T,), F32)
            gmark = AR.off
            tmr = AR.ring(2, (512,), F32)
            for tt_ in range(NT):
                ps = PS_T.next()
                for kc in range(8):
                    P.mm(ps[:, 0:128], HT[:, kc, tt_ * 128:(tt_ + 1) * 128], WH[:, kc, 1, :], start=(kc == 0), stop=(kc == 7))
                P.copy(VH[:, tt_, :], ps[:, 0:128], eng='act')
            for (t0, n) in TB:
                ps = PS_T.next()
                proj(ps[:, :n], WH[:, :, 0, :], HT, t0, n)
                P.act(QS[:, t0:t0 + n], ps[:, :n], AF.Copy, scale=128 ** -0.5)
            P.memset(GBUF[:, 0:32], 0.0)
            for d in range(2):
                for (t0, n) in TB:
                    ps = PS_T.next()
                    proj(ps[:, :n], WH[:, :, 2 + d, :], HT, t0, n)
                    sk = tmr.next()[:, :n]
                    sigmoid_act(sk, ps[:, :n], -1)
                    P.ts(GBUF[:, 32 + t0:32 + t0 + n], sk, SM[:, C_OML + 8 * e + 4 * d + h:C_OML + 8 * e + 4 * d + h + 1], ALU.mult)
                    P.copy(KF[:, t0:t0 + n], GBUF[:, 32 + t0:32 + t0 + n], eng='pool')
                P.act(GBUF[:, 32:32 + T], GBUF[:, 32:32 + T], AF.Ln, scale=-1.0, bias=ONEC)
                AR.reset(gmark)
                gla(128, QS, KF, GBUF, VH, d, OACC, first=(d == 0), C=HGRN_C[e])
                AR.reset(gmark)
                tmr = AR.ring(2, (512,), F32)
            AR.reset(gmark)
            rings = (AR.ring(2, (512,), BF16), AR.ring(2, (512,), F32), AR.ring(3, (512,), F32), AR.ring(2, (512,), BF16))
            gcol = VEC[:, VCOL[('hgrn_g', e)]:VCOL[('hgrn_g', e)] + 1]
            for (t0, n) in TB:
                readout_gated(l, WO, OACC, WH[:, :, 4, :], gcol, t0, n, rings)
        AR.reset(hmark)
        bg_start(l + 1)
        WUQ = AR.alloc((3, 768), BF16)
        load_w(WUQ, d_wuq[e].rearrange("(kc p) f -> p kc f", p=128))
        WUKV = AR.alloc((2, 1024), BF16)
        load_w(WUKV, d_wukv[e].rearrange("(kc p) f -> p kc f", p=128))
        CQN = AR.alloc((3, T), BF16)
        CKV = AR.alloc((2, T), BF16)
        KRT = AR.alloc((T,), BF16)
        raw = AR.ring(2, (512,), BF16)
        cosr = AR.ring(2, (512,), F32)
        sinr = AR.ring(2, (512,), F32)
        t1r = AR.ring(2, (512,), F32)
        lmark = AR.off
        WL = AR.alloc((8, 640), BF16)
        load_w(WL, W[:, 2560:3200].rearrange("(kc p) f -> p kc f", p=128))
        WKR = AR.alloc((8, 96), BF16)
        P.memset(WKR, 0.0)
        load_w(WKR[:, :, 64:96], W[:, 3200:3232].rearrange("(kc p) f -> p kc f", p=128))
        c32 = AR.alloc((3, 512), F32)
        sqr = AR.ring(2, (512,), BF16)
        rsr = AR.ring(2, (512,), F32)
        for (t0, n) in TB:
            for (c0, nch, dst, gname) in ((0, 3, CQN, 'qn_g'), (3, 2, CKV, 'kvn_g')):
                R = PS_A.next()
                for c in range(nch):
                    ps = PS_T.next()
                    proj(ps[:, :n], WL[:, :, (c0 + c) * 128:(c0 + c + 1) * 128], HT, t0, n)
                    P.copy(c32[:, c, :n], ps[:, :n], eng='act')
                    sq = sqr.next()[:, :n]
                    P.act(sq, ps[:, :n], AF.Square)
                    P.mm(R[:, :n], ONES, sq, start=(c == 0), stop=(c == nch - 1))
                rs = rsr.next()[:, :n]
                rstd_from_sumsq(rs, R[:, :n], 128 * nch)
                g0 = VCOL[(gname, e)]
                for c in range(nch):
                    P.stt(dst[:, c, t0:t0 + n], c32[:, c, :n], VEC[:, g0 + c:g0 + c + 1], rs, ALU.mult, ALU.mult)
            ps = PS_T.next()
            proj(ps[:96, :n], WKR, HT, t0, n)
            rope(KRT[:96, t0:t0 + n], ps, 96, t0, n, ROPE_MLA, raw, cosr, sinr, t1r)
        AR.reset(lmark)
        pmark = AR.off
        sc = 96 ** -0.5
        for hp in range(4):
            AR.reset(pmark)
            WO = WOr.next()
            load_w(WO, d_wout[l, (4 + hp) * 128:(5 + hp) * 128, :])
            QTs, KTs, VAs = [], [], []
            for hh in range(2):
                h = 2 * hp + hh
                QTh = AR.alloc((T,), BF16)
                KTh = AR.alloc((T,), BF16)
                VA = AR.alloc((NT, 128), BF16)
                P.memset(VA, 1.0, eng='pool')
                for (t0, n) in TB:
                    ps = PS_T.next()
                    proj(ps[:96, :n], WUQ[:, :, 96 * h:96 * h + 96], CQN, t0, n, KC=3)
                    rope(QTh[:96, t0:t0 + n], ps, 96, t0, n, ROPE_MLA, raw, cosr, sinr, t1r)
                    ps = PS_T.next()
                    proj(ps[:64, :n], WUKV[:, :, 128 * h:128 * h + 64], CKV, t0, n, KC=2)
                    P.copy(KTh[:64, t0:t0 + n], ps[:64, :n], eng='act')
                P.copy(KTh[64:96, :], KRT[64:96, :], eng='pool')
                for tt_ in range(NT):
                    ps = PS_T.next()
                    for kc in range(2):
                        P.mm(ps[:, 0:64], CKV[:, kc, tt_ * 128:(tt_ + 1) * 128], WUKV[:, kc, 128 * h + 64:128 * h + 128],
                             start=(kc == 0), stop=(kc == 1))
                    P.copy(VA[:, tt_, 64 * hh:64 * hh + 64], ps[:, 0:64], eng='act')
                QTs.append(QTh)
                KTs.append(KTh)
                VAs.append(VA)
            ptr = AR.ring(3, (512,), BF16)
            rdr = AR.ring(1, (512,), F32)
            otr = AR.ring(2, (512,), BF16)
            for (t0, n) in TB:
                pairs = [(KTs[hh][:96, :], QTs[hh][:96, :], (lambda kt, hh=hh: VAs[hh][:, kt, :])) for hh in range(2)]
                res = attn_block(pairs, t0, n, sc, ptr, with_den=False)
                ot = otr.next()[:, :n]
                for hh in range(2):
                    O, _ = res[hh]
                    rd = rdr.next()[:, :n]
                    sl = slice(64 * hh, 64 * hh + 64)
                    so = slice(64 * (1 - hh), 64 * (1 - hh) + 64)
                    P.act(rd[sl, :], O[so, :n], AF.Ln)
                    P.act(rd[sl, :], rd[sl, :], AF.Exp, scale=-1.0)
                    P.tt(ot[sl, :], O[sl, :n], rd[sl, :], ALU.mult)
                outproj(l, WO, ot, t0, n)
        bg_finish()

    def odd_mixer(l):
        o = l // 2
        W = d_owin[o]
        AR.reset()
        WOr = AR.ring(2, (1024,), BF16)
        hmark = AR.off
        for h in range(4):
            AR.reset(hmark)
            WR = AR.alloc((8, 384), BF16)
            Wv = W.rearrange("(kc p) f -> p kc f", p=128)
            load_w(WR[:, :, 0:64], Wv[:, :, 64 * h:64 * h + 64])
            load_w(WR[:, :, 64:128], Wv[:, :, 256 + 64 * h:256 + 64 * h + 64])
            load_w(WR[:, :, 128:256], Wv[:, :, 512 + 128 * h:512 + 128 * h + 128])
            load_w(WR[:, :, 256:384], Wv[:, :, 1024 + 128 * h:1024 + 128 * h + 128])
            WO = WOr.next()
            load_w(WO, d_wout[l, h * 128:(h + 1) * 128, :])
            VH = AR.alloc((NT, 128), BF16)
            QS = AR.alloc((T,), F32)
            KF = AR.alloc((T,), BF16)
            GBUF = AR.alloc((32 + T,), F32)
            OACC = AR.alloc((T,), F32)
            gmark = AR.off
            raw = AR.ring(2, (512,), BF16)
            cosr = AR.ring(2, (512,), F32)
            sinr = AR.ring(2, (512,), F32)
            t1r = AR.ring(2, (512,), F32)
            kfr = AR.ring(2, (512,), F32)
            for tt_ in range(NT):
                ps = PS_T.next()
                for kc in range(8):
                    P.mm(ps[:, 0:128], HT[:, kc, tt_ * 128:(tt_ + 1) * 128], WR[:, kc, 128:256], start=(kc == 0), stop=(kc == 7))
                P.copy(VH[:, tt_, :], ps[:, 0:128], eng='act')
            for (t0, n) in TB:
                ps = PS_T.next()
                proj(ps[:64, :n], WR[:, :, 0:64], HT, t0, n)
                rope(QS[:64, t0:t0 + n], ps, 64, t0, n, ROPE_RET, raw, cosr, sinr, t1r)
                ps = PS_T.next()
                proj(ps[:64, :n], WR[:, :, 64:128], HT, t0, n)
                kf = kfr.next()[:64, :n]
                rope(kf, ps, 64, t0, n, ROPE_RET, raw, cosr, sinr, t1r)
                P.ts(KF[:64, t0:t0 + n], kf, 64 ** -0.5, ALU.mult)
            P.memset(GBUF[:, 0:32], 0.0)
            for d in range(2):
                col = C_LG + 8 * o + 4 * d + h
                P.act(GBUF[:64, 32:32 + T], ZEROC[:64, :].to_broadcast([64, T]), AF.Identity, scale=0.0,
                      bias=SM[:64, col:col + 1])
                AR.reset(gmark)
                gla(64, QS[:64], KF[:64], GBUF[:64], VH, d, OACC, first=(d == 0), C=RET_C)
            AR.reset(gmark)
            rings = (AR.ring(2, (512,), BF16), AR.ring(2, (512,), F32), AR.ring(3, (512,), F32), AR.ring(2, (512,), BF16))
            for (t0, n) in TB:
                readout_gated(l, WO, OACC, WR[:, :, 256:384], None, t0, n, rings)
        sc = 64 ** -0.5
        AR.reset(hmark)
        bg_start(l + 1)
        hmark = AR.off
        for h in range(4):
            AR.reset(hmark)
            WD_ = AR.alloc((8, 384), BF16)
            Wv = W.rearrange("(kc p) f -> p kc f", p=128)
            load_w(WD_[:, :, 0:128], Wv[:, :, 1536 + 128 * h:1536 + 128 * h + 128])
            load_w(WD_[:, :, 128:256], Wv[:, :, 2048 + 128 * h:2048 + 128 * h + 128])
            load_w(WD_[:, :, 256:384], Wv[:, :, 2560 + 128 * h:2560 + 128 * h + 128])
            WO = WOr.next()
            load_w(WO, d_wout[l, (4 + h) * 128:(5 + h) * 128, :])
            VH = AR.alloc((NT, 128), BF16)
            QD = AR.alloc((T,), BF16)
            KD = AR.alloc((T,), BF16)
            raw = AR.ring(2, (512,), BF16)
            cosr = AR.ring(2, (512,), F32)
            sinr = AR.ring(2, (512,), F32)
            t1r = AR.ring(2, (512,), F32)
            for tt_ in range(NT):
                ps = PS_T.next()
                for kc in range(8):
                    P.mm(ps[:, 0:128], HT[:, kc, tt_ * 128:(tt_ + 1) * 128], WD_[:, kc, 256:384], start=(kc == 0), stop=(kc == 7))
                P.copy(VH[:, tt_, :], ps[:, 0:128], eng='act')
            for (t0, n) in TB:
                ps = PS_T.next()
                proj(ps[:, :n], WD_[:, :, 0:128], HT, t0, n)
                rope(QD[:, t0:t0 + n], ps, 128, t0, n, ROPE_DIFF, raw, cosr, sinr, t1r)
                ps = PS_T.next()
                proj(ps[:, :n], WD_[:, :, 128:256], HT, t0, n)
                rope(KD[:, t0:t0 + n], ps, 128, t0, n, ROPE_DIFF, raw, cosr, sinr, t1r)
            ptr = AR.ring(4, (512,), BF16)
            f32r = AR.ring(4, (512,), F32)
            sqr = AR.ring(2, (512,), BF16)
            otr = AR.ring(2, (512,), BF16)
            for (t0, n) in TB:
                pairs = [(KD[64 * m:64 * m + 64, :], QD[64 * m:64 * m + 64, :], (lambda kt: VH[:, kt, :])) for m in range(2)]
                res = attn_block(pairs, t0, n, sc, ptr)
                a = []
                for m in range(2):
                    O, DEN = res[m]
                    rd = f32r.next()[:, :n]
                    P.act(rd, DEN[:, :n], AF.Ln)
                    P.act(rd, rd, AF.Exp, scale=-1.0)
                    P.tt(rd, O[:, :n], rd, ALU.mult)
                    a.append(rd)
                ov = f32r.next()[:, :n]
                P.stt(ov, a[1], SM[:, C_NLAM + o:C_NLAM + o + 1], a[0], ALU.mult, ALU.add)
                sq = sqr.next()[:, :n]
                P.act(sq, ov, AF.Square)
                R = PS_T.next()
                P.mm(R[:, :n], ONES, sq)
                rs = f32r.next()[:, :n]
                rstd_from_sumsq(rs, R[:, :n], 128)
                ot = otr.next()[:, :n]
                P.stt(ot, ov, SM[:, C_GS + o:C_GS + o + 1], rs, ALU.mult, ALU.mult)
                outproj(l, WO, ot, t0, n)
        bg_finish()

    def ffn(l):
        SBS = [TB[0:3], TB[3:5]]
        for sbk in SBS:
            AR.reset()
            base = sbk[0][0]
            ntok = sum(n for _, n in sbk)
            ACTT = AR.alloc((22, ntok), BF16)
            wgr = AR.ring(2, (8, 256), BF16)
            wur = AR.ring(2, (8, 256), BF16)
            sgr = AR.ring(3, (512,), F32)
            for jg in range(11):
                wg = wgr.next()
                wu = wur.next()
                load_w(wg, d_wg[l, :, jg * 256:(jg + 1) * 256].rearrange("(kc p) f -> p kc f", p=128))
                load_w(wu, d_wu[l, :, jg * 256:(jg + 1) * 256].rearrange("(kc p) f -> p kc f", p=128))
                for jj in range(2):
                    j = jg * 2 + jj
                    for (t0, n) in sbk:
                        pg = PS_T.next()
                        proj(pg[:, :n], wg[:, :, jj * 128:(jj + 1) * 128], HT, t0, n)
                        pu = PS_A.next()
                        proj(pu[:, :n], wu[:, :, jj * 128:(jj + 1) * 128], HT, t0, n)
                        sg = sgr.next()[:, :n]
                        P.act(sg, pg[:, :n], AF.Silu)
                        P.tt(ACTT[:, j, t0 - base:t0 - base + n], sg, pu[:, :n], ALU.mult)
            wdr = AR.ring(2, (22, 128), BF16)
            for i in range(8):
                wd = wdr.next()
                load_w(wd, d_wd[l, :, i * 128:(i + 1) * 128].rearrange("(j p) c -> p j c", p=128))
                for (t0, n) in sbk:
                    k = kcol(t0)
                    ps = PS_T.next()
                    for j in range(22):
                        P.mm(ps[:, :n], wd[:, j, :], ACTT[:, j, t0 - base:t0 - base + n], start=(j == 0), stop=(j == 21))
                    P.stt(XT[:, i, t0:t0 + n], ps[:, :n], GATE(l, 1)[:, i, k:k + 1], XT[:, i, t0:t0 + n], ALU.mult, ALU.add)

    for l in range(n_layers):
        norm_mod(l, 0)
        if l % 2 == 0:
            even_mixer(l)
        else:
            odd_mixer(l)
        norm_mod(l, 1)
        ffn(l)

    AR.reset()
    sqr = AR.ring(3, (512,), BF16)
    rsr = AR.ring(2, (512,), F32)
    outr = AR.ring(3, (512,), F32)
    fg = VCOL['final_g']
    for (t0, n) in TB[1:]:
        R = PS_T.next()
        for c in range(8):
            sq = sqr.next()[:, :n]
            P.act(sq, XT[:, c, t0:t0 + n], AF.Square)
            P.mm(R[:, :n], ONES, sq, start=(c == 0), stop=(c == 7))
        rs = rsr.next()[:, :n]
        rstd_from_sumsq(rs, R[:, :n], D)
        for c in range(8):
            ob = outr.next()[:, :n]
            P.stt(ob, XT[:, c, t0:t0 + n], VEC[:, fg + c:fg + c + 1], rs, ALU.mult, ALU.mult)
            P.dma(d_out[c * 128:(c + 1) * 128, t0 - CT:t0 - CT + n], ob, is_output=True)
    return P.finalize()


def _rope_tables():
    tab = np.zeros((3, 2, 128, 2048), np.float32)
    tab[:, 0] = 1.0
    perm = np.zeros((3, 128, 128), np.float32)
    n = np.arange(2048, dtype=np.float32)
    row = np.floor(n / 64.0)
    col = n - 64.0 * row

    def fill(v, p0, width, pos):
        half = width // 2
        inv = (10000.0 ** (-np.arange(half, dtype=np.float32) / half)).astype(np.float32)
        ang = pos[None, :].astype(np.float32) * inv[:, None]
        c, s = np.cos(ang).astype(np.float32), np.sin(ang).astype(np.float32)
        tab[v, 0, p0:p0 + half] = c
        tab[v, 0, p0 + half:p0 + width] = c
        tab[v, 1, p0:p0 + half] = -s
        tab[v, 1, p0 + half:p0 + width] = s
        for j in range(half):
            perm[v, p0 + j + half, p0 + j] = 1.0
            perm[v, p0 + j, p0 + j + half] = 1.0
    for g in range(2):
        fill(ROPE_RET, 64 * g, 64, n)
        fill(ROPE_DIFF, 64 * g, 32, row)
        fill(ROPE_DIFF, 64 * g + 32, 32, col)
    fill(ROPE_MLA, 64, 16, row)
    fill(ROPE_MLA, 80, 16, col)
    return tab, perm


def _consts():
    tab, perm = _rope_tables()
    cb = np.zeros((128, NCB), np.float32)
    cb[:, CB_IDENT:CB_IDENT + 128] = np.eye(128, dtype=np.float32)
    cb[:, CB_ONES:CB_ONES + 128] = 1.0
    s = np.arange(128)[:, None]
    t = np.arange(128)[None, :]
    same = (s // 32) == (t // 32)
    cb[:, CB_MF:CB_MF + 128] = (same & (s <= t)).astype(np.float32)
    cb[:, CB_MB:CB_MB + 128] = (same & (s >= t)).astype(np.float32)
    cb[:, CB_BM:CB_BM + 4] = ((np.arange(128)[:, None] // 32) == np.arange(4)[None, :]).astype(np.float32)
    same = (s // 64) == (t // 64)
    cb[:, CB_M64:CB_M64 + 128] = (same & (s <= t)).astype(np.float32)
    cb[:, CB_M64 + 128:CB_M64 + 256] = (same & (s >= t)).astype(np.float32)
    cb[:, CB_BM2:CB_BM2 + 2] = ((np.arange(128)[:, None] // 64) == np.arange(2)[None, :]).astype(np.float32)
    cb[:, CB_M128:CB_M128 + 128] = (s <= t).astype(np.float32)
    cb[:, CB_M128 + 128:CB_M128 + 256] = (s >= t).astype(np.float32)
    for v in range(3):
        cb[:, CB_PERM + 128 * v:CB_PERM + 128 * (v + 1)] = perm[v]
    return cb, tab


_PROG_CACHE = {}


def _make_inputs(inputs, b):
    f = lambda a: np.ascontiguousarray(np.asarray(a, dtype=np.float32))
    rows = np.zeros((NV, 128), np.float32)

    def put(name, arr):
        a = f(arr).reshape(-1, 128)
        rows[VCOL[name]:VCOL[name] + a.shape[0]] = a
    put('c', inputs['c'][b])
    put('cctx', inputs['c_ctx'])
    for l in range(4):
        put(('ada_b', l), inputs['ada_b'][l])
        for j in range(2):
            put(('norm_g', l, j), inputs['norm_g'][l, j])
    put('final_g', inputs['final_norm_g'])
    for e in range(2):
        for d in range(2):
            put(('lb', e, d), inputs['hgrn_lb_logits'][e, d])
        put(('hgrn_g', e), inputs['hgrn_norm_g'][e])
        put(('qn_g', e), inputs['mla_q_norm_g'][e])
        put(('kvn_g', e), inputs['mla_kv_norm_g'][e])
    for o in range(2):
        put(('subln', o), inputs['diff_subln_g'][o])
    vecs = np.ascontiguousarray(rows.T)
    bc = np.zeros((128, 16 + 512), np.float32)
    bc[:, 0:16] = f(inputs['ret_decay_logits']).reshape(1, 16)
    bc[:, 16:] = f(inputs['diff_lambda']).reshape(1, 512)
    xT = np.ascontiguousarray(np.concatenate([f(inputs['ctx'][b]), f(inputs['x'][b])], axis=0).T)
    return xT, vecs, bc


def kernel(n_layers=4, core_ids=None, **inputs):
    inputs = {k: np.asarray(v) for k, v in inputs.items()}
    if n_layers not in _PROG_CACHE:
        _PROG_CACHE[n_layers] = build_program(n_layers)
    nc = _PROG_CACHE[n_layers]
    cb, tab = _consts()
    f = lambda a: np.ascontiguousarray(np.asarray(a, dtype=np.float32))
    shared = {
        "cb": cb, "rope": tab,
        "ada_w": f(inputs['ada_w']), "mix_w_out": f(inputs['mix_w_out']),
        "ffn_w_gate": f(inputs['ffn_w_gate']), "ffn_w_up": f(inputs['ffn_w_up']), "ffn_w_down": f(inputs['ffn_w_down']),
        "even_w_in": f(inputs['even_w_in']), "mla_w_uq": f(inputs['mla_w_uq']), "mla_w_ukv": f(inputs['mla_w_ukv']),
        "odd_w_in": f(inputs['odd_w_in']),
    }
    cores = list(range(8)) if core_ids is None else core_ids
    in_maps = []
    for b in cores:
        xT, vecs, bc = _make_inputs(inputs, b)
        m = dict(shared)
        m.update({"xT": xT, "vecs": vecs, "bc": bc})
        in_maps.append(m)
    res = run_bass_kernel_spmd(nc, in_maps, core_ids=list(range(len(cores))))
    outs = [np.ascontiguousarray(np.asarray(r["outT"]).T) for r in res.results]
    return np.stack(outs, axis=0).astype(np.float32)
```
